# Optimizing a Trainium2 kernel written in Bass

```python
import jax, jax.numpy as jnp
from jax import lax
import numpy as np

D_MODEL = 1024
BATCH = 32
SEQ = 2048
DEPTH = 1
DEC_BATCH = 32
DEC_SEQ = 32
PAST_LEN = 4096

CHUNK = 64
HEAD_DIM = 64
N_HEADS = D_MODEL // HEAD_DIM
N_HEADS_A = N_HEADS // 2
N_HEADS_B = N_HEADS - N_HEADS_A
D_A = N_HEADS_A * HEAD_DIM
D_B = N_HEADS_B * HEAD_DIM
D_MIX = D_A + D_B
D_IN = 3 * D_A + 3 * D_B + N_HEADS_B
IN_SPLITS = (D_A, 2 * D_A, 3 * D_A, 3 * D_A + D_B, 3 * D_A + 2 * D_B, 3 * D_A + 3 * D_B)
A_LEFT_CHUNKS = 8
A_WINDOW = A_LEFT_CHUNKS * CHUNK
A_BAND = A_WINDOW + CHUNK
REL_CLIP = 128
Q_BLOCK = 128
D_FF = ((8 * D_MODEL + 3 * 256 - 1) // (3 * 256)) * 256
PLE_DIM = 256
FORGET_BIAS_INIT = 3.0
ATTN_SCALE = HEAD_DIM ** -0.5
NEG_INF = -1e30
EPS = 1e-6

kernel_name = "hybrid_chunk_band_fox_streaming_step"


def rms_norm(x, g):
    xf = x.astype(jnp.float32)
    y = xf * lax.rsqrt(jnp.mean(xf * xf, axis=-1, keepdims=True) + EPS)
    return (y * g.astype(jnp.float32)).astype(x.dtype)


def _split_heads(x, n_heads):
    return x.reshape(*x.shape[:-1], n_heads, HEAD_DIM)


def _rel_bias(table, rel):
    return table[:, jnp.clip(rel, -REL_CLIP, REL_CLIP) + REL_CLIP].astype(jnp.float32)


def _mix_projections(h, w_in, b_f, qn_a, kn_a, qn_b, kn_b):
    proj = h @ w_in
    q_a, k_a, v_a, q_b, k_b, v_b, g_f = jnp.split(proj, IN_SPLITS, axis=-1)
    q_a = rms_norm(_split_heads(q_a, N_HEADS_A), qn_a)
    k_a = rms_norm(_split_heads(k_a, N_HEADS_A), kn_a)
    v_a = _split_heads(v_a, N_HEADS_A)
    q_b = rms_norm(_split_heads(q_b, N_HEADS_B), qn_b)
    k_b = rms_norm(_split_heads(k_b, N_HEADS_B), kn_b)
    v_b = _split_heads(v_b, N_HEADS_B)
    logf = jax.nn.log_sigmoid(g_f.astype(jnp.float32) + b_f.astype(jnp.float32))
    return q_a, k_a, v_a, q_b, k_b, v_b, logf


def chunk_band_attention_prompt(q, k, v, rel_table):
    b, t, h, d = q.shape
    n_c = t // CHUNK
    pad = ((0, 0), (A_WINDOW, 0), (0, 0), (0, 0))
    k_pad = jnp.pad(k, pad).reshape(b, n_c + A_LEFT_CHUNKS, CHUNK, h, d)
    v_pad = jnp.pad(v, pad).reshape(b, n_c + A_LEFT_CHUNKS, CHUNK, h, d)
    k_band = jnp.concatenate([k_pad[:, j:j + n_c] for j in range(A_LEFT_CHUNKS + 1)], axis=2)
    v_band = jnp.concatenate([v_pad[:, j:j + n_c] for j in range(A_LEFT_CHUNKS + 1)], axis=2)
    q_c = q.reshape(b, n_c, CHUNK, h, d)
    s = jnp.einsum("bcqhd,bckhd->bchqk", q_c, k_band).astype(jnp.float32) * ATTN_SCALE
    rel = (A_WINDOW + jnp.arange(CHUNK))[:, None] - jnp.arange(A_BAND)[None, :]
    s = s + _rel_bias(rel_table, rel)[None, None]
    valid = (jnp.arange(n_c)[:, None] + jnp.arange(A_BAND)[None, :] // CHUNK) >= A_LEFT_CHUNKS
    s = jnp.where(valid[None, :, None, None, :], s, NEG_INF)
    p = jax.nn.softmax(s, axis=-1).astype(v.dtype)
    o = jnp.einsum("bchqk,bckhd->bcqhd", p, v_band)
    return o.reshape(b, t, h, d)


def chunk_band_attention_sample(q, k, v, cache_k, cache_v, rel_table):
    w = cache_k.shape[1]
    t = q.shape[1]
    k_all = jnp.concatenate([cache_k.astype(k.dtype), k], axis=1)
    v_all = jnp.concatenate([cache_v.astype(v.dtype), v], axis=1)
    s = jnp.einsum("bqhd,bkhd->bhqk", q, k_all).astype(jnp.float32) * ATTN_SCALE
    rel = (w + jnp.arange(t))[:, None] - jnp.arange(w + t)[None, :]
    s = s + _rel_bias(rel_table, rel)[None]
    p = jax.nn.softmax(s, axis=-1).astype(v.dtype)
    return jnp.einsum("bhqk,bkhd->bqhd", p, v_all)


def forgetting_attention_prompt(q, k, v, logf):
    b, t, h, d = q.shape
    c_t = jnp.cumsum(logf, axis=1).transpose(0, 2, 1)
    k_pos = jnp.arange(t)

    def block(i):
        start = i * Q_BLOCK
        q_blk = lax.dynamic_slice_in_dim(q, start, Q_BLOCK, axis=1)
        c_q = lax.dynamic_slice_in_dim(c_t, start, Q_BLOCK, axis=2)
        q_pos = start + jnp.arange(Q_BLOCK)
        s = jnp.einsum("bqhd,bkhd->bhqk", q_blk, k).astype(jnp.float32) * ATTN_SCALE
        s = s + (c_q[:, :, :, None] - c_t[:, :, None, :])
        s = jnp.where((k_pos[None, :] <= q_pos[:, None])[None, None], s, NEG_INF)
        p = jax.nn.softmax(s, axis=-1).astype(v.dtype)
        return jnp.einsum("bhqk,bkhd->bqhd", p, v)

    o = lax.map(block, jnp.arange(t // Q_BLOCK))
    return o.transpose(1, 0, 2, 3, 4).reshape(b, t, h, d)


def forgetting_attention_sample(q, k, v, logf, cache_k, cache_v, cache_logf):
    past = cache_k.shape[1]
    t = q.shape[1]
    k_all = jnp.concatenate([cache_k.astype(k.dtype), k], axis=1)
    v_all = jnp.concatenate([cache_v.astype(v.dtype), v], axis=1)
    lf_all = jnp.concatenate([cache_logf.astype(jnp.float32), logf], axis=1)
    c_t = jnp.cumsum(lf_all, axis=1).transpose(0, 2, 1)
    c_q = c_t[:, :, past:]
    s = jnp.einsum("bqhd,bkhd->bhqk", q, k_all).astype(jnp.float32) * ATTN_SCALE
    s = s + (c_q[:, :, :, None] - c_t[:, :, None, :])
    causal = jnp.arange(past + t)[None, :] <= (past + jnp.arange(t))[:, None]
    s = jnp.where(causal[None, None], s, NEG_INF)
    p = jax.nn.softmax(s, axis=-1).astype(v.dtype)
    return jnp.einsum("bhqk,bkhd->bqhd", p, v_all)


def _layer_tail(x, o_a, o_b, p, w_out, norm_ffn, w_gate, w_up, w_down, norm_ple, w_ple_gate, w_ple_proj):
    b, t = x.shape[:2]
    o = jnp.concatenate([o_a.reshape(b, t, D_A), o_b.reshape(b, t, D_B)], axis=-1) @ w_out
    x = x + o
    h = rms_norm(x, norm_ffn)
    x = x + (jax.nn.silu(h @ w_gate) * (h @ w_up)) @ w_down
    gate = jax.nn.sigmoid(rms_norm(x, norm_ple) @ w_ple_gate)
    return x + (p @ w_ple_proj) * gate


def setup_inputs(seed: int = 0) -> dict:
    key = jax.random.key(seed)
    ks = jax.random.split(key, 32)
    f32 = jnp.float32

    def nrm(k, shape, scale):
        return scale * jax.random.normal(k, shape, f32)

    w_a = min(A_WINDOW, PAST_LEN)
    return {
        "x_prompt": nrm(ks[0], (BATCH, SEQ, D_MODEL), 1.0),
        "x_sample": nrm(ks[1], (DEC_BATCH, DEC_SEQ, D_MODEL), 1.0),
        "cache_k_a": nrm(ks[2], (DEPTH, DEC_BATCH, w_a, N_HEADS_A, HEAD_DIM), 1.0),
        "cache_v_a": nrm(ks[3], (DEPTH, DEC_BATCH, w_a, N_HEADS_A, HEAD_DIM), 1.0),
        "cache_k_b": nrm(ks[4], (DEPTH, DEC_BATCH, PAST_LEN, N_HEADS_B, HEAD_DIM), 1.0),
        "cache_v_b": nrm(ks[5], (DEPTH, DEC_BATCH, PAST_LEN, N_HEADS_B, HEAD_DIM), 1.0),
        "cache_logf_b": jax.nn.log_sigmoid(FORGET_BIAS_INIT + nrm(ks[6], (DEPTH, DEC_BATCH, PAST_LEN, N_HEADS_B), 1.0)),
        "p_prompt": nrm(ks[7], (DEPTH, BATCH, SEQ, PLE_DIM), 1.0),
        "p_sample": nrm(ks[8], (DEPTH, DEC_BATCH, DEC_SEQ, PLE_DIM), 1.0),
        "norm_mix": 1.0 + nrm(ks[9], (DEPTH, D_MODEL), 0.05),
        "w_in": nrm(ks[10], (DEPTH, D_MODEL, D_IN), D_MODEL ** -0.5),
        "b_f": FORGET_BIAS_INIT + nrm(ks[11], (DEPTH, N_HEADS_B), 0.5),
        "q_norm_a": 1.0 + nrm(ks[12], (DEPTH, HEAD_DIM), 0.05),
        "k_norm_a": 1.0 + nrm(ks[13], (DEPTH, HEAD_DIM), 0.05),
        "q_norm_b": 1.0 + nrm(ks[14], (DEPTH, HEAD_DIM), 0.05),
        "k_norm_b": 1.0 + nrm(ks[15], (DEPTH, HEAD_DIM), 0.05),
        "rel_bias_a": nrm(ks[16], (DEPTH, N_HEADS_A, 2 * REL_CLIP + 1), 0.3),
        "w_out": nrm(ks[17], (DEPTH, D_MIX, D_MODEL), D_MIX ** -0.5),
        "norm_ffn": 1.0 + nrm(ks[18], (DEPTH, D_MODEL), 0.05),
        "w_gate": nrm(ks[19], (DEPTH, D_MODEL, D_FF), D_MODEL ** -0.5),
        "w_up": nrm(ks[20], (DEPTH, D_MODEL, D_FF), D_MODEL ** -0.5),
        "w_down": nrm(ks[21], (DEPTH, D_FF, D_MODEL), D_FF ** -0.5),
        "norm_ple": 1.0 + nrm(ks[22], (DEPTH, D_MODEL), 0.05),
        "w_ple_gate": nrm(ks[23], (DEPTH, D_MODEL, D_MODEL), D_MODEL ** -0.5),
        "w_ple_proj": nrm(ks[24], (DEPTH, PLE_DIM, D_MODEL), PLE_DIM ** -0.5),
    }


def reference(x_prompt, x_sample, cache_k_a, cache_v_a, cache_k_b, cache_v_b, cache_logf_b,
              p_prompt, p_sample, norm_mix, w_in, b_f, q_norm_a, k_norm_a, q_norm_b, k_norm_b,
              rel_bias_a, w_out, norm_ffn, w_gate, w_up, w_down, norm_ple, w_ple_gate, w_ple_proj):
    xp = x_prompt
    xs = x_sample
    ka_p, va_p, kb_p, vb_p, lf_p = [], [], [], [], []
    ka_s, va_s, kb_s, vb_s, lf_s = [], [], [], [], []
    for i in range(DEPTH):
        tail_w = (w_out[i], norm_ffn[i], w_gate[i], w_up[i], w_down[i], norm_ple[i], w_ple_gate[i], w_ple_proj[i])
        h = rms_norm(xp, norm_mix[i])
        q_a, k_a, v_a, q_b, k_b, v_b, logf = _mix_projections(
            h, w_in[i], b_f[i], q_norm_a[i], k_norm_a[i], q_norm_b[i], k_norm_b[i])
        o_a = chunk_band_attention_prompt(q_a, k_a, v_a, rel_bias_a[i])
        o_b = forgetting_attention_prompt(q_b, k_b, v_b, logf)
        xp = _layer_tail(xp, o_a, o_b, p_prompt[i], *tail_w)
        t_p = k_a.shape[1]
        keep = min(A_WINDOW, t_p)
        ka_p.append(k_a[:, t_p - keep:])
        va_p.append(v_a[:, t_p - keep:])
        kb_p.append(k_b)
        vb_p.append(v_b)
        lf_p.append(logf)
        h = rms_norm(xs, norm_mix[i])
        q_a, k_a, v_a, q_b, k_b, v_b, logf = _mix_projections(
            h, w_in[i], b_f[i], q_norm_a[i], k_norm_a[i], q_norm_b[i], k_norm_b[i])
        o_a = chunk_band_attention_sample(q_a, k_a, v_a, cache_k_a[i], cache_v_a[i], rel_bias_a[i])
        o_b = forgetting_attention_sample(q_b, k_b, v_b, logf, cache_k_b[i], cache_v_b[i], cache_logf_b[i])
        xs = _layer_tail(xs, o_a, o_b, p_sample[i], *tail_w)
        ka_s.append(k_a)
        va_s.append(v_a)
        kb_s.append(k_b)
        vb_s.append(v_b)
        lf_s.append(logf)
    return (xp, xs,
            jnp.stack(ka_p), jnp.stack(va_p), jnp.stack(kb_p), jnp.stack(vb_p), jnp.stack(lf_p),
            jnp.stack(ka_s), jnp.stack(va_s), jnp.stack(kb_s), jnp.stack(vb_s), jnp.stack(lf_s))
```

```python
import contextlib
import numpy as np
import concourse.bass as bass
import concourse.mybir as mybir
from concourse.bass_utils import run_bass_kernel_spmd

F32 = mybir.dt.float32
BF16 = mybir.dt.bfloat16
AF = mybir.ActivationFunctionType
ALU = mybir.AluOpType
AX = mybir.AxisListType

NCORES = 8
D = 1024
T = 2048
NT = T // 128
SEQ_PER_CORE = 4
DIN = 3080
DFF = 2816
NFF = DFF // 128
PLE = 256
PAST = 4096
NCT = PAST // 128
WA = 512
RA = 6
EPS = 1e-6
NEG = -30000.0
EIDX = {0: 0, 1: 1, 2: 1, 3: 2, 4: 3}

ENGS = ('pe', 'act', 'dve', 'pool', 'sp')


class Op:
    __slots__ = ('eng', 'fn', 'deps', 'signal', 'semval', 'sem', 'is_dma')

    def __init__(self, eng, fn, is_dma=False):
        self.eng = eng
        self.fn = fn
        self.deps = []
        self.signal = False
        self.semval = None
        self.sem = None
        self.is_dma = is_dma


class Sched:
    def __init__(self, nc):
        self.nc = nc
        self.ops = {e: [] for e in ENGS}
        self.W = {}
        self.R = {}
        self.dma_cnt = {}
        self.dma_last = {}
        self.out_last = {}

    def _deps(self, op, reads, writes, ident):
        deps = []
        for k in reads:
            for e, w in self.W.get(k, {}).items():
                deps.append(w)
        for k in writes:
            for e, w in self.W.get(k, {}).items():
                if e != ident or op.is_dma or ident != 'pe':
                    deps.append(w)
            for e, r in self.R.get(k, {}).items():
                if e != ident or op.is_dma or ident != 'pe':
                    deps.append(r)
        seen = set()
        out = []
        for d in deps:
            if id(d) not in seen and d is not op:
                seen.add(id(d))
                out.append(d)
        op.deps = out
        for k in reads:
            self.R.setdefault(k, {})[ident] = op
        for k in writes:
            self.W[k] = {ident: op}
            self.R[k] = {}

    def op(self, eng, fn, reads=(), writes=()):
        o = Op(eng, fn)
        self._deps(o, reads, writes, eng)
        self.ops[eng].append(o)
        return o

    def dma(self, queue, fn, reads=(), writes=(), sem=None, is_output=False):
        o = Op(queue, fn, is_dma=True)
        o.sem = sem
        self.dma_cnt[sem] = self.dma_cnt.get(sem, 0) + 16
        o.semval = self.dma_cnt[sem]
        self._deps(o, reads, writes, 'dma:' + sem)
        self.ops[queue].append(o)
        self.dma_last[sem] = o
        if is_output:
            self.out_last[sem] = o
        return o

    def barrier(self):
        last = []
        for e in ENGS:
            for o in reversed(self.ops[e]):
                if o.fn is not None and not o.is_dma:
                    last.append(o)
                    break
        last += list(self.dma_last.values())
        for e in ENGS:
            b = Op(e, None)
            b.deps = list(last)
            self.ops[e].append(b)
        self.W = {}
        self.R = {}

    def emit(self):
        nc = self.nc
        fin = Op('sp', None)
        fin.deps = list(self.out_last.values())
        self.ops['sp'].append(fin)
        for e in ENGS:
            for o in self.ops[e]:
                for d in o.deps:
                    if not d.is_dma:
                        d.signal = True
        for e in ENGS:
            c = 0
            for o in self.ops[e]:
                if not o.is_dma and o.fn is not None and o.signal:
                    c += 1
                    o.semval = c
                    o.sem = 'eng:' + e
        stats = {}
        with contextlib.ExitStack() as es:
            sems = {}
            for e in ENGS:
                sems['eng:' + e] = es.enter_context(nc.semaphore('s_' + e))
            for s in self.dma_cnt:
                sems[s] = es.enter_context(nc.semaphore('d_' + s))
            block = es.enter_context(nc.Block())

            def run(e, engobj):
                waited = {}
                nw = 0
                for o in self.ops[e]:
                    for d in o.deps:
                        if waited.get(d.sem, 0) >= d.semval:
                            continue
                        engobj.wait_ge(sems[d.sem], d.semval)
                        waited[d.sem] = d.semval
                        nw += 1
                    if o.fn is None:
                        continue
                    ins = o.fn(engobj)
                    if o.is_dma:
                        ins.then_inc(sems[o.sem], 16)
                    elif o.signal:
                        ins.then_inc(sems[o.sem], 1)
                stats[e] = (len(self.ops[e]), nw)

            @block.tensor
            def _(eng):
                run('pe', eng)

            @block.scalar
            def _(eng):
                run('act', eng)

            @block.vector
            def _(eng):
                run('dve', eng)

            @block.gpsimd
            def _(eng):
                run('pool', eng)

            @block.sync
            def _(eng):
                run('sp', eng)
        return stats


class Arena:
    def __init__(self, tensor, nbytes):
        self.t = tensor
        self.nbytes = nbytes
        self.off = 0

    def alloc(self, shape, dt):
        nfree = int(np.prod(shape[1:]))
        sz = 4 if dt == F32 else 2
        nb = (nfree * sz + 63) // 64 * 64
        a = self.off
        self.off += nb
        assert self.off <= self.nbytes, f"arena overflow {self.off} > {self.nbytes}"
        v = self.t[:, a // 4:(a + nb) // 4]
        if dt != F32:
            v = v.bitcast(dt)
        v = v[:, 0:nfree]
        if len(shape) > 2:
            names = " ".join(f"d{i}" for i in range(1, len(shape)))
            kw = {f"d{i}": int(shape[i]) for i in range(1, len(shape))}
            v = v.rearrange(f"p ({names}) -> p {names}", **kw)
        return v[0:shape[0]]


C_TRI, C_ONES, C_TRIBLK, C_M0, C_M4, C_MBD01, C_M0N, C_M4N, C_MBDN = range(9)
NCONST_SB = 9 * 128 + 4
OFF_IDENT = 9 * 128 + 4
NCONST = 12 * 128 + 4


def make_consts():
    k = np.arange(128)[:, None]
    q = np.arange(128)[None, :]
    tri = (k <= q).astype(np.float32)
    ones = np.ones((128, 128), np.float32)
    triblk = ((k // 32 == q // 32) & (k <= q)).astype(np.float32)
    m0 = np.ones((128, 128), np.float32)
    m0[0:64, 64:128] = 0.0
    m4 = np.ones((128, 128), np.float32)
    m4[64:128, 0:64] = 0.0
    mbd01 = (k // 32 == q // 32).astype(np.float32)
    ident = np.eye(128, dtype=np.float32)
    maskc = np.where(k > q, NEG, 0.0).astype(np.float32)
    maskbd = np.where((k // 32 != q // 32) | (k > q), NEG, 0.0).astype(np.float32)
    ind4 = (np.arange(128)[:, None] // 32 == np.arange(4)[None, :]).astype(np.float32)
    return np.ascontiguousarray(
        np.concatenate([tri, ones, triblk, m0, m4, mbd01, (m0 - 1) * (-NEG) , (m4 - 1) * (-NEG), (mbd01 - 1) * (-NEG),
                        ind4, ident, maskc, maskbd], axis=1).astype(np.float32))


def build_program(n_prompt_seq=SEQ_PER_CORE, do_sample=True, do_phase2=True):
    nc = bass.Bass("TRN2", target_bir_lowering=False)

    def din(name, shape):
        return nc.dram_tensor(name, list(shape), F32, kind="ExternalInput").ap()

    def dout(name, shape):
        return nc.dram_tensor(name, list(shape), F32, kind="ExternalOutput").ap()

    xp = din("xp", [SEQ_PER_CORE, T, D])
    pp = din("pp", [SEQ_PER_CORE, T, PLE])
    xs = din("xs", [128, D])
    psm = din("psm", [128, PLE])
    cka = din("cka", [SEQ_PER_CORE, WA, 512])
    cva = din("cva", [SEQ_PER_CORE, WA, 512])
    ckb = din("ckb", [SEQ_PER_CORE, PAST, 512])
    cvb = din("cvb", [SEQ_PER_CORE, PAST, 512])
    clf = din("clf", [SEQ_PER_CORE, PAST, 8])
    w_in = din("w_in", [D, DIN])
    w_out = din("w_out", [D, D])
    w_gate = din("w_gate", [D, DFF])
    w_up = din("w_up", [D, DFF])
    w_down = din("w_down", [DFF, D])
    w_pg = din("w_pg", [D, D])
    w_pp = din("w_pp", [PLE, D])
    gmix_d = din("gmix", [128, 8])
    gffn_d = din("gffn", [128, 8])
    gple_d = din("gple", [128, 8])
    bf_d = din("bf", [1, 8])
    qna_d = din("qna", [1, 64])
    kna_d = din("kna", [1, 64])
    qnb_d = din("qnb", [1, 64])
    knb_d = din("knb", [1, 64])
    biasT_d = din("biasT", [128, 8 * 4 * 128])
    biasN_d = din("biasN", [128, 8 * 128])
    cst_d = din("cst", [128, NCONST])

    yp = dout("yp", [SEQ_PER_CORE, T, D])
    ys = dout("ys", [128, D])
    kap = dout("kap", [SEQ_PER_CORE, WA, 512])
    vap = dout("vap", [SEQ_PER_CORE, WA, 512])
    kbp = dout("kbp", [SEQ_PER_CORE, T, 512])
    vbp = dout("vbp", [SEQ_PER_CORE, T, 512])
    lfp = dout("lfp", [SEQ_PER_CORE, T, 8])
    kas = dout("kas", [128, 512])
    vas = dout("vas", [128, 512])
    kbs = dout("kbs", [128, 512])
    vbs = dout("vbs", [128, 512])
    lfs = dout("lfs", [128, 8])

    NTILES = SEQ_PER_CORE * NT + 1
    def dscr(name, shape):
        return nc.dram_tensor(name, list(shape), BF16, kind="Internal").ap()
    wg_s = dscr("wg_s", [D, DFF])
    wu_s = dscr("wu_s", [D, DFF])
    wd_s = dscr("wd_s", [DFF, D])
    wpg_s = dscr("wpg_s", [D, D])
    wpp_s = dscr("wpp_s", [PLE, D])
    x1s = nc.dram_tensor("x1s", [NTILES * 128, D], F32, kind="Internal").ap()

    ARENA_BYTES = 212800
    with contextlib.ExitStack() as es:
        arena_t = es.enter_context(nc.sbuf_tensor("arena", [128, ARENA_BYTES // 4], F32))
        AR = Arena(arena_t, ARENA_BYTES)
        pbank = [es.enter_context(nc.psum_tensor(f"pb{i}", [128, 512], F32)) for i in range(8)]
        pbank_bf = [pb[:, :].bitcast(BF16) for pb in pbank]
        S = Sched(nc)

        bank_rot = {'mm': [0, 1, 2, 3], 'st': [4, 5, 6], 'oa': [7]}
        bank_ctr = {'mm': 0, 'st': 0, 'oa': 0}

        held = set()
        skipped = [0]

        def nextbank(cls, hold=False):
            lst = bank_rot[cls]
            for _ in range(len(lst)):
                b = lst[bank_ctr[cls] % len(lst)]
                bank_ctr[cls] += 1
                if b not in held:
                    if hold:
                        held.add(b)
                    return b
                skipped[0] += 1
            raise RuntimeError("all PSUM banks of class %s are held" % cls)

        def release(b):
            held.discard(b)

        def PK(b):
            return ('ps', b)

        def MM(out, lhsT, rhs, start, stop, reads, writes):
            S.op('pe', lambda e: e.matmul(out, lhsT=lhsT, rhs=rhs, start=start, stop=stop),
                 reads=reads, writes=writes)

        def TR(out, in_, reads, writes):
            S.op('pe', lambda e: e.transpose(out=out, in_=in_, identity=ident_bf[:]),
                 reads=list(reads) + ['cbf'], writes=writes)

        def ACT(out, in_, func, reads, writes, **kw):
            S.op('act', lambda e: e.activation(out=out, in_=in_, func=func, **kw),
                 reads=reads, writes=writes)

        def TT(eng, out, in0, in1, op, reads, writes):
            S.op(eng, lambda e: e.tensor_tensor(out=out, in0=in0, in1=in1, op=op),
                 reads=reads, writes=writes)

        def TS(eng, out, in0, s1, s2, op0, op1, reads, writes):
            if s2 is None:
                S.op(eng, lambda e: e.tensor_scalar(out=out, in0=in0, scalar1=s1, scalar2=None, op0=op0),
                     reads=reads, writes=writes)
            else:
                S.op(eng, lambda e: e.tensor_scalar(out=out, in0=in0, scalar1=s1, scalar2=s2, op0=op0, op1=op1),
                     reads=reads, writes=writes)

        def STT(out, in0, scalar, in1, op0, op1, reads, writes):
            S.op('dve', lambda e: e.scalar_tensor_tensor(out=out, in0=in0, scalar=scalar, in1=in1, op0=op0, op1=op1),
                 reads=reads, writes=writes)

        def CP(eng, out, in_, reads, writes):
            S.op(eng, lambda e: e.tensor_copy(out=out, in_=in_), reads=reads, writes=writes)

        def MEMSET(eng, ap, val, writes):
            S.op(eng, lambda e: e.memset(ap, val), writes=writes)

        def DMA(queue, out, in_, reads, writes, sem, is_output=False):
            S.dma(queue, lambda e: e.dma_start(out=out, in_=in_), reads=reads, writes=writes, sem=sem,
                  is_output=is_output)

        cst = AR.alloc([128, NCONST_SB], F32)
        tri_f = cst[:, C_TRI * 128:(C_TRI + 1) * 128]
        ones_f = cst[:, C_ONES * 128:(C_ONES + 1) * 128]
        triblk_f = cst[:, C_TRIBLK * 128:(C_TRIBLK + 1) * 128]
        m0_f = cst[:, C_M0 * 128:(C_M0 + 1) * 128]
        m4_f = cst[:, C_M4 * 128:(C_M4 + 1) * 128]
        mbd01_f = cst[:, C_MBD01 * 128:(C_MBD01 + 1) * 128]
        m0n_f = cst[:, C_M0N * 128:(C_M0N + 1) * 128]
        m4n_f = cst[:, C_M4N * 128:(C_M4N + 1) * 128]
        mbdn_f = cst[:, C_MBDN * 128:(C_MBDN + 1) * 128]
        ind4_f = cst[:, 9 * 128:9 * 128 + 4]
        cbf = AR.alloc([128, 3 * 128], BF16)
        ident_bf = cbf[:, 0:128]
        maskc_bf = cbf[:, 128:256]
        maskbd_bf = cbf[:, 256:384]
        gmix = AR.alloc([128, 8], F32)
        gffn = AR.alloc([128, 8], F32)
        gple = AR.alloc([128, 8], F32)
        bft = AR.alloc([128, 8], F32)
        qna = AR.alloc([128, 64], F32)
        kna = AR.alloc([128, 64], F32)
        qnb = AR.alloc([128, 64], F32)
        knb = AR.alloc([128, 64], F32)
        caug = AR.alloc([128, SEQ_PER_CORE, NCT * 8 * 3], BF16)
        tot8 = AR.alloc([128, SEQ_PER_CORE, 8], F32)
        ss = AR.alloc([128, 1], F32)
        rs = AR.alloc([128, 1], F32)
        cH = AR.alloc([128, 8], F32)
        PERSIST_END = AR.off

        DMA('sp', cst[:], cst_d[:, 0:NCONST_SB], [], ['cst'], 'c0_1')
        DMA('pool', cbf[:], cst_d[:, OFF_IDENT:OFF_IDENT + 3 * 128], [], ['cbf'], 'c1')
        DMA('sp', gmix[:], gmix_d[:, :], [], ['gmix'], 'c0_2')
        DMA('sp', gffn[:], gffn_d[:, :], [], ['gffn'], 'c0_3')
        DMA('sp', gple[:], gple_d[:, :], [], ['gple'], 'c0_4')
        DMA('sp', bft[:], bf_d.partition_broadcast(128), [], ['bft'], 'c0_5')
        DMA('sp', qna[:], qna_d.partition_broadcast(128), [], ['qna'], 'c0_6')
        DMA('sp', kna[:], kna_d.partition_broadcast(128), [], ['kna'], 'c0_7')
        DMA('sp', qnb[:], qnb_d.partition_broadcast(128), [], ['qnb'], 'c0_8')
        DMA('sp', knb[:], knb_d.partition_broadcast(128), [], ['knb'], 'c0_9')

        win = AR.alloc([128, 8, DIN], BF16)
        wout = AR.alloc([128, 8, D], BF16)
        Bhi = AR.alloc([128, 8, 4, 128], BF16)
        Blo = AR.alloc([128, 8, 4, 128], BF16)
        Xb = [AR.alloc([128, D], F32) for _ in range(3)]
        xn = AR.alloc([128, D], BF16)
        hTb = [AR.alloc([128, 8, 128], BF16) for _ in range(2)]
        kvo = [AR.alloc([128, 512], F32) for _ in range(4)]
        sqb = [AR.alloc([128, 512], F32)]
        ssqb = [AR.alloc([128, 8], F32) for _ in range(2)]
        qa_bf = AR.alloc([128, 8, 64], BF16)
        ka_bf = AR.alloc([128, 8, 64], BF16)
        qb_aug = AR.alloc([128, 8, 70], BF16)
        kb_aug = AR.alloc([128, 8, 70], BF16)
        qaT = [AR.alloc([64, 8, 128], BF16) for _ in range(2)]
        qbT = [AR.alloc([70, 8, 128], BF16) for _ in range(2)]
        KaT = AR.alloc([64, RA, 8, 128], BF16)
        Va = AR.alloc([128, RA, 8, 65], BF16)
        KbT = AR.alloc([70, NT, 8, 128], BF16)
        KBT_TILE_BYTES = 8 * 128 * 2
        kbt_base = AR.off - NT * KBT_TILE_BYTES
        Vb = AR.alloc([128, NT, 8, 65], BF16)
        lf_all = AR.alloc([128, NT, 8], F32)
        xf = AR.alloc([128, 8], F32)
        ef = AR.alloc([128, 8], F32)
        carry8 = AR.alloc([128, 8], F32)
        C8 = AR.alloc([128, 8], F32)
        r1 = AR.alloc([128, 8], F32)
        r2 = AR.alloc([128, 8], F32)
        sel8 = AR.alloc([128, 8], F32)
        PT = [AR.alloc([128, 512], BF16) for _ in range(3)]
        rinv = AR.alloc([128, 8], F32)
        O_bf = AR.alloc([128, D], BF16)
        OT = AR.alloc([128, 8, 128], BF16)
        P1_END = AR.off
        print("phase1 arena bytes", P1_END)

        SA = Arena(arena_t, kbt_base + NT * KBT_TILE_BYTES)
        SA.off = kbt_base + KBT_TILE_BYTES
        stgK = [SA.alloc([128, 512], F32) for _ in range(2)]
        stgV = [SA.alloc([128, 512], F32) for _ in range(2)]
        kc_aug = [SA.alloc([128, 8, 70], BF16) for _ in range(2)]
        vc_aug = [SA.alloc([128, 8, 65], BF16) for _ in range(2)]
        kcT = [SA.alloc([70, 8, 128], BF16) for _ in range(2)]
        PTz = [SA.alloc([128, 8, 128], BF16) for _ in range(2)]
        accA = SA.alloc([128, 8, 65], F32)
        accB = SA.alloc([128, 8, 65], F32)
        BNhi = SA.alloc([128, 8, 128], BF16)
        BNlo = SA.alloc([128, 8, 128], BF16)
        vb_base = kbt_base + NT * KBT_TILE_BYTES
        SB2 = Arena(arena_t, vb_base + NT * 8 * 65 * 2)
        SB2.off = vb_base + 1088
        BNf = SB2.alloc([128, 8, 128], F32)
        stgK3 = SB2.alloc([128, 512], F32)
        stgV3 = SB2.alloc([128, 512], F32)
        PA = Arena(arena_t, vb_base + 16 * 1024)
        PA.off = vb_base
        clf_sb = PA.alloc([128, NCT, 8], F32)
        Tsb = PA.alloc([128, NCT, 8], F32)
        Pincl = PA.alloc([128, NCT, 8], F32)
        ex8 = PA.alloc([128, NCT, 8], F32)
        Cc8 = PA.alloc([128, NCT, 8], F32)
        cr1 = PA.alloc([128, NCT, 8], F32)
        cr2 = PA.alloc([128, NCT, 8], F32)
        cbf1 = PA.alloc([128, NCT, 8], BF16)
        cbf2 = PA.alloc([128, NCT, 8], BF16)
        cbf3 = PA.alloc([128, NCT, 8], BF16)

        w_in_v = w_in.rearrange("(c p) n -> p c n", p=128)
        WIN_TILES = [(3072, DIN), (0, 512), (512, 1024), (1024, 1536), (1536, 2048), (2048, 2560), (2560, 3072)]
        for (c0, c1) in WIN_TILES:
            DMA('pool', win[:, :, c0:c1], w_in_v[:, :, c0:c1], [], [('win', c0)], f'w1_{c0}')
        DMA('pool', wout[:], w_out.rearrange("(c p) n -> p c n", p=128), [], ['wout'], 'w1b')

        PB = Arena(arena_t, kbt_base + NT * KBT_TILE_BYTES)
        PB.off = kbt_base
        Bf = PB.alloc([128, 8, 4, 128], F32)
        Bf2 = PB.alloc([128, 8 * 4 * 128], F32)
        Bff = Bf.rearrange("p h k q -> p (h k q)")
        DMA('sp', Bff, biasT_d[:, :], [], ['Bf'], 'c2')
        TS('dve', Bff, Bff, 8.0, None, ALU.mult, None, ['Bf'], ['Bf'])
        CP('dve', cH[:], Bf[:, :, 1, 0], ['Bf'], ['cH'])
        for idx in (0, 2, 3):
            TT('dve', Bf[:, :, idx, :], Bf[:, :, idx, :], cH.unsqueeze(2).to_broadcast([128, 8, 128]), ALU.subtract,
               ['Bf', 'cH'], ['Bf'])
        for (idx, mf, mn) in ((0, m0_f, m0n_f), (3, m4_f, m4n_f)):
            TT('dve', Bf[:, :, idx, :], Bf[:, :, idx, :], mf.unsqueeze(1).to_broadcast([128, 8, 128]), ALU.mult,
               ['Bf', 'cst'], ['Bf'])
            TT('dve', Bf[:, :, idx, :], Bf[:, :, idx, :], mn.unsqueeze(1).to_broadcast([128, 8, 128]), ALU.add,
               ['Bf', 'cst'], ['Bf'])
        Bhf = Bhi.rearrange("p h k q -> p (h k q)")
        Blf = Blo.rearrange("p h k q -> p (h k q)")
        CP('dve', Bhf, Bff, ['Bf'], ['Bhi'])
        TT('dve', Bf2[:], Bff, Bhf, ALU.subtract, ['Bf', 'Bhi'], ['Bf2'])
        CP('dve', Blf, Bf2[:], ['Bf2'], ['Blo'])

        if do_sample:
            for s in range(SEQ_PER_CORE):
                clf_v = clf[s].rearrange("(t p) h -> p t h", p=128)
                for t0 in range(0, NCT, 4):
                    DMA('sp', clf_sb[:, t0:t0 + 4, :], clf_v[:, t0:t0 + 4, :], [], ['clf_sb'], 'c3')
                clf2 = clf_sb.rearrange("p t h -> p (t h)")
                bT = nextbank('mm')
                bR = nextbank('mm')
                MM(pbank[bT][:, 0:256], ones_f, clf2, True, True, ['cst', 'clf_sb'], [PK(bT)])
                MM(pbank[bR][:, 0:256], tri_f, clf2, True, True, ['cst', 'clf_sb'], [PK(bR)])
                Ts2 = Tsb.rearrange("p t h -> p (t h)")
                ACT(Ts2, pbank[bT][:, 0:256], AF.Copy, [PK(bT)], ['Tsb'])
                for h in range(8):
                    S.op('dve', lambda e, h=h: e.tensor_tensor_scan(
                        out=Pincl[:, :, h], data0=ones_f[:, 0:NCT], data1=Tsb[:, :, h], initial=0.0,
                        op0=ALU.mult, op1=ALU.add), reads=['Tsb', 'cst'], writes=['Pincl'])
                P2 = Pincl.rearrange("p t h -> p (t h)")
                e2 = ex8.rearrange("p t h -> p (t h)")
                c2 = Cc8.rearrange("p t h -> p (t h)")
                STT(e2, Ts2, -1.0, P2, ALU.mult, ALU.add, ['Tsb', 'Pincl'], ['ex8'])
                TS('dve', e2, e2, 8.0, None, ALU.mult, None, ['ex8'], ['ex8'])
                STT(c2, pbank[bR][:, 0:256], 8.0, e2, ALU.mult, ALU.add, [PK(bR), 'ex8'], ['Cc8'])
                TS('dve', tot8[:, s, :], Pincl[:, NCT - 1, :], 8.0, None, ALU.mult, None, ['Pincl'], ['tot8'])
                b1 = cbf1.rearrange("p t h -> p (t h)")
                b2 = cbf2.rearrange("p t h -> p (t h)")
                b3 = cbf3.rearrange("p t h -> p (t h)")
                q1 = cr1.rearrange("p t h -> p (t h)")
                q2 = cr2.rearrange("p t h -> p (t h)")
                CP('dve', b1, c2, ['Cc8'], ['cbf1'])
                TT('dve', q1, c2, b1, ALU.subtract, ['Cc8', 'cbf1'], ['cr1'])
                CP('dve', b2, q1, ['cr1'], ['cbf2'])
                TT('dve', q2, q1, b2, ALU.subtract, ['cr1', 'cbf2'], ['cr2'])
                CP('dve', b3, q2, ['cr2'], ['cbf3'])
                cav = caug[:, s, :].rearrange("p (t h c) -> p t h c", t=NCT, h=8)
                for ci, bsrc in enumerate((cbf1, cbf2, cbf3)):
                    TS('dve', cav[:, :, :, ci], bsrc[:], -1.0, None, ALU.mult, None,
                       [f'cbf{ci + 1}'], ['caug'])
        S.barrier()

        MEMSET('pool', qb_aug[:, :, 67:70], 1.0, ['qb_aug'])
        MEMSET('pool', kb_aug[:, :, 64:67], 1.0, ['kb_aug'])
        MEMSET('pool', Va[:, :, :, 64:65], 1.0, ['Va_ones'])
        MEMSET('pool', Vb[:, :, :, 64:65], 1.0, ['Vb_ones'])

        xcnt = [0]
        tile_ctr = [0]
        conv_jobs = []
        for kc in range(8):
            for (c0, c1) in ((0, 1408), (1408, DFF)):
                conv_jobs.append((wg_s[kc * 128:(kc + 1) * 128, c0:c1], w_gate[kc * 128:(kc + 1) * 128, c0:c1]))
        for kc in range(8):
            for (c0, c1) in ((0, 1408), (1408, DFF)):
                conv_jobs.append((wu_s[kc * 128:(kc + 1) * 128, c0:c1], w_up[kc * 128:(kc + 1) * 128, c0:c1]))
        for kc in range(NFF):
            conv_jobs.append((wd_s[kc * 128:(kc + 1) * 128, :], w_down[kc * 128:(kc + 1) * 128, :]))
        for kc in range(8):
            conv_jobs.append((wpg_s[kc * 128:(kc + 1) * 128, :], w_pg[kc * 128:(kc + 1) * 128, :]))
        for kc in range(2):
            conv_jobs.append((wpp_s[kc * 128:(kc + 1) * 128, :], w_pp[kc * 128:(kc + 1) * 128, :]))
        conv_pos = [0]

        def emit_conv(n=1):
            for _ in range(n):
                if conv_pos[0] < len(conv_jobs):
                    o, i = conv_jobs[conv_pos[0]]
                    conv_pos[0] += 1
                    DMA('pool', o, i, [], ['wconv'], 'cvw')

        def rmsnorm_to_T(X, kx, gain, gkey, dstT, dkey='hT'):
            ACT(xn[:], X[:], AF.Square, [kx], ['xn', 'ss'], accum_out=ss[:])
            ACT(rs[:], ss[:], AF.Ln, ['ss'], ['rs'], scale=1.0 / D, bias=EPS)
            ACT(rs[:], rs[:], AF.Exp, ['rs'], ['rs'], scale=-0.5)
            TS('dve', xn[:], X[:], rs[:], None, ALU.mult, None, [kx, 'rs'], ['xn'])
            b = nextbank('mm')
            for c in range(8):
                TR(pbank_bf[b][:, c * 128:(c + 1) * 128], xn[:, c * 128:(c + 1) * 128], ['xn'], [PK(b)])
            TT('dve', dstT[:], pbank_bf[b][:, :].rearrange("p (c t) -> p c t", c=8),
               gain.unsqueeze(2).to_broadcast([128, 8, 128]), ALU.mult, [PK(b), gkey], [dkey])

        def head_norm_a(b, sq_buf, sqk, ssq_buf, ssk):
            ACT(sq_buf[:], pbank[b][:, :], AF.Square, [PK(b)], [sqk])
            S.op('dve', lambda e: e.tensor_reduce(out=ssq_buf[:], in_=sq_buf.rearrange("p (h d) -> p h d", h=8),
                                                  axis=AX.X, op=ALU.add), reads=[sqk], writes=[ssk])

        def head_norm_b(b, sq_buf, sqk, ssq_buf, ssk, gain, gkey, out_ap, out_key):
            pv = pbank[b][:, :].rearrange("p (h d) -> p h d", h=8)
            ACT(ssq_buf[:], ssq_buf[:], AF.Ln, [ssk], [ssk], scale=1.0 / 64, bias=EPS)
            ACT(ssq_buf[:], ssq_buf[:], AF.Exp, [ssk], [ssk], scale=-0.5)
            tv = sq_buf.rearrange("p (h d) -> p h d", h=8)
            TT('dve', tv, pv, ssq_buf.unsqueeze(2).to_broadcast([128, 8, 64]), ALU.mult, [PK(b), ssk], [sqk])
            TT('dve', out_ap, tv, gain.unsqueeze(1).to_broadcast([128, 8, 64]), ALU.mult, [sqk, gkey], [out_key])

        def stageA1(seq, j, sample=False, defer_pe=False):
            slot = xcnt[0] % 3
            xcnt[0] += 1
            par = tile_ctr[0] % 2
            tile_ctr[0] += 1
            X = Xb[slot]
            kx = ('X', slot)
            src = xs[:, :] if sample else xp[seq, j * 128:(j + 1) * 128, :]
            DMA('sp', X[:], src, [], [kx], f'x{slot}')
            ACT(xn[:], X[:], AF.Square, [kx], ['xn', 'ss'], accum_out=ss[:])
            ACT(rs[:], ss[:], AF.Ln, ['ss'], ['rs'], scale=1.0 / D, bias=EPS)
            ACT(rs[:], rs[:], AF.Exp, ['rs'], ['rs'], scale=-0.5)
            TS('dve', xn[:], X[:], rs[:], None, ALU.mult, None, [kx, 'rs'], ['xn'])

            def pe_part():
                b = nextbank('mm')
                for c in range(8):
                    TR(pbank_bf[b][:, c * 128:(c + 1) * 128], xn[:, c * 128:(c + 1) * 128], ['xn'], [PK(b)])
                TT('dve', hTb[par][:], pbank_bf[b][:, :].rearrange("p (c t) -> p c t", c=8),
                   gmix.unsqueeze(2).to_broadcast([128, 8, 128]), ALU.mult, [PK(b), 'gmix'], [('hT', par)])
            ctx = dict(slot=slot, par=par, j=j, seq=seq, sample=sample)
            if defer_pe:
                ctx['pe_part'] = pe_part
            else:
                pe_part()
            return ctx

        def stageA2(ctx):
            seq = ctx['seq']; j = ctx['j']; sample = ctx['sample']; par = ctx['par']
            hT = hTb[par]
            hk = ('hT', par)
            ra = j % RA
            chunks = []

            def inproj(c0, n, hold=False):
                b = nextbank('mm', hold)
                for kc in range(8):
                    MM(pbank[b][:, 0:n], hT[:, kc, :], win[:, kc, c0:c0 + n], kc == 0, kc == 7,
                       [hk, ('win', c0)], [PK(b)])
                return b

            st_ = {}

            def qa1():
                st_['qa'] = inproj(0, 512, True)
                head_norm_a(st_['qa'], sqb[0], ('sq', 0), ssqb[0], ('ssq', 0))

            def qa2():
                head_norm_b(st_['qa'], sqb[0], ('sq', 0), ssqb[0], ('ssq', 0), qna, 'qna', qa_bf[:], 'qa_bf')
                release(st_['qa'])

            def qa3():
                b = nextbank('mm')
                for h in range(8):
                    TR(pbank_bf[b][0:64, h * 128:(h + 1) * 128], qa_bf[:, h, :], ['qa_bf'], [PK(b)])
                ACT(qaT[par][:], pbank_bf[b][0:64, :].rearrange("p (h t) -> p h t", h=8), AF.Copy, [PK(b)], [('qaT', par)])

            def ka1():
                st_['ka'] = inproj(512, 512, True)
                head_norm_a(st_['ka'], kvo[0], ('kvo', 0), ssqb[1], ('ssq', 1))

            def ka2():
                head_norm_b(st_['ka'], kvo[0], ('kvo', 0), ssqb[1], ('ssq', 1), kna, 'kna',
                            kvo[0].rearrange("p (h d) -> p h d", h=8), ('kvo', 0))
                release(st_['ka'])
                CP('pool', ka_bf[:], kvo[0].rearrange("p (h d) -> p h d", h=8), [('kvo', 0)], ['ka_bf'])
                if sample:
                    DMA('sp', kas[:, :], kvo[0][:], [('kvo', 0)], [], 'o0', True)
                elif j >= NT - 4:
                    r0 = (j - (NT - 4)) * 128
                    DMA('sp', kap[seq, r0:r0 + 128, :], kvo[0][:], [('kvo', 0)], [], 'o0', True)

            def ka3():
                b = nextbank('mm')
                for h in range(8):
                    TR(pbank_bf[b][0:64, h * 128:(h + 1) * 128], ka_bf[:, h, :], ['ka_bf'], [PK(b)])
                ACT(KaT[:, ra, :, :], pbank_bf[b][0:64, :].rearrange("p (h t) -> p h t", h=8), AF.Copy, [PK(b)], [('KaT', ra)])

            def va1():
                b = inproj(1024, 512)
                ACT(kvo[1][:], pbank[b][:, :], AF.Copy, [PK(b)], [('kvo', 1)])
                CP('pool', Va[:, ra, :, 0:64], kvo[1].rearrange("p (h d) -> p h d", h=8), [('kvo', 1)], [('Va', ra)])
                if sample:
                    DMA('sp', vas[:, :], kvo[1][:], [('kvo', 1)], [], 'o1', True)
                elif j >= NT - 4:
                    r0 = (j - (NT - 4)) * 128
                    DMA('sp', vap[seq, r0:r0 + 128, :], kvo[1][:], [('kvo', 1)], [], 'o1', True)

            def gf1():
                b = inproj(3072, 8)
                TT('dve', xf[:], pbank[b][:, 0:8], bft[:], ALU.add, [PK(b), 'bft'], ['xf'])

            def gf1b():
                ACT(ef[:], xf[:], AF.Exp, ['xf'], ['ef'], scale=-1.0)
                ACT(ef[:], ef[:], AF.Ln, ['ef'], ['ef'], bias=1.0)
                lfj = lf_all[:, j, :]
                TS('dve', lfj, ef[:], -1.0, None, ALU.mult, None, ['ef'], [('lf', j)])
                if sample:
                    DMA('sp', lfs[:, :], lf_all[:, 0, :], [('lf', 0)], [], 'o4', True)
                elif j == NT - 1:
                    lfp_v = lfp[seq].rearrange("(t p) h -> p t h", p=128)
                    for t0 in range(0, NT, 4):
                        DMA('sp', lfp_v[:, t0:t0 + 4, :], lf_all[:, t0:t0 + 4, :],
                            [('lf', t) for t in range(t0, t0 + 4)], [], 'o4', True)

            def gf2():
                lfj = lf_all[:, j, :]
                b = nextbank('mm', True)
                if sample:
                    MM(pbank[b][:, 0:8], triblk_f, lfj, True, True, ['cst', ('lf', j)], [PK(b)])
                    for s_ in range(SEQ_PER_CORE):
                        if s_ == 0:
                            TS('dve', sel8[:], tot8[:, 0, :], ind4_f[:, 0:1], None, ALU.mult, None,
                               ['tot8', 'cst'], ['sel8'])
                        else:
                            STT(sel8[:], tot8[:, s_, :], ind4_f[:, s_:s_ + 1], sel8[:], ALU.mult, ALU.add,
                                ['tot8', 'cst', 'sel8'], ['sel8'])
                    st_['cb'] = b
                else:
                    if j == 0:
                        MEMSET('dve', carry8[:], 0.0, ['carry8'])
                    MM(pbank[b][:, 0:8], tri_f, lfj, True, True, ['cst', ('lf', j)], [PK(b)])
                    MM(pbank[b][:, 8:16], ones_f, lfj, True, True, ['cst', ('lf', j)], [PK(b)])
                    st_['cb'] = b

            def gf3():
                b = st_['cb']
                release(b)
                if sample:
                    STT(C8[:], pbank[b][:, 0:8], 8.0, sel8[:], ALU.mult, ALU.add, [PK(b), 'sel8'], ['C8'])
                else:
                    STT(C8[:], pbank[b][:, 0:8], 8.0, carry8[:], ALU.mult, ALU.add, [PK(b), 'carry8'], ['C8'])
                    STT(carry8[:], pbank[b][:, 8:16], 8.0, carry8[:], ALU.mult, ALU.add, [PK(b), 'carry8'], ['carry8'])

            def qb1():
                st_['qb'] = inproj(1536, 512, True)
                head_norm_a(st_['qb'], sqb[0], ('sq', 0), ssqb[0], ('ssq', 0))

            def qb2():
                head_norm_b(st_['qb'], sqb[0], ('sq', 0), ssqb[0], ('ssq', 0), qnb, 'qnb', qb_aug[:, :, 0:64], 'qb_aug')
                release(st_['qb'])
                CP('dve', qb_aug[:, :, 64], C8[:], ['C8'], ['qb_aug'])
                TT('dve', r1[:], C8[:], qb_aug[:, :, 64], ALU.subtract, ['C8', 'qb_aug'], ['r1'])
                CP('dve', qb_aug[:, :, 65], r1[:], ['r1'], ['qb_aug'])
                TT('dve', r2[:], r1[:], qb_aug[:, :, 65], ALU.subtract, ['r1', 'qb_aug'], ['r2'])
                CP('dve', qb_aug[:, :, 66], r2[:], ['r2'], ['qb_aug'])

            def qb3():
                b = nextbank('mm')
                for h in range(8):
                    TR(pbank_bf[b][0:70, h * 128:(h + 1) * 128], qb_aug[:, h, :], ['qb_aug'], [PK(b)])
                CP('dve', qbT[par][:], pbank_bf[b][0:70, :].rearrange("p (h t) -> p h t", h=8), [PK(b)], [('qbT', par)])

            def kb1():
                st_['kb'] = inproj(2048, 512, True)
                head_norm_a(st_['kb'], kvo[2], ('kvo', 2), ssqb[1], ('ssq', 1))

            def kb2():
                head_norm_b(st_['kb'], kvo[2], ('kvo', 2), ssqb[1], ('ssq', 1), knb, 'knb',
                            kvo[2].rearrange("p (h d) -> p h d", h=8), ('kvo', 2))
                release(st_['kb'])
                CP('pool', kb_aug[:, :, 0:64], kvo[2].rearrange("p (h d) -> p h d", h=8), [('kvo', 2)], ['kb_aug'])
                TS('dve', kb_aug[:, :, 67:70], qb_aug[:, :, 64:67], -1.0, None, ALU.mult, None, ['qb_aug'], ['kb_aug'])
                if sample:
                    DMA('sp', kbs[:, :], kvo[2][:], [('kvo', 2)], [], 'o2', True)
                else:
                    DMA('sp', kbp[seq, j * 128:(j + 1) * 128, :], kvo[2][:], [('kvo', 2)], [], 'o2', True)

            def kb3():
                b = nextbank('mm')
                for h in range(8):
                    TR(pbank_bf[b][0:70, h * 128:(h + 1) * 128], kb_aug[:, h, :], ['kb_aug'], [PK(b)])
                CP('dve', KbT[:, j, :, :], pbank_bf[b][0:70, :].rearrange("p (h t) -> p h t", h=8), [PK(b)], [('KbT', j)])

            def vb1():
                b = inproj(2560, 512)
                ACT(kvo[3][:], pbank[b][:, :], AF.Copy, [PK(b)], [('kvo', 3)])
                CP('pool', Vb[:, j, :, 0:64], kvo[3].rearrange("p (h d) -> p h d", h=8), [('kvo', 3)], [('Vb', j)])
                if sample:
                    DMA('sp', vbs[:, :], kvo[3][:], [('kvo', 3)], [], 'o3', True)
                else:
                    DMA('sp', vbp[seq, j * 128:(j + 1) * 128, :], kvo[3][:], [('kvo', 3)], [], 'o3', True)

            def seqc(*fs):
                def f():
                    for g in fs:
                        g()
                return f
            return [seqc(gf1, qa1), seqc(gf1b, qa2, ka1), seqc(gf2, ka2, va1, qa3), seqc(gf3, qb1, ka3),
                    seqc(qb2, kb1), seqc(kb2, vb1, qb3), seqc(kb3, emit_conv)]

        pt_ctr = [0]
        pe_ctr = [0]

        def run_units(units, inject=None):
            def emit_st(u):
                b = nextbank('st')
                for i, (lhsT, rhs, extra, rd) in enumerate(u['st']):
                    o = pbank[b][:, i * 128:(i + 1) * 128]
                    MM(o, lhsT, rhs, True, len(extra) == 0, rd, [PK(b)])
                    for ei, (xr, xrd) in enumerate(extra):
                        MM(o, ident_bf[:], xr, False, ei == len(extra) - 1, ['cbf'] + xrd, [PK(b)])
                u['b_st'] = b

            def emit_rest(u):
                b = u['b_st']
                n = len(u['st'])
                ps = pt_ctr[0] % 3
                pt_ctr[0] += 1
                ACT(PT[ps][:, 0:n * 128], pbank[b][:, 0:n * 128], AF.Exp, [PK(b)], [('PT', ps)], scale=0.125)
                ob = u['ob']
                hh = u['hh']
                for i, (rhs, rd) in enumerate(u['pv']):
                    MM(pbank[ob][:, hh * 65:(hh + 1) * 65], PT[ps][:, i * 128:(i + 1) * 128], rhs,
                       u['first'] and i == 0, u['last'] and i == n - 1, [('PT', ps)] + rd, [PK(ob)])
                if u.get('after') is not None:
                    u['after']()

            LOOK = 2
            n_u = len(units)
            for ui in range(n_u + LOOK):
                if ui < n_u:
                    emit_st(units[ui])
                if ui - LOOK >= 0:
                    emit_rest(units[ui - LOOK])
                    if inject is not None:
                        inject(ui - LOOK)

        def finish_heads(ob, col0, nheads=4):
            ov = pbank[ob][:, 0:nheads * 65].rearrange("p (h c) -> p h c", h=nheads)
            S.op('dve', lambda e: e.reciprocal(out=rinv[:, 0:nheads], in_=ov[:, :, 64]), reads=[PK(ob)], writes=['rinv'])
            TT('dve', O_bf[:, col0:col0 + 64 * nheads].rearrange("p (h d) -> p h d", h=nheads), ov[:, :, 0:64],
               rinv[:, 0:nheads].unsqueeze(2).to_broadcast([128, nheads, 64]), ALU.mult, [PK(ob), 'rinv'], [('O_bf', col0 // 256)])

        def o_transpose(half):
            b = nextbank('mm')
            rd = [('O_bf', half * 2), ('O_bf', half * 2 + 1)]
            for c in range(4):
                cc = half * 4 + c
                TR(pbank_bf[b][:, c * 128:(c + 1) * 128], O_bf[:, cc * 128:(cc + 1) * 128], rd, [PK(b)])
            eng = 'act' if half == 0 else 'dve'
            if eng == 'act':
                ACT(OT[:, half * 4:(half + 1) * 4, :], pbank_bf[b][:, 0:512].rearrange("p (c t) -> p c t", c=4), AF.Copy,
                    [PK(b)], [('OT', half)])
            else:
                CP('dve', OT[:, half * 4:(half + 1) * 4, :], pbank_bf[b][:, 0:512].rearrange("p (c t) -> p c t", c=4),
                   [PK(b)], [('OT', half)])

        def out_proj_and_store(ctx, gtile):
            slot = ctx['slot']
            X = Xb[slot]
            kx = ('X', slot)
            for n in range(2):
                b = nextbank('mm')
                for kc in range(8):
                    MM(pbank[b][:, :], OT[:, kc, :], wout[:, kc, n * 512:(n + 1) * 512], kc == 0, kc == 7,
                       [('OT', kc // 4), 'wout'], [PK(b)])
                TT('dve', X[:, n * 512:(n + 1) * 512], X[:, n * 512:(n + 1) * 512], pbank[b][:, :], ALU.add,
                   [kx, PK(b)], [kx])
            DMA('sp', x1s[gtile * 128:(gtile + 1) * 128, :], X[:], [kx], [('x1s', gtile)], f'xs{slot}')

        def assign_oacc(units, after_fn):
            cur = None
            for u in units:
                if cur is None:
                    cur = nextbank('oa')
                u['ob'] = cur
                if 'hg_end' in u:
                    kind, hg = u['hg_end']
                    u['after'] = after_fn(cur, kind, hg)
                    cur = None

        def band_extra(h, o):
            if o == 0:
                return [(Bhi[:, h, 0, :], ['Bhi'])]
            if o in (1, 2):
                return []
            return [(Bhi[:, h, EIDX[o], :], ['Bhi']), (Blo[:, h, EIDX[o], :], ['Blo'])]

        def build_units(j, par):
            units = []
            tiles = list(range(max(0, j - 4), j + 1))
            for hg in range(2):
                for hh in range(4):
                    h = hg * 4 + hh
                    groups = [[t for t in tiles if t - (j - 4) <= 1], [t for t in tiles if t - (j - 4) >= 2]]
                    groups = [g for g in groups if g]
                    for gi, g in enumerate(groups):
                        u = dict(kind='A', h=h, hh=hh, first=(gi == 0), last=(gi == len(groups) - 1),
                                 st=[(KaT[:, t % RA, h, :], qaT[par][:, h, :], band_extra(h, t - (j - 4)),
                                      [('KaT', t % RA), ('qaT', par)]) for t in g],
                                 pv=[(Va[:, t % RA, h, :], [('Va', t % RA), 'Va_ones']) for t in g])
                        units.append(u)
                units[-1]['hg_end'] = ('A', hg)
            tilesb = list(range(0, j + 1))
            for hg in range(2):
                for hh in range(4):
                    h = hg * 4 + hh
                    groups = [tilesb[i:i + 4] for i in range(0, len(tilesb), 4)]
                    for gi, g in enumerate(groups):
                        u = dict(kind='B', h=h, hh=hh, first=(gi == 0), last=(gi == len(groups) - 1),
                                 st=[(KbT[:, t, h, :], qbT[par][:, h, :], ([(maskc_bf[:], [])] if t == j else []),
                                      [('KbT', t), ('qbT', par)]) for t in g],
                                 pv=[(Vb[:, t, h, :], [('Vb', t), 'Vb_ones']) for t in g])
                        units.append(u)
                units[-1]['hg_end'] = ('B', hg)
            return units

        def stageB(ctx, chunks=(), mid=None):
            j = ctx['j']
            par = ctx['par']
            units = build_units(j, par)

            pending = []
            cur_unit = [0]

            def after_fn(ob, kind, hg):
                col0 = (0 if kind == 'A' else 512) + hg * 256

                def f():
                    finish_heads(ob, col0)
                    if hg == 1:
                        pending.append((cur_unit[0], lambda: o_transpose(0 if kind == 'A' else 1)))
                return f
            assign_oacc(units, after_fn)
            chunks = list(chunks)
            nu = len(units)
            nch = len(chunks)
            sched_at = {}
            for k in range(nch):
                sched_at.setdefault(max(0, (k + 1) * nu // (nch + 1) - 1), []).append(chunks[k])
            mid_at = max(0, nu // 3 - 1)
            mid2_at = max(mid_at + 1, (2 * nu) // 3 - 1)
            state = {'mid': mid, 'mid2': None}

            def inject(i):
                while pending and pending[0][0] < i:
                    pending.pop(0)[1]()
                for c in sched_at.get(i, []):
                    c()
                if i == mid_at and state['mid'] is not None:
                    state['mid2'] = state['mid']()
                    state['mid'] = None
                if i >= mid2_at and state['mid2'] is not None:
                    state['mid2']()
                    state['mid2'] = None
                cur_unit[0] = i + 1
            run_units(units, inject)
            if state['mid'] is not None:
                state['mid2'] = state['mid']()
            if state['mid2'] is not None:
                state['mid2']()
            while pending:
                pending.pop(0)[1]()
            out_proj_and_store(ctx, ctx['gtile'])

        a1ctx = {}

        def do_A1(seq, j, defer_pe=False):
            c = stageA1(seq, j, defer_pe=defer_pe)
            c['gtile'] = seq * NT + j
            a1ctx[(seq, j)] = c
            return c.get('pe_part')

        for seq in range(n_prompt_seq):
            if (seq, 0) not in a1ctx:
                do_A1(seq, 0)
            for c in stageA2(a1ctx[(seq, 0)]):
                c()
            do_A1(seq, 1)
            for j in range(NT):
                chunks = stageA2(a1ctx[(seq, j + 1)]) if j + 1 < NT else []
                if j + 2 < NT:
                    mid = (lambda seq=seq, j=j: do_A1(seq, j + 2, True))
                elif j + 2 == NT + 1 and seq + 1 < n_prompt_seq:
                    mid = (lambda seq=seq: do_A1(seq + 1, 0, True))
                else:
                    mid = None
                stageB(a1ctx[(seq, j)], chunks, mid)

        if do_sample:
            S.barrier()
            MEMSET('pool', qb_aug[:, :, 67:70], 1.0, ['qb_aug'])
            MEMSET('pool', kb_aug[:, :, 64:67], 1.0, ['kb_aug'])
            for i in range(2):
                MEMSET('pool', kc_aug[i][:, :, 64:67], 1.0, [('kc_ones', i)])
                MEMSET('pool', vc_aug[i][:, :, 64:65], 1.0, [('vc_ones', i)])
                MEMSET('pool', PTz[i][:], 0.0, [('PTz', i)])
            MEMSET('dve', accA[:], 0.0, ['accA'])
            MEMSET('dve', accB[:], 0.0, ['accB'])
            BN2 = BNf.rearrange("p h q -> p (h q)")
            DMA('sp', BN2, biasN_d[:, :], [], ['BNf'], 'c2b')
            TS('dve', BN2, BN2, 8.0, None, ALU.mult, None, ['BNf'], ['BNf'])
            TT('dve', BNf[:], BNf[:], cH.unsqueeze(2).to_broadcast([128, 8, 128]), ALU.subtract, ['BNf', 'cH'], ['BNf'])
            TT('dve', BNf[:], BNf[:], mbd01_f.unsqueeze(1).to_broadcast([128, 8, 128]), ALU.mult, ['BNf', 'cst'], ['BNf'])
            TT('dve', BNf[:], BNf[:], mbdn_f.unsqueeze(1).to_broadcast([128, 8, 128]), ALU.add, ['BNf', 'cst'], ['BNf'])
            CP('dve', BNhi[:], BNf[:], ['BNf'], ['BNhi'])
            TT('dve', BNf[:], BNf[:], BNhi[:], ALU.subtract, ['BNf', 'BNhi'], ['BNf'])
            CP('dve', BNlo[:], BNf[:], ['BNf'], ['BNlo'])

            ctx = stageA1(0, 0, sample=True)
            ctx['gtile'] = SEQ_PER_CORE * NT
            for c in stageA2(ctx):
                c()
            par = ctx['par']
            stgK.append(stgK3)
            stgV.append(stgV3)
            ctiles_ = []
            for s_ in range(SEQ_PER_CORE):
                for i_ in range(WA // 128):
                    ctiles_.append((s_, i_, True))
                for i_ in range(NCT):
                    ctiles_.append((s_, i_, False))
            NCTL = len(ctiles_)
            cst8 = {}

            def c_load(t):
                s_, i_, band = ctiles_[t]
                k = t % 3
                ksrc = (cka if band else ckb)[s_, i_ * 128:(i_ + 1) * 128, :]
                vsrc = (cva if band else cvb)[s_, i_ * 128:(i_ + 1) * 128, :]
                DMA('sp', stgK[k][:], ksrc, [], [('stgK', k)], f'ck{k}')
                DMA('sp', stgV[k][:], vsrc, [], [('stgV', k)], f'cv{k}')

            def c_prep_k(t):
                s_, i_, band = ctiles_[t]
                k = t % 2
                nr = 64 if band else 70
                CP('dve', kc_aug[k][:, :, 0:64], stgK[t % 3].rearrange("p (h d) -> p h d", h=8),
                   [('stgK', t % 3)], [('kc', k)])
                rdk = [('kc', k)]
                if not band:
                    cav = caug[:, s_, :].rearrange("p (t h c) -> p t h c", t=NCT, h=8)
                    CP('pool', kc_aug[k][:, :, 67:70], cav[:, i_, :, :], ['caug'], [('kc', k)])
                    rdk.append(('kc_ones', k))
                b = nextbank('mm')
                for h in range(8):
                    TR(pbank_bf[b][0:nr, h * 128:(h + 1) * 128], kc_aug[k][:, h, 0:nr], rdk, [PK(b)])
                ACT(kcT[k][0:nr], pbank_bf[b][0:nr, :].rearrange("p (h t) -> p h t", h=8), AF.Copy,
                    [PK(b)], [('kcT', k)])

            def c_prep_v(t):
                k = t % 2
                ACT(vc_aug[k][:, :, 0:64], stgV[t % 3].rearrange("p (h d) -> p h d", h=8), AF.Copy,
                    [('stgV', t % 3)], [('vc', k)])

            def c_score(t):
                s_, i_, band = ctiles_[t]
                k = t % 2
                nr = 64 if band else 70
                bs = nextbank('st')
                qT = qaT[par] if band else qbT[par]
                qk = ('qaT', par) if band else ('qbT', par)
                kt = 2
                wb = band and i_ == 3
                for h in range(8):
                    o = pbank[bs][:, h * 32:(h + 1) * 32]
                    MM(o, kcT[k][0:nr, h, :], qT[0:nr, h, s_ * 32:(s_ + 1) * 32], True, not wb,
                       [('kcT', k), qk], [PK(bs)])
                    if wb:
                        MM(o, ident_bf[:], Bhi[:, h, kt, 0:32], False, False, ['cbf', 'Bhi'], [PK(bs)])
                        MM(o, ident_bf[:], Blo[:, h, kt, 0:32], False, True, ['cbf', 'Blo'], [PK(bs)])
                pz = PTz[k][:, :, s_ * 32:(s_ + 1) * 32]
                sv = pbank[bs][:, 0:256].rearrange("p (h q) -> p h q", h=8)
                ACT(pz, sv, AF.Exp, [PK(bs)], [('PTz', k)], scale=0.125)

            def c_pv(t):
                s_, i_, band = ctiles_[t]
                k = t % 2
                acc = accA if band else accB
                ak = 'accA' if band else 'accB'
                for half in range(2):
                    ob = nextbank('mm')
                    for hh in range(4):
                        h = half * 4 + hh
                        MM(pbank[ob][:, hh * 65:(hh + 1) * 65], PTz[k][:, h, :], vc_aug[k][:, h, :], True, True,
                           [('PTz', k), ('vc', k), ('vc_ones', k)], [PK(ob)])
                    TT('dve', acc[:, half * 4:(half + 1) * 4, :], acc[:, half * 4:(half + 1) * 4, :],
                       pbank[ob][:, 0:260].rearrange("p (h c) -> p h c", h=4), ALU.add, [ak, PK(ob)], [ak])
                if t + 1 == NCTL or ctiles_[t + 1][0] != s_:
                    for i2 in range(2):
                        MEMSET('pool', PTz[i2][:, :, s_ * 32:(s_ + 1) * 32], 0.0, [('PTz', i2)])

            c_load(0)
            c_load(1)
            for t in range(NCTL + 2):
                if t < NCTL:
                    c_prep_k(t)
                if 0 <= t - 2 < NCTL:
                    c_pv(t - 2)
                if 0 <= t - 1 < NCTL:
                    c_score(t - 1)
                if t + 2 < NCTL:
                    c_load(t + 2)
                if t < NCTL:
                    c_prep_v(t)

            units = []
            for hg in range(2):
                for hh in range(4):
                    h = hg * 4 + hh
                    units.append(dict(kind='A', h=h, hh=hh, first=True, last=True,
                                      st=[(KaT[:, 0, h, :], qaT[par][:, h, :],
                                           [(BNhi[:, h, :], ['BNhi']), (BNlo[:, h, :], ['BNlo'])],
                                           [('KaT', 0), ('qaT', par)])],
                                      pv=[(Va[:, 0, h, :], [('Va', 0), 'Va_ones'])]))
                units[-1]['hg_end'] = ('A', hg)
            for hg in range(2):
                for hh in range(4):
                    h = hg * 4 + hh
                    units.append(dict(kind='B', h=h, hh=hh, first=True, last=True,
                                      st=[(KbT[:, 0, h, :], qbT[par][:, h, :], [(maskbd_bf[:], [])],
                                           [('KbT', 0), ('qbT', par)])],
                                      pv=[(Vb[:, 0, h, :], [('Vb', 0), 'Vb_ones'])]))
                units[-1]['hg_end'] = ('B', hg)

            def after_fn(ob, kind, hg):
                acc = accA if kind == 'A' else accB
                ak = 'accA' if kind == 'A' else 'accB'

                def after():
                    TT('dve', acc[:, hg * 4:(hg + 1) * 4, :], acc[:, hg * 4:(hg + 1) * 4, :],
                       pbank[ob][:, 0:260].rearrange("p (h c) -> p h c", h=4), ALU.add, [ak, PK(ob)], [ak])
                return after
            assign_oacc(units, after_fn)
            S.op('pool', lambda e: e.memset(Va[:, 0, :, 64:65], 1.0), writes=['Va_ones'])
            S.op('pool', lambda e: e.memset(Vb[:, 0, :, 64:65], 1.0), writes=['Vb_ones'])
            run_units(units)
            for (acc, ak, col0) in ((accA, 'accA', 0), (accB, 'accB', 512)):
                S.op('dve', lambda e, acc=acc: e.reciprocal(out=rinv[:, 0:8], in_=acc[:, :, 64]), reads=[ak], writes=['rinv'])
                TT('dve', O_bf[:, col0:col0 + 512].rearrange("p (h d) -> p h d", h=8), acc[:, :, 0:64],
                   rinv[:, 0:8].unsqueeze(2).to_broadcast([128, 8, 64]), ALU.mult, [ak, 'rinv'],
                   [('O_bf', col0 // 256), ('O_bf', col0 // 256 + 1)])
                o_transpose(col0 // 512)
            out_proj_and_store(ctx, ctx['gtile'])

        if do_phase2:
            S.barrier()
            AR.off = PERSIST_END
            wg = AR.alloc([128, 8, DFF], BF16)
            wu = AR.alloc([128, 8, DFF], BF16)
            wd = AR.alloc([128, NFF, D], BF16)
            wpg = AR.alloc([128, 8, D], BF16)
            wpp = AR.alloc([128, 2, D], BF16)
            X2 = [AR.alloc([128, D], F32) for _ in range(3)]
            xnf = AR.alloc([128, D], BF16)
            xnp = AR.alloc([128, D], BF16)
            hTf = [AR.alloc([128, 8, 128], BF16) for _ in range(2)]
            hTp = AR.alloc([128, 8, 128], BF16)
            a_bf = AR.alloc([128, DFF], BF16)
            aT = AR.alloc([128, NFF, 128], BF16)
            p_sb = [AR.alloc([128, PLE], F32) for _ in range(2)]
            p_bf = AR.alloc([128, PLE], BF16)
            pT = [AR.alloc([128, 2, 128], BF16) for _ in range(3)]
            sig = [AR.alloc([128, 512], F32) for _ in range(2)]
            ssp = AR.alloc([128, 1], F32)
            rsp = AR.alloc([128, 1], F32)
            print("phase2 arena bytes", AR.off)
            emit_conv(len(conv_jobs))
            def wload(dst, src, nk, key, sem, step=4):
                v = src.rearrange("(c p) n -> p c n", p=128)
                for k0 in range(0, nk, step):
                    k1 = min(nk, k0 + step)
                    DMA('sp', dst[:, k0:k1, :], v[:, k0:k1, :], ['wconv'], [key], sem)
            wload(wg, wg_s, 8, 'wg', 'w2')
            wload(wu, wu_s, 8, 'wu', 'w2b')
            wload(wd, wd_s, NFF, 'wd', 'w3')
            wload(wpg, wpg_s, 8, 'wpg', 'w3b')
            wload(wpp, wpp_s, 2, 'wpp', 'w3c')

            tiles2 = []
            for seq in range(n_prompt_seq):
                for j in range(NT):
                    tiles2.append((seq * NT + j, pp[seq, j * 128:(j + 1) * 128, :], yp[seq, j * 128:(j + 1) * 128, :]))
            if do_sample:
                tiles2.append((SEQ_PER_CORE * NT, psm[:, :], ys[:, :]))
            NT2 = len(tiles2)
            bank_rot['mm'] = [0, 1, 2, 3, 4, 5, 6, 7]
            ctiles = [(c0, min(512, DFF - c0)) for c0 in range(0, DFF, 512)]

            def XK(t):
                return X2[t % 3], ('X2', t % 3)

            def rms_nonpe(X, kx, xnb, xk, ssb, sk, rsb, rk):
                ACT(xnb[:], X[:], AF.Square, [kx], [xk, sk], accum_out=ssb[:])
                ACT(rsb[:], ssb[:], AF.Ln, [sk], [rk], scale=1.0 / D, bias=EPS)
                ACT(rsb[:], rsb[:], AF.Exp, [rk], [rk], scale=-0.5)
                TS('dve', xnb[:], X[:], rsb[:], None, ALU.mult, None, [kx, rk], [xk])

            def rms_pe(xnb, xk, gain, gkey, dst, dkey):
                b = nextbank('mm')
                for c in range(8):
                    TR(pbank_bf[b][:, c * 128:(c + 1) * 128], xnb[:, c * 128:(c + 1) * 128], [xk], [PK(b)])
                TT('dve', dst[:], pbank_bf[b][:, :].rearrange("p (c t) -> p c t", c=8),
                   gain.unsqueeze(2).to_broadcast([128, 8, 128]), ALU.mult, [PK(b), gkey], [dkey])

            def P_nonpe(t):
                g, psrc, ydst = tiles2[t]
                X, kx = XK(t)
                pk = t % 2
                DMA('sp', X[:], x1s[g * 128:(g + 1) * 128, :], [('x1s', g)], [kx], f'y{t % 3}')
                DMA('sp', p_sb[pk][:], psrc, [], [('p', pk)], f'p{pk}')
                rms_nonpe(X, kx, xnf, 'xnf', ss, 'ss', rs, 'rs')
                CP('pool', p_bf[:], p_sb[pk][:], [('p', pk)], ['p_bf'])

            def P_pe(t):
                pk = t % 2
                rms_pe(xnf, 'xnf', gffn, 'gffn', hTf[pk], ('hTf', pk))
                b = nextbank('mm')
                for c in range(2):
                    TR(pbank_bf[b][:, c * 128:(c + 1) * 128], p_bf[:, c * 128:(c + 1) * 128], ['p_bf'], [PK(b)])
                ACT(pT[t % 3][:], pbank_bf[b][:, 0:256].rearrange("p (c t) -> p c t", c=2), AF.Copy, [PK(b)], [('pT', t % 3)])

            def GU_stage(t, cis):
                pk = t % 2
                hT2 = hTf[pk]
                hk = ('hTf', pk)
                for ci in cis:
                    c0, n = ctiles[ci]
                    bg = nextbank('mm')
                    for kc in range(8):
                        MM(pbank[bg][:, 0:n], hT2[:, kc, :], wg[:, kc, c0:c0 + n], kc == 0, kc == 7, [hk, 'wg'], [PK(bg)])
                    bu = nextbank('mm')
                    for kc in range(8):
                        MM(pbank[bu][:, 0:n], hT2[:, kc, :], wu[:, kc, c0:c0 + n], kc == 0, kc == 7, [hk, 'wu'], [PK(bu)])
                    k2 = ci % 2
                    ACT(sig[k2][:, 0:n], pbank[bg][:, 0:n], AF.Sigmoid, [PK(bg)], [('sig', k2)])
                    TT('dve', sig[k2][:, 0:n], pbank[bg][:, 0:n], sig[k2][:, 0:n], ALU.mult, [PK(bg), ('sig', k2)], [('sig', k2)])
                    TT('dve', a_bf[:, c0:c0 + n], sig[k2][:, 0:n], pbank[bu][:, 0:n], ALU.mult, [('sig', k2), PK(bu)], [('a', ci)])

            def AT_stage(t):
                for g0 in range(0, NFF, 8):
                    ng = min(8, NFF - g0)
                    b = nextbank('mm')
                    rd = sorted(set(('a', (c * 128) // 512) for c in range(g0, g0 + ng)))
                    for c in range(ng):
                        TR(pbank_bf[b][:, c * 128:(c + 1) * 128], a_bf[:, (g0 + c) * 128:(g0 + c + 1) * 128], rd, [PK(b)])
                    ACT(aT[:, g0:g0 + ng, :], pbank_bf[b][:, 0:ng * 128].rearrange("p (c t) -> p c t", c=ng), AF.Copy,
                        [PK(b)], [('aT', g0)])

            def DN_stage(t):
                X, kx = XK(t)
                for n in range(2):
                    b = nextbank('mm')
                    for kc in range(NFF):
                        MM(pbank[b][:, :], aT[:, kc, :], wd[:, kc, n * 512:(n + 1) * 512], kc == 0, kc == NFF - 1,
                           [('aT', (kc // 8) * 8), 'wd'], [PK(b)])
                    TT('dve', X[:, n * 512:(n + 1) * 512], X[:, n * 512:(n + 1) * 512], pbank[b][:, :], ALU.add,
                       [kx, PK(b)], [kx])

            def R2_nonpe(t):
                X, kx = XK(t)
                rms_nonpe(X, kx, xnp, 'xnp', ssp, 'ssp', rsp, 'rsp')

            def R2_pe(t):
                rms_pe(xnp, 'xnp', gple, 'gple', hTp, 'hTp')

            def PL_stage(t):
                g, psrc, ydst = tiles2[t]
                X, kx = XK(t)
                pk = t % 2
                for n in range(2):
                    bg = nextbank('mm')
                    for kc in range(8):
                        MM(pbank[bg][:, :], hTp[:, kc, :], wpg[:, kc, n * 512:(n + 1) * 512], kc == 0, kc == 7,
                           ['hTp', 'wpg'], [PK(bg)])
                    bp = nextbank('mm')
                    for kc in range(2):
                        MM(pbank[bp][:, :], pT[t % 3][:, kc, :], wpp[:, kc, n * 512:(n + 1) * 512], kc == 0, kc == 1,
                           [('pT', t % 3), 'wpp'], [PK(bp)])
                    k2 = n % 2
                    ACT(sig[k2][:], pbank[bg][:, :], AF.Sigmoid, [PK(bg)], [('sig', k2)])
                    TT('dve', sig[k2][:], pbank[bp][:, :], sig[k2][:], ALU.mult, [PK(bp), ('sig', k2)], [('sig', k2)])
                    TT('dve', X[:, n * 512:(n + 1) * 512], X[:, n * 512:(n + 1) * 512], sig[k2][:], ALU.add,
                       [kx, ('sig', k2)], [kx])
                DMA('sp', ydst, X[:], [kx], [], f'yo{t % 3}', True)

            P_nonpe(0)
            P_pe(0)
            if NT2 > 1:
                P_nonpe(1)
                P_pe(1)
            for k in range(NT2 + 1):
                if k >= 1:
                    R2_nonpe(k - 1)
                if k < NT2:
                    GU_stage(k, [0, 1, 2])
                if k >= 1:
                    R2_pe(k - 1)
                if k < NT2:
                    GU_stage(k, [3, 4, 5])
                if k >= 1:
                    PL_stage(k - 1)
                if k < NT2:
                    AT_stage(k)
                if k + 2 < NT2:
                    P_nonpe(k + 2)
                if k < NT2:
                    DN_stage(k)
                if k + 2 < NT2:
                    P_pe(k + 2)

        stats = S.emit()
        print("sched stats (ops, waits):", stats, "held-bank skips:", skipped[0], "still held:", sorted(held))
    return nc


_PROGRAM = {}


def _rel_bias_layout(rel):
    k = np.arange(128)[:, None]
    q = np.arange(128)[None, :]
    out = np.empty((128, 8, 4, 128), np.float32)
    for si, kt in enumerate((0, 1, 3, 4)):
        r = (4 - kt) * 128 + q - k
        idx = np.clip(r, -128, 128) + 128
        out[:, :, si, :] = rel[:, idx].transpose(1, 0, 2)
    idxn = np.clip((q % 32) - (k % 32), -128, 128) + 128
    outn = np.ascontiguousarray(rel[:, idxn].transpose(1, 0, 2))
    return np.ascontiguousarray(out.reshape(128, -1)), np.ascontiguousarray(outn.reshape(128, -1))


def kernel(x_prompt, x_sample, cache_k_a, cache_v_a, cache_k_b, cache_v_b, cache_logf_b,
           p_prompt, p_sample, norm_mix, w_in, b_f, q_norm_a, k_norm_a, q_norm_b, k_norm_b,
           rel_bias_a, w_out, norm_ffn, w_gate, w_up, w_down, norm_ple, w_ple_gate, w_ple_proj):
    f = lambda a: np.ascontiguousarray(np.asarray(a, dtype=np.float32))
    x_prompt = f(x_prompt); x_sample = f(x_sample)
    cache_k_a = f(cache_k_a); cache_v_a = f(cache_v_a); cache_k_b = f(cache_k_b); cache_v_b = f(cache_v_b)
    cache_logf_b = f(cache_logf_b); p_prompt = f(p_prompt); p_sample = f(p_sample)
    if 'nc' not in _PROGRAM:
        _PROGRAM['nc'] = build_program()
    nc = _PROGRAM['nc']
    biasT, biasN = _rel_bias_layout(f(rel_bias_a)[0])
    cst = make_consts()
    g2 = lambda g: np.ascontiguousarray(f(g)[0].reshape(8, 128).T)
    shared = dict(
        w_in=f(w_in)[0], w_out=f(w_out)[0], w_gate=f(w_gate)[0], w_up=f(w_up)[0], w_down=f(w_down)[0],
        w_pg=f(w_ple_gate)[0], w_pp=f(w_ple_proj)[0],
        gmix=g2(norm_mix), gffn=g2(norm_ffn), gple=g2(norm_ple),
        bf=f(b_f), qna=f(q_norm_a), kna=f(k_norm_a), qnb=f(q_norm_b), knb=f(k_norm_b),
        biasT=biasT, biasN=biasN, cst=cst)
    in_maps = []
    for c in range(NCORES):
        sl = slice(c * SEQ_PER_CORE, (c + 1) * SEQ_PER_CORE)
        m = dict(shared)
        m.update(
            xp=x_prompt[sl], pp=p_prompt[0, sl],
            xs=x_sample[sl].reshape(128, D), psm=p_sample[0, sl].reshape(128, PLE),
            cka=cache_k_a[0, sl].reshape(SEQ_PER_CORE, WA, 512), cva=cache_v_a[0, sl].reshape(SEQ_PER_CORE, WA, 512),
            ckb=cache_k_b[0, sl].reshape(SEQ_PER_CORE, PAST, 512), cvb=cache_v_b[0, sl].reshape(SEQ_PER_CORE, PAST, 512),
            clf=cache_logf_b[0, sl])
        in_maps.append(m)
    res = run_bass_kernel_spmd(nc, in_maps, core_ids=list(range(NCORES)))
    R = res.results
    cat = lambda k: np.concatenate([r[k] for r in R], axis=0)
    B = NCORES * SEQ_PER_CORE
    y_prompt = cat("yp")
    y_sample = cat("ys").reshape(B, 32, D)
    kap = cat("kap").reshape(1, B, WA, 8, 64)
    vap = cat("vap").reshape(1, B, WA, 8, 64)
    kbp = cat("kbp").reshape(1, B, T, 8, 64)
    vbp = cat("vbp").reshape(1, B, T, 8, 64)
    lfp = cat("lfp").reshape(1, B, T, 8)
    kas = cat("kas").reshape(1, B, 32, 8, 64)
    vas = cat("vas").reshape(1, B, 32, 8, 64)
    kbs = cat("kbs").reshape(1, B, 32, 8, 64)
    vbs = cat("vbs").reshape(1, B, 32, 8, 64)
    lfs = cat("lfs").reshape(1, B, 32, 8)
    return (y_prompt, y_sample, kap, vap, kbp, vbp, lfp, kas, vas, kbs, vbs, lfs)
```

```python
import contextlib
import numpy as np
import concourse.bass as bass
import concourse.mybir as mybir
from concourse.bass_utils import run_bass_kernel_spmd

F32 = mybir.dt.float32
BF16 = mybir.dt.bfloat16
AF = mybir.ActivationFunctionType
ALU = mybir.AluOpType
AX = mybir.AxisListType

NCORES = 8
D = 1024
T = 2048
NT = T // 128
SEQ_PER_CORE = 4
DIN = 3080
DFF = 2816
NFF = DFF // 128
PLE = 256
PAST = 4096
NCT = PAST // 128
WA = 512
RA = 6
EPS = 1e-6
NEG = -30000.0
EIDX = {0: 0, 1: 1, 2: 1, 3: 2, 4: 3}

ENGS = ('pe', 'act', 'dve', 'pool', 'sp')


class Op:
    __slots__ = ('eng', 'fn', 'deps', 'signal', 'semval', 'sem', 'is_dma')

    def __init__(self, eng, fn, is_dma=False):
        self.eng = eng
        self.fn = fn
        self.deps = []
        self.signal = False
        self.semval = None
        self.sem = None
        self.is_dma = is_dma


class Sched:
    def __init__(self, nc):
        self.nc = nc
        self.ops = {e: [] for e in ENGS}
        self.W = {}
        self.R = {}
        self.dma_cnt = {}
        self.dma_last = {}
        self.out_last = {}

    def _deps(self, op, reads, writes, ident):
        deps = []
        for k in reads:
            for e, w in self.W.get(k, {}).items():
                deps.append(w)
        for k in writes:
            for e, w in self.W.get(k, {}).items():
                if e != ident or op.is_dma or ident != 'pe':
                    deps.append(w)
            for e, r in self.R.get(k, {}).items():
                if e != ident or op.is_dma or ident != 'pe':
                    deps.append(r)
        seen = set()
        out = []
        for d in deps:
            if id(d) not in seen and d is not op:
                seen.add(id(d))
                out.append(d)
        op.deps = out
        for k in reads:
            self.R.setdefault(k, {})[ident] = op
        for k in writes:
            self.W[k] = {ident: op}
            self.R[k] = {}

    def op(self, eng, fn, reads=(), writes=()):
        o = Op(eng, fn)
        self._deps(o, reads, writes, eng)
        self.ops[eng].append(o)
        return o

    def dma(self, queue, fn, reads=(), writes=(), sem=None, is_output=False):
        o = Op(queue, fn, is_dma=True)
        o.sem = sem
        self.dma_cnt[sem] = self.dma_cnt.get(sem, 0) + 16
        o.semval = self.dma_cnt[sem]
        self._deps(o, reads, writes, 'dma:' + sem)
        self.ops[queue].append(o)
        self.dma_last[sem] = o
        if is_output:
            self.out_last[sem] = o
        return o

    def barrier(self):
        last = []
        for e in ENGS:
            for o in reversed(self.ops[e]):
                if o.fn is not None and not o.is_dma:
                    last.append(o)
                    break
        last += list(self.dma_last.values())
        for e in ENGS:
            b = Op(e, None)
            b.deps = list(last)
            self.ops[e].append(b)
        self.W = {}
        self.R = {}

    def emit(self):
        nc = self.nc
        fin = Op('sp', None)
        fin.deps = list(self.out_last.values())
        self.ops['sp'].append(fin)
        for e in ENGS:
            for o in self.ops[e]:
                for d in o.deps:
                    if not d.is_dma:
                        d.signal = True
        for e in ENGS:
            c = 0
            for o in self.ops[e]:
                if not o.is_dma and o.fn is not None and o.signal:
                    c += 1
                    o.semval = c
                    o.sem = 'eng:' + e
        stats = {}
        with contextlib.ExitStack() as es:
            sems = {}
            for e in ENGS:
                sems['eng:' + e] = es.enter_context(nc.semaphore('s_' + e))
            for s in self.dma_cnt:
                sems[s] = es.enter_context(nc.semaphore('d_' + s))
            block = es.enter_context(nc.Block())

            def run(e, engobj):
                waited = {}
                nw = 0
                for o in self.ops[e]:
                    for d in o.deps:
                        if waited.get(d.sem, 0) >= d.semval:
                            continue
                        engobj.wait_ge(sems[d.sem], d.semval)
                        waited[d.sem] = d.semval
                        nw += 1
                    if o.fn is None:
                        continue
                    ins = o.fn(engobj)
                    if o.is_dma:
                        ins.then_inc(sems[o.sem], 16)
                    elif o.signal:
                        ins.then_inc(sems[o.sem], 1)
                stats[e] = (len(self.ops[e]), nw)

            @block.tensor
            def _(eng):
                run('pe', eng)

            @block.scalar
            def _(eng):
                run('act', eng)

            @block.vector
            def _(eng):
                run('dve', eng)

            @block.gpsimd
            def _(eng):
                run('pool', eng)

            @block.sync
            def _(eng):
                run('sp', eng)
        return stats


class Arena:
    def __init__(self, tensor, nbytes):
        self.t = tensor
        self.nbytes = nbytes
        self.off = 0

    def alloc(self, shape, dt):
        nfree = int(np.prod(shape[1:]))
        sz = 4 if dt == F32 else 2
        nb = (nfree * sz + 63) // 64 * 64
        a = self.off
        self.off += nb
        assert self.off <= self.nbytes, f"arena overflow {self.off} > {self.nbytes}"
        v = self.t[:, a // 4:(a + nb) // 4]
        if dt != F32:
            v = v.bitcast(dt)
        v = v[:, 0:nfree]
        if len(shape) > 2:
            names = " ".join(f"d{i}" for i in range(1, len(shape)))
            kw = {f"d{i}": int(shape[i]) for i in range(1, len(shape))}
            v = v.rearrange(f"p ({names}) -> p {names}", **kw)
        return v[0:shape[0]]


C_TRI, C_ONES, C_TRIBLK, C_M0, C_M4, C_MBD01, C_M0N, C_M4N, C_MBDN = range(9)
NCONST_SB = 9 * 128 + 4
OFF_IDENT = 9 * 128 + 4
NCONST = 12 * 128 + 4


def make_consts():
    k = np.arange(128)[:, None]
    q = np.arange(128)[None, :]
    tri = (k <= q).astype(np.float32)
    ones = np.ones((128, 128), np.float32)
    triblk = ((k // 32 == q // 32) & (k <= q)).astype(np.float32)
    m0 = np.ones((128, 128), np.float32)
    m0[0:64, 64:128] = 0.0
    m4 = np.ones((128, 128), np.float32)
    m4[64:128, 0:64] = 0.0
    mbd01 = (k // 32 == q // 32).astype(np.float32)
    ident = np.eye(128, dtype=np.float32)
    maskc = np.where(k > q, NEG, 0.0).astype(np.float32)
    maskbd = np.where((k // 32 != q // 32) | (k > q), NEG, 0.0).astype(np.float32)
    ind4 = (np.arange(128)[:, None] // 32 == np.arange(4)[None, :]).astype(np.float32)
    return np.ascontiguousarray(
        np.concatenate([tri, ones, triblk, m0, m4, mbd01, (m0 - 1) * (-NEG) , (m4 - 1) * (-NEG), (mbd01 - 1) * (-NEG),
                        ind4, ident, maskc, maskbd], axis=1).astype(np.float32))


def build_program(n_prompt_seq=SEQ_PER_CORE, do_sample=True, do_phase2=True):
    nc = bass.Bass("TRN2", target_bir_lowering=False)

    def din(name, shape):
        return nc.dram_tensor(name, list(shape), F32, kind="ExternalInput").ap()

    def dout(name, shape):
        return nc.dram_tensor(name, list(shape), F32, kind="ExternalOutput").ap()

    xp = din("xp", [SEQ_PER_CORE, T, D])
    pp = din("pp", [SEQ_PER_CORE, T, PLE])
    xs = din("xs", [128, D])
    psm = din("psm", [128, PLE])
    cka = din("cka", [SEQ_PER_CORE, WA, 512])
    cva = din("cva", [SEQ_PER_CORE, WA, 512])
    ckb = din("ckb", [SEQ_PER_CORE, PAST, 512])
    cvb = din("cvb", [SEQ_PER_CORE, PAST, 512])
    clf = din("clf", [SEQ_PER_CORE, PAST, 8])
    w_in = din("w_in", [D, DIN])
    w_out = din("w_out", [D, D])
    w_gate = din("w_gate", [D, DFF])
    w_up = din("w_up", [D, DFF])
    w_down = din("w_down", [DFF, D])
    w_pg = din("w_pg", [D, D])
    w_pp = din("w_pp", [PLE, D])
    gmix_d = din("gmix", [128, 8])
    gffn_d = din("gffn", [128, 8])
    gple_d = din("gple", [128, 8])
    bf_d = din("bf", [1, 8])
    qna_d = din("qna", [1, 64])
    kna_d = din("kna", [1, 64])
    qnb_d = din("qnb", [1, 64])
    knb_d = din("knb", [1, 64])
    biasT_d = din("biasT", [128, 8 * 4 * 128])
    biasN_d = din("biasN", [128, 8 * 128])
    cst_d = din("cst", [128, NCONST])

    yp = dout("yp", [SEQ_PER_CORE, T, D])
    ys = dout("ys", [128, D])
    kap = dout("kap", [SEQ_PER_CORE, WA, 512])
    vap = dout("vap", [SEQ_PER_CORE, WA, 512])
    kbp = dout("kbp", [SEQ_PER_CORE, T, 512])
    vbp = dout("vbp", [SEQ_PER_CORE, T, 512])
    lfp = dout("lfp", [SEQ_PER_CORE, T, 8])
    kas = dout("kas", [128, 512])
    vas = dout("vas", [128, 512])
    kbs = dout("kbs", [128, 512])
    vbs = dout("vbs", [128, 512])
    lfs = dout("lfs", [128, 8])

    NTILES = SEQ_PER_CORE * NT + 1
    def dscr(name, shape):
        return nc.dram_tensor(name, list(shape), BF16, kind="Internal").ap()
    wg_s = dscr("wg_s", [D, DFF])
    wu_s = dscr("wu_s", [D, DFF])
    wd_s = dscr("wd_s", [DFF, D])
    wpg_s = dscr("wpg_s", [D, D])
    wpp_s = dscr("wpp_s", [PLE, D])
    x1s = nc.dram_tensor("x1s", [NTILES * 128, D], F32, kind="Internal").ap()

    ARENA_BYTES = 212800
    with contextlib.ExitStack() as es:
        arena_t = es.enter_context(nc.sbuf_tensor("arena", [128, ARENA_BYTES // 4], F32))
        AR = Arena(arena_t, ARENA_BYTES)
        pbank = [es.enter_context(nc.psum_tensor(f"pb{i}", [128, 512], F32)) for i in range(8)]
        pbank_bf = [pb[:, :].bitcast(BF16) for pb in pbank]
        S = Sched(nc)

        bank_rot = {'mm': [0, 1, 2, 3], 'st': [4, 5, 6], 'oa': [7]}
        bank_ctr = {'mm': 0, 'st': 0, 'oa': 0}

        held = set()
        skipped = [0]

        def nextbank(cls, hold=False):
            lst = bank_rot[cls]
            for _ in range(len(lst)):
                b = lst[bank_ctr[cls] % len(lst)]
                bank_ctr[cls] += 1
                if b not in held:
                    if hold:
                        held.add(b)
                    return b
                skipped[0] += 1
            raise RuntimeError("all PSUM banks of class %s are held" % cls)

        def release(b):
            held.discard(b)

        def PK(b):
            return ('ps', b)

        def MM(out, lhsT, rhs, start, stop, reads, writes):
            S.op('pe', lambda e: e.matmul(out, lhsT=lhsT, rhs=rhs, start=start, stop=stop),
                 reads=reads, writes=writes)

        def TR(out, in_, reads, writes):
            S.op('pe', lambda e: e.transpose(out=out, in_=in_, identity=ident_bf[:]),
                 reads=list(reads) + ['cbf'], writes=writes)

        def ACT(out, in_, func, reads, writes, **kw):
            S.op('act', lambda e: e.activation(out=out, in_=in_, func=func, **kw),
                 reads=reads, writes=writes)

        def TT(eng, out, in0, in1, op, reads, writes):
            S.op(eng, lambda e: e.tensor_tensor(out=out, in0=in0, in1=in1, op=op),
                 reads=reads, writes=writes)

        def TS(eng, out, in0, s1, s2, op0, op1, reads, writes):
            if s2 is None:
                S.op(eng, lambda e: e.tensor_scalar(out=out, in0=in0, scalar1=s1, scalar2=None, op0=op0),
                     reads=reads, writes=writes)
            else:
                S.op(eng, lambda e: e.tensor_scalar(out=out, in0=in0, scalar1=s1, scalar2=s2, op0=op0, op1=op1),
                     reads=reads, writes=writes)

        def STT(out, in0, scalar, in1, op0, op1, reads, writes):
            S.op('dve', lambda e: e.scalar_tensor_tensor(out=out, in0=in0, scalar=scalar, in1=in1, op0=op0, op1=op1),
                 reads=reads, writes=writes)

        def CP(eng, out, in_, reads, writes):
            S.op(eng, lambda e: e.tensor_copy(out=out, in_=in_), reads=reads, writes=writes)

        def MEMSET(eng, ap, val, writes):
            S.op(eng, lambda e: e.memset(ap, val), writes=writes)

        def DMA(queue, out, in_, reads, writes, sem, is_output=False):
            S.dma(queue, lambda e: e.dma_start(out=out, in_=in_), reads=reads, writes=writes, sem=sem,
                  is_output=is_output)

        cst = AR.alloc([128, NCONST_SB], F32)
        tri_f = cst[:, C_TRI * 128:(C_TRI + 1) * 128]
        ones_f = cst[:, C_ONES * 128:(C_ONES + 1) * 128]
        triblk_f = cst[:, C_TRIBLK * 128:(C_TRIBLK + 1) * 128]
        m0_f = cst[:, C_M0 * 128:(C_M0 + 1) * 128]
        m4_f = cst[:, C_M4 * 128:(C_M4 + 1) * 128]
        mbd01_f = cst[:, C_MBD01 * 128:(C_MBD01 + 1) * 128]
        m0n_f = cst[:, C_M0N * 128:(C_M0N + 1) * 128]
        m4n_f = cst[:, C_M4N * 128:(C_M4N + 1) * 128]
        mbdn_f = cst[:, C_MBDN * 128:(C_MBDN + 1) * 128]
        ind4_f = cst[:, 9 * 128:9 * 128 + 4]
        cbf = AR.alloc([128, 3 * 128], BF16)
        ident_bf = cbf[:, 0:128]
        maskc_bf = cbf[:, 128:256]
        maskbd_bf = cbf[:, 256:384]
        gmix = AR.alloc([128, 8], F32)
        gffn = AR.alloc([128, 8], F32)
        gple = AR.alloc([128, 8], F32)
        bft = AR.alloc([128, 8], F32)
        qna = AR.alloc([128, 64], F32)
        kna = AR.alloc([128, 64], F32)
        qnb = AR.alloc([128, 64], F32)
        knb = AR.alloc([128, 64], F32)
        caug = AR.alloc([128, SEQ_PER_CORE, NCT * 8 * 3], BF16)
        tot8 = AR.alloc([128, SEQ_PER_CORE, 8], F32)
        ss = AR.alloc([128, 1], F32)
        rs = AR.alloc([128, 1], F32)
        cH = AR.alloc([128, 8], F32)
        PERSIST_END = AR.off

        DMA('sp', cst[:], cst_d[:, 0:NCONST_SB], [], ['cst'], 'c0_1')
        DMA('pool', cbf[:], cst_d[:, OFF_IDENT:OFF_IDENT + 3 * 128], [], ['cbf'], 'c1')
        DMA('sp', gmix[:], gmix_d[:, :], [], ['gmix'], 'c0_2')
        DMA('sp', gffn[:], gffn_d[:, :], [], ['gffn'], 'c0_3')
        DMA('sp', gple[:], gple_d[:, :], [], ['gple'], 'c0_4')
        DMA('sp', bft[:], bf_d.partition_broadcast(128), [], ['bft'], 'c0_5')
        DMA('sp', qna[:], qna_d.partition_broadcast(128), [], ['qna'], 'c0_6')
        DMA('sp', kna[:], kna_d.partition_broadcast(128), [], ['kna'], 'c0_7')
        DMA('sp', qnb[:], qnb_d.partition_broadcast(128), [], ['qnb'], 'c0_8')
        DMA('sp', knb[:], knb_d.partition_broadcast(128), [], ['knb'], 'c0_9')

        win = AR.alloc([128, 8, DIN], BF16)
        wout = AR.alloc([128, 8, D], BF16)
        Bhi = AR.alloc([128, 8, 4, 128], BF16)
        Blo = AR.alloc([128, 8, 4, 128], BF16)
        Xb = [AR.alloc([128, D], F32) for _ in range(3)]
        xn = AR.alloc([128, D], BF16)
        hTb = [AR.alloc([128, 8, 128], BF16) for _ in range(2)]
        kvo = [AR.alloc([128, 512], F32) for _ in range(4)]
        sqb = [AR.alloc([128, 512], F32)]
        ssqb = [AR.alloc([128, 8], F32) for _ in range(2)]
        qa_bf = AR.alloc([128, 8, 64], BF16)
        ka_bf = AR.alloc([128, 8, 64], BF16)
        qb_aug = AR.alloc([128, 8, 70], BF16)
        kb_aug = AR.alloc([128, 8, 70], BF16)
        qaT = [AR.alloc([64, 8, 128], BF16) for _ in range(2)]
        qbT = [AR.alloc([70, 8, 128], BF16) for _ in range(2)]
        KaT = AR.alloc([64, RA, 8, 128], BF16)
        Va = AR.alloc([128, RA, 8, 65], BF16)
        KbT = AR.alloc([70, NT, 8, 128], BF16)
        KBT_TILE_BYTES = 8 * 128 * 2
        kbt_base = AR.off - NT * KBT_TILE_BYTES
        Vb = AR.alloc([128, NT, 8, 65], BF16)
        lf_all = AR.alloc([128, NT, 8], F32)
        xf = AR.alloc([128, 8], F32)
        ef = AR.alloc([128, 8], F32)
        carry8 = AR.alloc([128, 8], F32)
        C8 = AR.alloc([128, 8], F32)
        r1 = AR.alloc([128, 8], F32)
        r2 = AR.alloc([128, 8], F32)
        sel8 = AR.alloc([128, 8], F32)
        PT = [AR.alloc([128, 512], BF16) for _ in range(3)]
        rinv = AR.alloc([128, 8], F32)
        O_bf = AR.alloc([128, D], BF16)
        OT = AR.alloc([128, 8, 128], BF16)
        P1_END = AR.off
        print("phase1 arena bytes", P1_END)

        SA = Arena(arena_t, kbt_base + NT * KBT_TILE_BYTES)
        SA.off = kbt_base + KBT_TILE_BYTES
        stgK = [SA.alloc([128, 512], F32) for _ in range(2)]
        stgV = [SA.alloc([128, 512], F32) for _ in range(2)]
        kc_aug = [SA.alloc([128, 8, 70], BF16) for _ in range(2)]
        vc_aug = [SA.alloc([128, 8, 65], BF16) for _ in range(2)]
        kcT = [SA.alloc([70, 8, 128], BF16) for _ in range(2)]
        PTz = [SA.alloc([128, 8, 128], BF16) for _ in range(2)]
        accA = SA.alloc([128, 8, 65], F32)
        accB = SA.alloc([128, 8, 65], F32)
        BNhi = SA.alloc([128, 8, 128], BF16)
        BNlo = SA.alloc([128, 8, 128], BF16)
        vb_base = kbt_base + NT * KBT_TILE_BYTES
        SB2 = Arena(arena_t, vb_base + NT * 8 * 65 * 2)
        SB2.off = vb_base + 1088
        BNf = SB2.alloc([128, 8, 128], F32)
        stgK3 = SB2.alloc([128, 512], F32)
        stgV3 = SB2.alloc([128, 512], F32)
        PA = Arena(arena_t, vb_base + 16 * 1024)
        PA.off = vb_base
        clf_sb = PA.alloc([128, NCT, 8], F32)
        Tsb = PA.alloc([128, NCT, 8], F32)
        Pincl = PA.alloc([128, NCT, 8], F32)
        ex8 = PA.alloc([128, NCT, 8], F32)
        Cc8 = PA.alloc([128, NCT, 8], F32)
        cr1 = PA.alloc([128, NCT, 8], F32)
        cr2 = PA.alloc([128, NCT, 8], F32)
        cbf1 = PA.alloc([128, NCT, 8], BF16)
        cbf2 = PA.alloc([128, NCT, 8], BF16)
        cbf3 = PA.alloc([128, NCT, 8], BF16)

        w_in_v = w_in.rearrange("(c p) n -> p c n", p=128)
        WIN_TILES = [(3072, DIN), (0, 512), (512, 1024), (1024, 1536), (1536, 2048), (2048, 2560), (2560, 3072)]
        for (c0, c1) in WIN_TILES:
            DMA('pool', win[:, :, c0:c1], w_in_v[:, :, c0:c1], [], [('win', c0)], f'w1_{c0}')
        DMA('pool', wout[:], w_out.rearrange("(c p) n -> p c n", p=128), [], ['wout'], 'w1b')

        PB = Arena(arena_t, kbt_base + NT * KBT_TILE_BYTES)
        PB.off = kbt_base
        Bf = PB.alloc([128, 8, 4, 128], F32)
        Bf2 = PB.alloc([128, 8 * 4 * 128], F32)
        Bff = Bf.rearrange("p h k q -> p (h k q)")
        DMA('sp', Bff, biasT_d[:, :], [], ['Bf'], 'c2')
        TS('dve', Bff, Bff, 8.0, None, ALU.mult, None, ['Bf'], ['Bf'])
        CP('dve', cH[:], Bf[:, :, 1, 0], ['Bf'], ['cH'])
        for idx in (0, 2, 3):
            TT('dve', Bf[:, :, idx, :], Bf[:, :, idx, :], cH.unsqueeze(2).to_broadcast([128, 8, 128]), ALU.subtract,
               ['Bf', 'cH'], ['Bf'])
        for (idx, mf, mn) in ((0, m0_f, m0n_f), (3, m4_f, m4n_f)):
            TT('dve', Bf[:, :, idx, :], Bf[:, :, idx, :], mf.unsqueeze(1).to_broadcast([128, 8, 128]), ALU.mult,
               ['Bf', 'cst'], ['Bf'])
            TT('dve', Bf[:, :, idx, :], Bf[:, :, idx, :], mn.unsqueeze(1).to_broadcast([128, 8, 128]), ALU.add,
               ['Bf', 'cst'], ['Bf'])
        Bhf = Bhi.rearrange("p h k q -> p (h k q)")
        Blf = Blo.rearrange("p h k q -> p (h k q)")
        CP('dve', Bhf, Bff, ['Bf'], ['Bhi'])
        TT('dve', Bf2[:], Bff, Bhf, ALU.subtract, ['Bf', 'Bhi'], ['Bf2'])
        CP('dve', Blf, Bf2[:], ['Bf2'], ['Blo'])

        if do_sample:
            for s in range(SEQ_PER_CORE):
                clf_v = clf[s].rearrange("(t p) h -> p t h", p=128)
                for t0 in range(0, NCT, 4):
                    DMA('sp', clf_sb[:, t0:t0 + 4, :], clf_v[:, t0:t0 + 4, :], [], ['clf_sb'], 'c3')
                clf2 = clf_sb.rearrange("p t h -> p (t h)")
                bT = nextbank('mm')
                bR = nextbank('mm')
                MM(pbank[bT][:, 0:256], ones_f, clf2, True, True, ['cst', 'clf_sb'], [PK(bT)])
                MM(pbank[bR][:, 0:256], tri_f, clf2, True, True, ['cst', 'clf_sb'], [PK(bR)])
                Ts2 = Tsb.rearrange("p t h -> p (t h)")
                ACT(Ts2, pbank[bT][:, 0:256], AF.Copy, [PK(bT)], ['Tsb'])
                for h in range(8):
                    S.op('dve', lambda e, h=h: e.tensor_tensor_scan(
                        out=Pincl[:, :, h], data0=ones_f[:, 0:NCT], data1=Tsb[:, :, h], initial=0.0,
                        op0=ALU.mult, op1=ALU.add), reads=['Tsb', 'cst'], writes=['Pincl'])
                P2 = Pincl.rearrange("p t h -> p (t h)")
                e2 = ex8.rearrange("p t h -> p (t h)")
                c2 = Cc8.rearrange("p t h -> p (t h)")
                STT(e2, Ts2, -1.0, P2, ALU.mult, ALU.add, ['Tsb', 'Pincl'], ['ex8'])
                TS('dve', e2, e2, 8.0, None, ALU.mult, None, ['ex8'], ['ex8'])
                STT(c2, pbank[bR][:, 0:256], 8.0, e2, ALU.mult, ALU.add, [PK(bR), 'ex8'], ['Cc8'])
                TS('dve', tot8[:, s, :], Pincl[:, NCT - 1, :], 8.0, None, ALU.mult, None, ['Pincl'], ['tot8'])
                b1 = cbf1.rearrange("p t h -> p (t h)")
                b2 = cbf2.rearrange("p t h -> p (t h)")
                b3 = cbf3.rearrange("p t h -> p (t h)")
                q1 = cr1.rearrange("p t h -> p (t h)")
                q2 = cr2.rearrange("p t h -> p (t h)")
                CP('dve', b1, c2, ['Cc8'], ['cbf1'])
                TT('dve', q1, c2, b1, ALU.subtract, ['Cc8', 'cbf1'], ['cr1'])
                CP('dve', b2, q1, ['cr1'], ['cbf2'])
                TT('dve', q2, q1, b2, ALU.subtract, ['cr1', 'cbf2'], ['cr2'])
                CP('dve', b3, q2, ['cr2'], ['cbf3'])
                cav = caug[:, s, :].rearrange("p (t h c) -> p t h c", t=NCT, h=8)
                for ci, bsrc in enumerate((cbf1, cbf2, cbf3)):
                    TS('dve', cav[:, :, :, ci], bsrc[:], -1.0, None, ALU.mult, None,
                       [f'cbf{ci + 1}'], ['caug'])
        S.barrier()

        MEMSET('pool', qb_aug[:, :, 67:70], 1.0, ['qb_aug'])
        MEMSET('pool', kb_aug[:, :, 64:67], 1.0, ['kb_aug'])
        MEMSET('pool', Va[:, :, :, 64:65], 1.0, ['Va_ones'])
        MEMSET('pool', Vb[:, :, :, 64:65], 1.0, ['Vb_ones'])

        xcnt = [0]
        tile_ctr = [0]
        conv_jobs = []
        for kc in range(8):
            for (c0, c1) in ((0, 1408), (1408, DFF)):
                conv_jobs.append((wg_s[kc * 128:(kc + 1) * 128, c0:c1], w_gate[kc * 128:(kc + 1) * 128, c0:c1]))
        for kc in range(8):
            for (c0, c1) in ((0, 1408), (1408, DFF)):
                conv_jobs.append((wu_s[kc * 128:(kc + 1) * 128, c0:c1], w_up[kc * 128:(kc + 1) * 128, c0:c1]))
        for kc in range(NFF):
            conv_jobs.append((wd_s[kc * 128:(kc + 1) * 128, :], w_down[kc * 128:(kc + 1) * 128, :]))
        for kc in range(8):
            conv_jobs.append((wpg_s[kc * 128:(kc + 1) * 128, :], w_pg[kc * 128:(kc + 1) * 128, :]))
        for kc in range(2):
            conv_jobs.append((wpp_s[kc * 128:(kc + 1) * 128, :], w_pp[kc * 128:(kc + 1) * 128, :]))
        conv_pos = [0]

        def emit_conv(n=1):
            for _ in range(n):
                if conv_pos[0] < len(conv_jobs):
                    o, i = conv_jobs[conv_pos[0]]
                    conv_pos[0] += 1
                    DMA('pool', o, i, [], ['wconv'], 'cvw')

        def rmsnorm_to_T(X, kx, gain, gkey, dstT, dkey='hT'):
            ACT(xn[:], X[:], AF.Square, [kx], ['xn', 'ss'], accum_out=ss[:])
            ACT(rs[:], ss[:], AF.Ln, ['ss'], ['rs'], scale=1.0 / D, bias=EPS)
            ACT(rs[:], rs[:], AF.Exp, ['rs'], ['rs'], scale=-0.5)
            TS('dve', xn[:], X[:], rs[:], None, ALU.mult, None, [kx, 'rs'], ['xn'])
            b = nextbank('mm')
            for c in range(8):
                TR(pbank_bf[b][:, c * 128:(c + 1) * 128], xn[:, c * 128:(c + 1) * 128], ['xn'], [PK(b)])
            TT('dve', dstT[:], pbank_bf[b][:, :].rearrange("p (c t) -> p c t", c=8),
               gain.unsqueeze(2).to_broadcast([128, 8, 128]), ALU.mult, [PK(b), gkey], [dkey])

        def head_norm_a(b, sq_buf, sqk, ssq_buf, ssk):
            ACT(sq_buf[:], pbank[b][:, :], AF.Square, [PK(b)], [sqk])
            S.op('dve', lambda e: e.tensor_reduce(out=ssq_buf[:], in_=sq_buf.rearrange("p (h d) -> p h d", h=8),
                                                  axis=AX.X, op=ALU.add), reads=[sqk], writes=[ssk])

        def head_norm_b(b, sq_buf, sqk, ssq_buf, ssk, gain, gkey, out_ap, out_key):
            pv = pbank[b][:, :].rearrange("p (h d) -> p h d", h=8)
            ACT(ssq_buf[:], ssq_buf[:], AF.Ln, [ssk], [ssk], scale=1.0 / 64, bias=EPS)
            ACT(ssq_buf[:], ssq_buf[:], AF.Exp, [ssk], [ssk], scale=-0.5)
            tv = sq_buf.rearrange("p (h d) -> p h d", h=8)
            TT('dve', tv, pv, ssq_buf.unsqueeze(2).to_broadcast([128, 8, 64]), ALU.mult, [PK(b), ssk], [sqk])
            TT('dve', out_ap, tv, gain.unsqueeze(1).to_broadcast([128, 8, 64]), ALU.mult, [sqk, gkey], [out_key])

        def stageA1(seq, j, sample=False, defer_pe=False):
            slot = xcnt[0] % 3
            xcnt[0] += 1
            par = tile_ctr[0] % 2
            tile_ctr[0] += 1
            X = Xb[slot]
            kx = ('X', slot)
            src = xs[:, :] if sample else xp[seq, j * 128:(j + 1) * 128, :]
            DMA('sp', X[:], src, [], [kx], f'x{slot}')
            ACT(xn[:], X[:], AF.Square, [kx], ['xn', 'ss'], accum_out=ss[:])
            ACT(rs[:], ss[:], AF.Ln, ['ss'], ['rs'], scale=1.0 / D, bias=EPS)
            ACT(rs[:], rs[:], AF.Exp, ['rs'], ['rs'], scale=-0.5)
            TS('dve', xn[:], X[:], rs[:], None, ALU.mult, None, [kx, 'rs'], ['xn'])

            def pe_part():
                b = nextbank('mm')
                for c in range(8):
                    TR(pbank_bf[b][:, c * 128:(c + 1) * 128], xn[:, c * 128:(c + 1) * 128], ['xn'], [PK(b)])
                TT('dve', hTb[par][:], pbank_bf[b][:, :].rearrange("p (c t) -> p c t", c=8),
                   gmix.unsqueeze(2).to_broadcast([128, 8, 128]), ALU.mult, [PK(b), 'gmix'], [('hT', par)])
            ctx = dict(slot=slot, par=par, j=j, seq=seq, sample=sample)
            if defer_pe:
                ctx['pe_part'] = pe_part
            else:
                pe_part()
            return ctx

        def stageA2(ctx):
            seq = ctx['seq']; j = ctx['j']; sample = ctx['sample']; par = ctx['par']
            hT = hTb[par]
            hk = ('hT', par)
            ra = j % RA
            chunks = []

            def inproj(c0, n, hold=False):
                b = nextbank('mm', hold)
                for kc in range(8):
                    MM(pbank[b][:, 0:n], hT[:, kc, :], win[:, kc, c0:c0 + n], kc == 0, kc == 7,
                       [hk, ('win', c0)], [PK(b)])
                return b

            st_ = {}

            def qa1():
                st_['qa'] = inproj(0, 512, True)
                head_norm_a(st_['qa'], sqb[0], ('sq', 0), ssqb[0], ('ssq', 0))

            def qa2():
                head_norm_b(st_['qa'], sqb[0], ('sq', 0), ssqb[0], ('ssq', 0), qna, 'qna', qa_bf[:], 'qa_bf')
                release(st_['qa'])

            def qa3():
                b = nextbank('mm')
                for h in range(8):
                    TR(pbank_bf[b][0:64, h * 128:(h + 1) * 128], qa_bf[:, h, :], ['qa_bf'], [PK(b)])
                ACT(qaT[par][:], pbank_bf[b][0:64, :].rearrange("p (h t) -> p h t", h=8), AF.Copy, [PK(b)], [('qaT', par)])

            def ka1():
                st_['ka'] = inproj(512, 512, True)
                head_norm_a(st_['ka'], kvo[0], ('kvo', 0), ssqb[1], ('ssq', 1))

            def ka2():
                head_norm_b(st_['ka'], kvo[0], ('kvo', 0), ssqb[1], ('ssq', 1), kna, 'kna',
                            kvo[0].rearrange("p (h d) -> p h d", h=8), ('kvo', 0))
                release(st_['ka'])
                CP('pool', ka_bf[:], kvo[0].rearrange("p (h d) -> p h d", h=8), [('kvo', 0)], ['ka_bf'])
                if sample:
                    DMA('sp', kas[:, :], kvo[0][:], [('kvo', 0)], [], 'o0', True)
                elif j >= NT - 4:
                    r0 = (j - (NT - 4)) * 128
                    DMA('sp', kap[seq, r0:r0 + 128, :], kvo[0][:], [('kvo', 0)], [], 'o0', True)

            def ka3():
                b = nextbank('mm')
                for h in range(8):
                    TR(pbank_bf[b][0:64, h * 128:(h + 1) * 128], ka_bf[:, h, :], ['ka_bf'], [PK(b)])
                ACT(KaT[:, ra, :, :], pbank_bf[b][0:64, :].rearrange("p (h t) -> p h t", h=8), AF.Copy, [PK(b)], [('KaT', ra)])

            def va1():
                b = inproj(1024, 512)
                ACT(kvo[1][:], pbank[b][:, :], AF.Copy, [PK(b)], [('kvo', 1)])
                CP('pool', Va[:, ra, :, 0:64], kvo[1].rearrange("p (h d) -> p h d", h=8), [('kvo', 1)], [('Va', ra)])
                if sample:
                    DMA('sp', vas[:, :], kvo[1][:], [('kvo', 1)], [], 'o1', True)
                elif j >= NT - 4:
                    r0 = (j - (NT - 4)) * 128
                    DMA('sp', vap[seq, r0:r0 + 128, :], kvo[1][:], [('kvo', 1)], [], 'o1', True)

            def gf1():
                b = inproj(3072, 8)
                TT('dve', xf[:], pbank[b][:, 0:8], bft[:], ALU.add, [PK(b), 'bft'], ['xf'])

            def gf1b():
                ACT(ef[:], xf[:], AF.Exp, ['xf'], ['ef'], scale=-1.0)
                ACT(ef[:], ef[:], AF.Ln, ['ef'], ['ef'], bias=1.0)
                lfj = lf_all[:, j, :]
                TS('dve', lfj, ef[:], -1.0, None, ALU.mult, None, ['ef'], [('lf', j)])
                if sample:
                    DMA('sp', lfs[:, :], lf_all[:, 0, :], [('lf', 0)], [], 'o4', True)
                elif j == NT - 1:
                    lfp_v = lfp[seq].rearrange("(t p) h -> p t h", p=128)
                    for t0 in range(0, NT, 4):
                        DMA('sp', lfp_v[:, t0:t0 + 4, :], lf_all[:, t0:t0 + 4, :],
                            [('lf', t) for t in range(t0, t0 + 4)], [], 'o4', True)

            def gf2():
                lfj = lf_all[:, j, :]
                b = nextbank('mm', True)
                if sample:
                    MM(pbank[b][:, 0:8], triblk_f, lfj, True, True, ['cst', ('lf', j)], [PK(b)])
                    for s_ in range(SEQ_PER_CORE):
                        if s_ == 0:
                            TS('dve', sel8[:], tot8[:, 0, :], ind4_f[:, 0:1], None, ALU.mult, None,
                               ['tot8', 'cst'], ['sel8'])
                        else:
                            STT(sel8[:], tot8[:, s_, :], ind4_f[:, s_:s_ + 1], sel8[:], ALU.mult, ALU.add,
                                ['tot8', 'cst', 'sel8'], ['sel8'])
                    st_['cb'] = b
                else:
                    if j == 0:
                        MEMSET('dve', carry8[:], 0.0, ['carry8'])
                    MM(pbank[b][:, 0:8], tri_f, lfj, True, True, ['cst', ('lf', j)], [PK(b)])
                    MM(pbank[b][:, 8:16], ones_f, lfj, True, True, ['cst', ('lf', j)], [PK(b)])
                    st_['cb'] = b

            def gf3():
                b = st_['cb']
                release(b)
                if sample:
                    STT(C8[:], pbank[b][:, 0:8], 8.0, sel8[:], ALU.mult, ALU.add, [PK(b), 'sel8'], ['C8'])
                else:
                    STT(C8[:], pbank[b][:, 0:8], 8.0, carry8[:], ALU.mult, ALU.add, [PK(b), 'carry8'], ['C8'])
                    STT(carry8[:], pbank[b][:, 8:16], 8.0, carry8[:], ALU.mult, ALU.add, [PK(b), 'carry8'], ['carry8'])

            def qb1():
                st_['qb'] = inproj(1536, 512, True)
                head_norm_a(st_['qb'], sqb[0], ('sq', 0), ssqb[0], ('ssq', 0))

            def qb2():
                head_norm_b(st_['qb'], sqb[0], ('sq', 0), ssqb[0], ('ssq', 0), qnb, 'qnb', qb_aug[:, :, 0:64], 'qb_aug')
                release(st_['qb'])
                CP('dve', qb_aug[:, :, 64], C8[:], ['C8'], ['qb_aug'])
                TT('dve', r1[:], C8[:], qb_aug[:, :, 64], ALU.subtract, ['C8', 'qb_aug'], ['r1'])
                CP('dve', qb_aug[:, :, 65], r1[:], ['r1'], ['qb_aug'])
                TT('dve', r2[:], r1[:], qb_aug[:, :, 65], ALU.subtract, ['r1', 'qb_aug'], ['r2'])
                CP('dve', qb_aug[:, :, 66], r2[:], ['r2'], ['qb_aug'])

            def qb3():
                b = nextbank('mm')
                for h in range(8):
                    TR(pbank_bf[b][0:70, h * 128:(h + 1) * 128], qb_aug[:, h, :], ['qb_aug'], [PK(b)])
                CP('dve', qbT[par][:], pbank_bf[b][0:70, :].rearrange("p (h t) -> p h t", h=8), [PK(b)], [('qbT', par)])

            def kb1():
                st_['kb'] = inproj(2048, 512, True)
                head_norm_a(st_['kb'], kvo[2], ('kvo', 2), ssqb[1], ('ssq', 1))

            def kb2():
                head_norm_b(st_['kb'], kvo[2], ('kvo', 2), ssqb[1], ('ssq', 1), knb, 'knb',
                            kvo[2].rearrange("p (h d) -> p h d", h=8), ('kvo', 2))
                release(st_['kb'])
                CP('pool', kb_aug[:, :, 0:64], kvo[2].rearrange("p (h d) -> p h d", h=8), [('kvo', 2)], ['kb_aug'])
                TS('dve', kb_aug[:, :, 67:70], qb_aug[:, :, 64:67], -1.0, None, ALU.mult, None, ['qb_aug'], ['kb_aug'])
                if sample:
                    DMA('sp', kbs[:, :], kvo[2][:], [('kvo', 2)], [], 'o2', True)
                else:
                    DMA('sp', kbp[seq, j * 128:(j + 1) * 128, :], kvo[2][:], [('kvo', 2)], [], 'o2', True)

            def kb3():
                b = nextbank('mm')
                for h in range(8):
                    TR(pbank_bf[b][0:70, h * 128:(h + 1) * 128], kb_aug[:, h, :], ['kb_aug'], [PK(b)])
                CP('dve', KbT[:, j, :, :], pbank_bf[b][0:70, :].rearrange("p (h t) -> p h t", h=8), [PK(b)], [('KbT', j)])

            def vb1():
                b = inproj(2560, 512)
                ACT(kvo[3][:], pbank[b][:, :], AF.Copy, [PK(b)], [('kvo', 3)])
                CP('pool', Vb[:, j, :, 0:64], kvo[3].rearrange("p (h d) -> p h d", h=8), [('kvo', 3)], [('Vb', j)])
                if sample:
                    DMA('sp', vbs[:, :], kvo[3][:], [('kvo', 3)], [], 'o3', True)
                else:
                    DMA('sp', vbp[seq, j * 128:(j + 1) * 128, :], kvo[3][:], [('kvo', 3)], [], 'o3', True)

            def seqc(*fs):
                def f():
                    for g in fs:
                        g()
                return f
            return [seqc(gf1, qa1), seqc(gf1b, qa2, ka1), seqc(gf2, ka2, va1, qa3), seqc(gf3, qb1, ka3),
                    seqc(qb2, kb1), seqc(kb2, vb1, qb3), seqc(kb3, emit_conv)]

        pt_ctr = [0]
        pe_ctr = [0]

        def run_units(units, inject=None):
            def emit_st(u):
                b = nextbank('st')
                for i, (lhsT, rhs, extra, rd) in enumerate(u['st']):
                    o = pbank[b][:, i * 128:(i + 1) * 128]
                    MM(o, lhsT, rhs, True, len(extra) == 0, rd, [PK(b)])
                    for ei, (xr, xrd) in enumerate(extra):
                        MM(o, ident_bf[:], xr, False, ei == len(extra) - 1, ['cbf'] + xrd, [PK(b)])
                u['b_st'] = b

            def emit_rest(u):
                b = u['b_st']
                n = len(u['st'])
                ps = pt_ctr[0] % 3
                pt_ctr[0] += 1
                ACT(PT[ps][:, 0:n * 128], pbank[b][:, 0:n * 128], AF.Exp, [PK(b)], [('PT', ps)], scale=0.125)
                ob = u['ob']
                hh = u['hh']
                for i, (rhs, rd) in enumerate(u['pv']):
                    MM(pbank[ob][:, hh * 65:(hh + 1) * 65], PT[ps][:, i * 128:(i + 1) * 128], rhs,
                       u['first'] and i == 0, u['last'] and i == n - 1, [('PT', ps)] + rd, [PK(ob)])
                if u.get('after') is not None:
                    u['after']()

            LOOK = 2
            n_u = len(units)
            for ui in range(n_u + LOOK):
                if ui < n_u:
                    emit_st(units[ui])
                if ui - LOOK >= 0:
                    emit_rest(units[ui - LOOK])
                    if inject is not None:
                        inject(ui - LOOK)

        def finish_heads(ob, col0, nheads=4):
            ov = pbank[ob][:, 0:nheads * 65].rearrange("p (h c) -> p h c", h=nheads)
            S.op('dve', lambda e: e.reciprocal(out=rinv[:, 0:nheads], in_=ov[:, :, 64]), reads=[PK(ob)], writes=['rinv'])
            TT('dve', O_bf[:, col0:col0 + 64 * nheads].rearrange("p (h d) -> p h d", h=nheads), ov[:, :, 0:64],
               rinv[:, 0:nheads].unsqueeze(2).to_broadcast([128, nheads, 64]), ALU.mult, [PK(ob), 'rinv'], [('O_bf', col0 // 256)])

        def o_transpose(half):
            b = nextbank('mm')
            rd = [('O_bf', half * 2), ('O_bf', half * 2 + 1)]
            for c in range(4):
                cc = half * 4 + c
                TR(pbank_bf[b][:, c * 128:(c + 1) * 128], O_bf[:, cc * 128:(cc + 1) * 128], rd, [PK(b)])
            eng = 'act' if half == 0 else 'dve'
            if eng == 'act':
                ACT(OT[:, half * 4:(half + 1) * 4, :], pbank_bf[b][:, 0:512].rearrange("p (c t) -> p c t", c=4), AF.Copy,
                    [PK(b)], [('OT', half)])
            else:
                CP('dve', OT[:, half * 4:(half + 1) * 4, :], pbank_bf[b][:, 0:512].rearrange("p (c t) -> p c t", c=4),
                   [PK(b)], [('OT', half)])

        def out_proj_and_store(ctx, gtile):
            slot = ctx['slot']
            X = Xb[slot]
            kx = ('X', slot)
            for n in range(2):
                b = nextbank('mm')
                for kc in range(8):
                    MM(pbank[b][:, :], OT[:, kc, :], wout[:, kc, n * 512:(n + 1) * 512], kc == 0, kc == 7,
                       [('OT', kc // 4), 'wout'], [PK(b)])
                TT('dve', X[:, n * 512:(n + 1) * 512], X[:, n * 512:(n + 1) * 512], pbank[b][:, :], ALU.add,
                   [kx, PK(b)], [kx])
            DMA('sp', x1s[gtile * 128:(gtile + 1) * 128, :], X[:], [kx], [('x1s', gtile)], f'xs{slot}')

        def assign_oacc(units, after_fn):
            cur = None
            for u in units:
                if cur is None:
                    cur = nextbank('oa')
                u['ob'] = cur
                if 'hg_end' in u:
                    kind, hg = u['hg_end']
                    u['after'] = after_fn(cur, kind, hg)
                    cur = None

        def band_extra(h, o):
            if o == 0:
                return [(Bhi[:, h, 0, :], ['Bhi'])]
            if o in (1, 2):
                return []
            return [(Bhi[:, h, EIDX[o], :], ['Bhi']), (Blo[:, h, EIDX[o], :], ['Blo'])]

        def build_units(j, par):
            units = []
            tiles = list(range(max(0, j - 4), j + 1))
            for hg in range(2):
                for hh in range(4):
                    h = hg * 4 + hh
                    groups = [[t for t in tiles if t - (j - 4) <= 1], [t for t in tiles if t - (j - 4) >= 2]]
                    groups = [g for g in groups if g]
                    for gi, g in enumerate(groups):
                        u = dict(kind='A', h=h, hh=hh, first=(gi == 0), last=(gi == len(groups) - 1),
                                 st=[(KaT[:, t % RA, h, :], qaT[par][:, h, :], band_extra(h, t - (j - 4)),
                                      [('KaT', t % RA), ('qaT', par)]) for t in g],
                                 pv=[(Va[:, t % RA, h, :], [('Va', t % RA), 'Va_ones']) for t in g])
                        units.append(u)
                units[-1]['hg_end'] = ('A', hg)
            tilesb = list(range(0, j + 1))
            for hg in range(2):
                for hh in range(4):
                    h = hg * 4 + hh
                    groups = [tilesb[i:i + 4] for i in range(0, len(tilesb), 4)]
                    for gi, g in enumerate(groups):
                        u = dict(kind='B', h=h, hh=hh, first=(gi == 0), last=(gi == len(groups) - 1),
                                 st=[(KbT[:, t, h, :], qbT[par][:, h, :], ([(maskc_bf[:], [])] if t == j else []),
                                      [('KbT', t), ('qbT', par)]) for t in g],
                                 pv=[(Vb[:, t, h, :], [('Vb', t), 'Vb_ones']) for t in g])
                        units.append(u)
                units[-1]['hg_end'] = ('B', hg)
            return units

        def stageB(ctx, chunks=(), mid=None):
            j = ctx['j']
            par = ctx['par']
            units = build_units(j, par)

            pending = []
            cur_unit = [0]

            def after_fn(ob, kind, hg):
                col0 = (0 if kind == 'A' else 512) + hg * 256

                def f():
                    finish_heads(ob, col0)
                    if hg == 1:
                        pending.append((cur_unit[0], lambda: o_transpose(0 if kind == 'A' else 1)))
                return f
            assign_oacc(units, after_fn)
            chunks = list(chunks)
            nu = len(units)
            nch = len(chunks)
            sched_at = {}
            for k in range(nch):
                sched_at.setdefault(max(0, (k + 1) * nu // (nch + 1) - 1), []).append(chunks[k])
            mid_at = max(0, nu // 3 - 1)
            mid2_at = max(mid_at + 1, (2 * nu) // 3 - 1)
            state = {'mid': mid, 'mid2': None}

            def inject(i):
                while pending and pending[0][0] < i:
                    pending.pop(0)[1]()
                for c in sched_at.get(i, []):
                    c()
                if i == mid_at and state['mid'] is not None:
                    state['mid2'] = state['mid']()
                    state['mid'] = None
                if i >= mid2_at and state['mid2'] is not None:
                    state['mid2']()
                    state['mid2'] = None
                cur_unit[0] = i + 1
            run_units(units, inject)
            if state['mid'] is not None:
                state['mid2'] = state['mid']()
            if state['mid2'] is not None:
                state['mid2']()
            while pending:
                pending.pop(0)[1]()
            out_proj_and_store(ctx, ctx['gtile'])

        a1ctx = {}

        def do_A1(seq, j, defer_pe=False):
            c = stageA1(seq, j, defer_pe=defer_pe)
            c['gtile'] = seq * NT + j
            a1ctx[(seq, j)] = c
            return c.get('pe_part')

        for seq in range(n_prompt_seq):
            if (seq, 0) not in a1ctx:
                do_A1(seq, 0)
            for c in stageA2(a1ctx[(seq, 0)]):
                c()
            do_A1(seq, 1)
            for j in range(NT):
                chunks = stageA2(a1ctx[(seq, j + 1)]) if j + 1 < NT else []
                if j + 2 < NT:
                    mid = (lambda seq=seq, j=j: do_A1(seq, j + 2, True))
                elif j + 2 == NT + 1 and seq + 1 < n_prompt_seq:
                    mid = (lambda seq=seq: do_A1(seq + 1, 0, True))
                else:
                    mid = None
                stageB(a1ctx[(seq, j)], chunks, mid)

        if do_sample:
            S.barrier()
            MEMSET('pool', qb_aug[:, :, 67:70], 1.0, ['qb_aug'])
            MEMSET('pool', kb_aug[:, :, 64:67], 1.0, ['kb_aug'])
            for i in range(2):
                MEMSET('pool', kc_aug[i][:, :, 64:67], 1.0, [('kc_ones', i)])
                MEMSET('pool', vc_aug[i][:, :, 64:65], 1.0, [('vc_ones', i)])
                MEMSET('pool', PTz[i][:], 0.0, [('PTz', i)])
            MEMSET('dve', accA[:], 0.0, ['accA'])
            MEMSET('dve', accB[:], 0.0, ['accB'])
            BN2 = BNf.rearrange("p h q -> p (h q)")
            DMA('sp', BN2, biasN_d[:, :], [], ['BNf'], 'c2b')
            TS('dve', BN2, BN2, 8.0, None, ALU.mult, None, ['BNf'], ['BNf'])
            TT('dve', BNf[:], BNf[:], cH.unsqueeze(2).to_broadcast([128, 8, 128]), ALU.subtract, ['BNf', 'cH'], ['BNf'])
            TT('dve', BNf[:], BNf[:], mbd01_f.unsqueeze(1).to_broadcast([128, 8, 128]), ALU.mult, ['BNf', 'cst'], ['BNf'])
            TT('dve', BNf[:], BNf[:], mbdn_f.unsqueeze(1).to_broadcast([128, 8, 128]), ALU.add, ['BNf', 'cst'], ['BNf'])
            CP('dve', BNhi[:], BNf[:], ['BNf'], ['BNhi'])
            TT('dve', BNf[:], BNf[:], BNhi[:], ALU.subtract, ['BNf', 'BNhi'], ['BNf'])
            CP('dve', BNlo[:], BNf[:], ['BNf'], ['BNlo'])

            ctx = stageA1(0, 0, sample=True)
            ctx['gtile'] = SEQ_PER_CORE * NT
            for c in stageA2(ctx):
                c()
            par = ctx['par']
            stgK.append(stgK3)
            stgV.append(stgV3)
            ctiles_ = []
            for s_ in range(SEQ_PER_CORE):
                for i_ in range(WA // 128):
                    ctiles_.append((s_, i_, True))
                for i_ in range(NCT):
                    ctiles_.append((s_, i_, False))
            NCTL = len(ctiles_)
            cst8 = {}

            def c_load(t):
                s_, i_, band = ctiles_[t]
                k = t % 3
                ksrc = (cka if band else ckb)[s_, i_ * 128:(i_ + 1) * 128, :]
                vsrc = (cva if band else cvb)[s_, i_ * 128:(i_ + 1) * 128, :]
                DMA('sp', stgK[k][:], ksrc, [], [('stgK', k)], f'ck{k}')
                DMA('sp', stgV[k][:], vsrc, [], [('stgV', k)], f'cv{k}')

            def c_cast_k(t):
                s_, i_, band = ctiles_[t]
                k = t % 2
                CP('dve', kc_aug[k][:, :, 0:64], stgK[t % 3].rearrange("p (h d) -> p h d", h=8),
                   [('stgK', t % 3)], [('kc', k)])
                if not band:
                    cav = caug[:, s_, :].rearrange("p (t h c) -> p t h c", t=NCT, h=8)
                    CP('pool', kc_aug[k][:, :, 67:70], cav[:, i_, :, :], ['caug'], [('kc', k)])

            def c_tr_k(t):
                s_, i_, band = ctiles_[t]
                k = t % 2
                nr = 64 if band else 70
                rdk = [('kc', k)]
                if not band:
                    rdk.append(('kc_ones', k))
                b = nextbank('mm')
                for h in range(8):
                    TR(pbank_bf[b][0:nr, h * 128:(h + 1) * 128], kc_aug[k][:, h, 0:nr], rdk, [PK(b)])
                ACT(kcT[k][0:nr], pbank_bf[b][0:nr, :].rearrange("p (h t) -> p h t", h=8), AF.Copy,
                    [PK(b)], [('kcT', k)])

            def c_prep_v(t):
                k = t % 2
                ACT(vc_aug[k][:, :, 0:64], stgV[t % 3].rearrange("p (h d) -> p h d", h=8), AF.Copy,
                    [('stgV', t % 3)], [('vc', k)])

            def c_score(t):
                s_, i_, band = ctiles_[t]
                k = t % 2
                nr = 64 if band else 70
                bs = nextbank('st')
                qT = qaT[par] if band else qbT[par]
                qk = ('qaT', par) if band else ('qbT', par)
                kt = 2
                wb = band and i_ == 3
                for h in range(8):
                    o = pbank[bs][:, h * 32:(h + 1) * 32]
                    MM(o, kcT[k][0:nr, h, :], qT[0:nr, h, s_ * 32:(s_ + 1) * 32], True, not wb,
                       [('kcT', k), qk], [PK(bs)])
                    if wb:
                        MM(o, ident_bf[:], Bhi[:, h, kt, 0:32], False, False, ['cbf', 'Bhi'], [PK(bs)])
                        MM(o, ident_bf[:], Blo[:, h, kt, 0:32], False, True, ['cbf', 'Blo'], [PK(bs)])
                pz = PTz[k][:, :, s_ * 32:(s_ + 1) * 32]
                sv = pbank[bs][:, 0:256].rearrange("p (h q) -> p h q", h=8)
                ACT(pz, sv, AF.Exp, [PK(bs)], [('PTz', k)], scale=0.125)

            def c_pv(t):
                s_, i_, band = ctiles_[t]
                k = t % 2
                acc = accA if band else accB
                ak = 'accA' if band else 'accB'
                for half in range(2):
                    ob = nextbank('mm')
                    for hh in range(4):
                        h = half * 4 + hh
                        MM(pbank[ob][:, hh * 65:(hh + 1) * 65], PTz[k][:, h, :], vc_aug[k][:, h, :], True, True,
                           [('PTz', k), ('vc', k), ('vc_ones', k)], [PK(ob)])
                    TT('dve', acc[:, half * 4:(half + 1) * 4, :], acc[:, half * 4:(half + 1) * 4, :],
                       pbank[ob][:, 0:260].rearrange("p (h c) -> p h c", h=4), ALU.add, [ak, PK(ob)], [ak])
                if t + 1 == NCTL or ctiles_[t + 1][0] != s_:
                    for i2 in range(2):
                        MEMSET('pool', PTz[i2][:, :, s_ * 32:(s_ + 1) * 32], 0.0, [('PTz', i2)])

            c_load(0)
            c_load(1)
            c_cast_k(0)
            for t in range(NCTL + 2):
                if t < NCTL:
                    c_tr_k(t)
                if t + 1 < NCTL:
                    c_cast_k(t + 1)
                if 0 <= t - 2 < NCTL:
                    c_pv(t - 2)
                if 0 <= t - 1 < NCTL:
                    c_score(t - 1)
                if t + 2 < NCTL:
                    c_load(t + 2)
                if t < NCTL:
                    c_prep_v(t)

            units = []
            for hg in range(2):
                for hh in range(4):
                    h = hg * 4 + hh
                    units.append(dict(kind='A', h=h, hh=hh, first=True, last=True,
                                      st=[(KaT[:, 0, h, :], qaT[par][:, h, :],
                                           [(BNhi[:, h, :], ['BNhi']), (BNlo[:, h, :], ['BNlo'])],
                                           [('KaT', 0), ('qaT', par)])],
                                      pv=[(Va[:, 0, h, :], [('Va', 0), 'Va_ones'])]))
                units[-1]['hg_end'] = ('A', hg)
            for hg in range(2):
                for hh in range(4):
                    h = hg * 4 + hh
                    units.append(dict(kind='B', h=h, hh=hh, first=True, last=True,
                                      st=[(KbT[:, 0, h, :], qbT[par][:, h, :], [(maskbd_bf[:], [])],
                                           [('KbT', 0), ('qbT', par)])],
                                      pv=[(Vb[:, 0, h, :], [('Vb', 0), 'Vb_ones'])]))
                units[-1]['hg_end'] = ('B', hg)

            def after_fn(ob, kind, hg):
                acc = accA if kind == 'A' else accB
                ak = 'accA' if kind == 'A' else 'accB'

                def after():
                    TT('dve', acc[:, hg * 4:(hg + 1) * 4, :], acc[:, hg * 4:(hg + 1) * 4, :],
                       pbank[ob][:, 0:260].rearrange("p (h c) -> p h c", h=4), ALU.add, [ak, PK(ob)], [ak])
                return after
            assign_oacc(units, after_fn)
            S.op('pool', lambda e: e.memset(Va[:, 0, :, 64:65], 1.0), writes=['Va_ones'])
            S.op('pool', lambda e: e.memset(Vb[:, 0, :, 64:65], 1.0), writes=['Vb_ones'])
            run_units(units)
            for (acc, ak, col0) in ((accA, 'accA', 0), (accB, 'accB', 512)):
                S.op('dve', lambda e, acc=acc: e.reciprocal(out=rinv[:, 0:8], in_=acc[:, :, 64]), reads=[ak], writes=['rinv'])
                TT('dve', O_bf[:, col0:col0 + 512].rearrange("p (h d) -> p h d", h=8), acc[:, :, 0:64],
                   rinv[:, 0:8].unsqueeze(2).to_broadcast([128, 8, 64]), ALU.mult, [ak, 'rinv'],
                   [('O_bf', col0 // 256), ('O_bf', col0 // 256 + 1)])
                o_transpose(col0 // 512)
            out_proj_and_store(ctx, ctx['gtile'])

        if do_phase2:
            S.barrier()
            AR.off = PERSIST_END
            wg = AR.alloc([128, 8, DFF], BF16)
            wu = AR.alloc([128, 8, DFF], BF16)
            wd = AR.alloc([128, NFF, D], BF16)
            wpg = AR.alloc([128, 8, D], BF16)
            wpp = AR.alloc([128, 2, D], BF16)
            X2 = [AR.alloc([128, D], F32) for _ in range(3)]
            xnf = AR.alloc([128, D], BF16)
            xnp = AR.alloc([128, D], BF16)
            hTf = [AR.alloc([128, 8, 128], BF16) for _ in range(2)]
            hTp = AR.alloc([128, 8, 128], BF16)
            a_bf = AR.alloc([128, DFF], BF16)
            aT = AR.alloc([128, NFF, 128], BF16)
            p_sb = [AR.alloc([128, PLE], F32) for _ in range(2)]
            p_bf = AR.alloc([128, PLE], BF16)
            pT = [AR.alloc([128, 2, 128], BF16) for _ in range(3)]
            sig = [AR.alloc([128, 512], F32) for _ in range(2)]
            ssp = AR.alloc([128, 1], F32)
            rsp = AR.alloc([128, 1], F32)
            print("phase2 arena bytes", AR.off)
            emit_conv(len(conv_jobs))
            def wload(dst, src, nk, key, sem, step=4):
                v = src.rearrange("(c p) n -> p c n", p=128)
                for k0 in range(0, nk, step):
                    k1 = min(nk, k0 + step)
                    DMA('sp', dst[:, k0:k1, :], v[:, k0:k1, :], ['wconv'], [key], sem)
            wload(wg, wg_s, 8, 'wg', 'w2')
            wload(wu, wu_s, 8, 'wu', 'w2b')
            wload(wd, wd_s, NFF, 'wd', 'w3')
            wload(wpg, wpg_s, 8, 'wpg', 'w3b')
            wload(wpp, wpp_s, 2, 'wpp', 'w3c')

            tiles2 = []
            for seq in range(n_prompt_seq):
                for j in range(NT):
                    tiles2.append((seq * NT + j, pp[seq, j * 128:(j + 1) * 128, :], yp[seq, j * 128:(j + 1) * 128, :]))
            if do_sample:
                tiles2.append((SEQ_PER_CORE * NT, psm[:, :], ys[:, :]))
            NT2 = len(tiles2)
            bank_rot['mm'] = [0, 1, 2, 3, 4, 5, 6, 7]
            ctiles = [(c0, min(512, DFF - c0)) for c0 in range(0, DFF, 512)]

            def XK(t):
                return X2[t % 3], ('X2', t % 3)

            def rms_nonpe(X, kx, xnb, xk, ssb, sk, rsb, rk):
                ACT(xnb[:], X[:], AF.Square, [kx], [xk, sk], accum_out=ssb[:])
                ACT(rsb[:], ssb[:], AF.Ln, [sk], [rk], scale=1.0 / D, bias=EPS)
                ACT(rsb[:], rsb[:], AF.Exp, [rk], [rk], scale=-0.5)
                TS('dve', xnb[:], X[:], rsb[:], None, ALU.mult, None, [kx, rk], [xk])

            def rms_pe(xnb, xk, gain, gkey, dst, dkey):
                b = nextbank('mm')
                for c in range(8):
                    TR(pbank_bf[b][:, c * 128:(c + 1) * 128], xnb[:, c * 128:(c + 1) * 128], [xk], [PK(b)])
                TT('dve', dst[:], pbank_bf[b][:, :].rearrange("p (c t) -> p c t", c=8),
                   gain.unsqueeze(2).to_broadcast([128, 8, 128]), ALU.mult, [PK(b), gkey], [dkey])

            def P_nonpe(t):
                g, psrc, ydst = tiles2[t]
                X, kx = XK(t)
                pk = t % 2
                DMA('sp', X[:], x1s[g * 128:(g + 1) * 128, :], [('x1s', g)], [kx], f'y{t % 3}')
                DMA('sp', p_sb[pk][:], psrc, [], [('p', pk)], f'p{pk}')
                rms_nonpe(X, kx, xnf, 'xnf', ss, 'ss', rs, 'rs')
                CP('pool', p_bf[:], p_sb[pk][:], [('p', pk)], ['p_bf'])

            def P_pe(t):
                pk = t % 2
                rms_pe(xnf, 'xnf', gffn, 'gffn', hTf[pk], ('hTf', pk))
                b = nextbank('mm')
                for c in range(2):
                    TR(pbank_bf[b][:, c * 128:(c + 1) * 128], p_bf[:, c * 128:(c + 1) * 128], ['p_bf'], [PK(b)])
                ACT(pT[t % 3][:], pbank_bf[b][:, 0:256].rearrange("p (c t) -> p c t", c=2), AF.Copy, [PK(b)], [('pT', t % 3)])

            def GU_stage(t, cis):
                pk = t % 2
                hT2 = hTf[pk]
                hk = ('hTf', pk)
                for ci in cis:
                    c0, n = ctiles[ci]
                    bg = nextbank('mm')
                    for kc in range(8):
                        MM(pbank[bg][:, 0:n], hT2[:, kc, :], wg[:, kc, c0:c0 + n], kc == 0, kc == 7, [hk, 'wg'], [PK(bg)])
                    bu = nextbank('mm')
                    for kc in range(8):
                        MM(pbank[bu][:, 0:n], hT2[:, kc, :], wu[:, kc, c0:c0 + n], kc == 0, kc == 7, [hk, 'wu'], [PK(bu)])
                    k2 = ci % 2
                    ACT(sig[k2][:, 0:n], pbank[bg][:, 0:n], AF.Sigmoid, [PK(bg)], [('sig', k2)])
                    TT('dve', sig[k2][:, 0:n], pbank[bg][:, 0:n], sig[k2][:, 0:n], ALU.mult, [PK(bg), ('sig', k2)], [('sig', k2)])
                    TT('dve', a_bf[:, c0:c0 + n], sig[k2][:, 0:n], pbank[bu][:, 0:n], ALU.mult, [('sig', k2), PK(bu)], [('a', ci)])

            def AT_stage(t):
                for g0 in range(0, NFF, 8):
                    ng = min(8, NFF - g0)
                    b = nextbank('mm')
                    rd = sorted(set(('a', (c * 128) // 512) for c in range(g0, g0 + ng)))
                    for c in range(ng):
                        TR(pbank_bf[b][:, c * 128:(c + 1) * 128], a_bf[:, (g0 + c) * 128:(g0 + c + 1) * 128], rd, [PK(b)])
                    ACT(aT[:, g0:g0 + ng, :], pbank_bf[b][:, 0:ng * 128].rearrange("p (c t) -> p c t", c=ng), AF.Copy,
                        [PK(b)], [('aT', g0)])

            def DN_stage(t):
                X, kx = XK(t)
                for n in range(2):
                    b = nextbank('mm')
                    for kc in range(NFF):
                        MM(pbank[b][:, :], aT[:, kc, :], wd[:, kc, n * 512:(n + 1) * 512], kc == 0, kc == NFF - 1,
                           [('aT', (kc // 8) * 8), 'wd'], [PK(b)])
                    TT('dve', X[:, n * 512:(n + 1) * 512], X[:, n * 512:(n + 1) * 512], pbank[b][:, :], ALU.add,
                       [kx, PK(b)], [kx])

            def R2_nonpe(t):
                X, kx = XK(t)
                rms_nonpe(X, kx, xnp, 'xnp', ssp, 'ssp', rsp, 'rsp')

            def R2_pe(t):
                rms_pe(xnp, 'xnp', gple, 'gple', hTp, 'hTp')

            def PL_stage(t):
                g, psrc, ydst = tiles2[t]
                X, kx = XK(t)
                pk = t % 2
                for n in range(2):
                    bg = nextbank('mm')
                    for kc in range(8):
                        MM(pbank[bg][:, :], hTp[:, kc, :], wpg[:, kc, n * 512:(n + 1) * 512], kc == 0, kc == 7,
                           ['hTp', 'wpg'], [PK(bg)])
                    bp = nextbank('mm')
                    for kc in range(2):
                        MM(pbank[bp][:, :], pT[t % 3][:, kc, :], wpp[:, kc, n * 512:(n + 1) * 512], kc == 0, kc == 1,
                           [('pT', t % 3), 'wpp'], [PK(bp)])
                    k2 = n % 2
                    ACT(sig[k2][:], pbank[bg][:, :], AF.Sigmoid, [PK(bg)], [('sig', k2)])
                    TT('dve', sig[k2][:], pbank[bp][:, :], sig[k2][:], ALU.mult, [PK(bp), ('sig', k2)], [('sig', k2)])
                    TT('dve', X[:, n * 512:(n + 1) * 512], X[:, n * 512:(n + 1) * 512], sig[k2][:], ALU.add,
                       [kx, ('sig', k2)], [kx])
                DMA('sp', ydst, X[:], [kx], [], f'yo{t % 3}', True)

            P_nonpe(0)
            P_pe(0)
            if NT2 > 1:
                P_nonpe(1)
                P_pe(1)
            for k in range(NT2 + 1):
                if k >= 1:
                    R2_nonpe(k - 1)
                if k < NT2:
                    GU_stage(k, [0, 1, 2])
                if k >= 1:
                    R2_pe(k - 1)
                if k < NT2:
                    GU_stage(k, [3, 4, 5])
                if k >= 1:
                    PL_stage(k - 1)
                if k < NT2:
                    AT_stage(k)
                if k + 2 < NT2:
                    P_nonpe(k + 2)
                if k < NT2:
                    DN_stage(k)
                if k + 2 < NT2:
                    P_pe(k + 2)

        stats = S.emit()
        print("sched stats (ops, waits):", stats, "held-bank skips:", skipped[0], "still held:", sorted(held))
    return nc


_PROGRAM = {}


def _rel_bias_layout(rel):
    k = np.arange(128)[:, None]
    q = np.arange(128)[None, :]
    out = np.empty((128, 8, 4, 128), np.float32)
    for si, kt in enumerate((0, 1, 3, 4)):
        r = (4 - kt) * 128 + q - k
        idx = np.clip(r, -128, 128) + 128
        out[:, :, si, :] = rel[:, idx].transpose(1, 0, 2)
    idxn = np.clip((q % 32) - (k % 32), -128, 128) + 128
    outn = np.ascontiguousarray(rel[:, idxn].transpose(1, 0, 2))
    return np.ascontiguousarray(out.reshape(128, -1)), np.ascontiguousarray(outn.reshape(128, -1))


def kernel(x_prompt, x_sample, cache_k_a, cache_v_a, cache_k_b, cache_v_b, cache_logf_b,
           p_prompt, p_sample, norm_mix, w_in, b_f, q_norm_a, k_norm_a, q_norm_b, k_norm_b,
           rel_bias_a, w_out, norm_ffn, w_gate, w_up, w_down, norm_ple, w_ple_gate, w_ple_proj):
    f = lambda a: np.ascontiguousarray(np.asarray(a, dtype=np.float32))
    x_prompt = f(x_prompt); x_sample = f(x_sample)
    cache_k_a = f(cache_k_a); cache_v_a = f(cache_v_a); cache_k_b = f(cache_k_b); cache_v_b = f(cache_v_b)
    cache_logf_b = f(cache_logf_b); p_prompt = f(p_prompt); p_sample = f(p_sample)
    if 'nc' not in _PROGRAM:
        _PROGRAM['nc'] = build_program()
    nc = _PROGRAM['nc']
    biasT, biasN = _rel_bias_layout(f(rel_bias_a)[0])
    cst = make_consts()
    g2 = lambda g: np.ascontiguousarray(f(g)[0].reshape(8, 128).T)
    shared = dict(
        w_in=f(w_in)[0], w_out=f(w_out)[0], w_gate=f(w_gate)[0], w_up=f(w_up)[0], w_down=f(w_down)[0],
        w_pg=f(w_ple_gate)[0], w_pp=f(w_ple_proj)[0],
        gmix=g2(norm_mix), gffn=g2(norm_ffn), gple=g2(norm_ple),
        bf=f(b_f), qna=f(q_norm_a), kna=f(k_norm_a), qnb=f(q_norm_b), knb=f(k_norm_b),
        biasT=biasT, biasN=biasN, cst=cst)
    in_maps = []
    for c in range(NCORES):
        sl = slice(c * SEQ_PER_CORE, (c + 1) * SEQ_PER_CORE)
        m = dict(shared)
        m.update(
            xp=x_prompt[sl], pp=p_prompt[0, sl],
            xs=x_sample[sl].reshape(128, D), psm=p_sample[0, sl].reshape(128, PLE),
            cka=cache_k_a[0, sl].reshape(SEQ_PER_CORE, WA, 512), cva=cache_v_a[0, sl].reshape(SEQ_PER_CORE, WA, 512),
            ckb=cache_k_b[0, sl].reshape(SEQ_PER_CORE, PAST, 512), cvb=cache_v_b[0, sl].reshape(SEQ_PER_CORE, PAST, 512),
            clf=cache_logf_b[0, sl])
        in_maps.append(m)
    res = run_bass_kernel_spmd(nc, in_maps, core_ids=list(range(NCORES)))
    R = res.results
    cat = lambda k: np.concatenate([r[k] for r in R], axis=0)
    B = NCORES * SEQ_PER_CORE
    y_prompt = cat("yp")
    y_sample = cat("ys").reshape(B, 32, D)
    kap = cat("kap").reshape(1, B, WA, 8, 64)
    vap = cat("vap").reshape(1, B, WA, 8, 64)
    kbp = cat("kbp").reshape(1, B, T, 8, 64)
    vbp = cat("vbp").reshape(1, B, T, 8, 64)
    lfp = cat("lfp").reshape(1, B, T, 8)
    kas = cat("kas").reshape(1, B, 32, 8, 64)
    vas = cat("vas").reshape(1, B, 32, 8, 64)
    kbs = cat("kbs").reshape(1, B, 32, 8, 64)
    vbs = cat("vbs").reshape(1, B, 32, 8, 64)
    lfs = cat("lfs").reshape(1, B, 32, 8)
    return (y_prompt, y_sample, kap, vap, kbp, vbp, lfp, kas, vas, kbs, vbs, lfs)
```

```python
import contextlib
import numpy as np
import concourse.bass as bass
import concourse.mybir as mybir
from concourse.bass_utils import run_bass_kernel_spmd

F32 = mybir.dt.float32
BF16 = mybir.dt.bfloat16
AF = mybir.ActivationFunctionType
ALU = mybir.AluOpType
AX = mybir.AxisListType

NCORES = 8
D = 1024
T = 2048
NT = T // 128
SEQ_PER_CORE = 4
DIN = 3080
DFF = 2816
NFF = DFF // 128
PLE = 256
PAST = 4096
NCT = PAST // 128
WA = 512
RA = 6
EPS = 1e-6
NEG = -30000.0
EIDX = {0: 0, 1: 1, 2: 1, 3: 2, 4: 3}

ENGS = ('pe', 'act', 'dve', 'pool', 'sp')


class Op:
    __slots__ = ('eng', 'fn', 'deps', 'signal', 'semval', 'sem', 'is_dma')

    def __init__(self, eng, fn, is_dma=False):
        self.eng = eng
        self.fn = fn
        self.deps = []
        self.signal = False
        self.semval = None
        self.sem = None
        self.is_dma = is_dma


class Sched:
    def __init__(self, nc):
        self.nc = nc
        self.ops = {e: [] for e in ENGS}
        self.W = {}
        self.R = {}
        self.dma_cnt = {}
        self.dma_last = {}
        self.out_last = {}

    def _deps(self, op, reads, writes, ident):
        deps = []
        for k in reads:
            for e, w in self.W.get(k, {}).items():
                deps.append(w)
        for k in writes:
            for e, w in self.W.get(k, {}).items():
                if e != ident or op.is_dma or ident != 'pe':
                    deps.append(w)
            for e, r in self.R.get(k, {}).items():
                if e != ident or op.is_dma or ident != 'pe':
                    deps.append(r)
        seen = set()
        out = []
        for d in deps:
            if id(d) not in seen and d is not op:
                seen.add(id(d))
                out.append(d)
        op.deps = out
        for k in reads:
            self.R.setdefault(k, {})[ident] = op
        for k in writes:
            self.W[k] = {ident: op}
            self.R[k] = {}

    def op(self, eng, fn, reads=(), writes=()):
        o = Op(eng, fn)
        self._deps(o, reads, writes, eng)
        self.ops[eng].append(o)
        return o

    def dma(self, queue, fn, reads=(), writes=(), sem=None, is_output=False):
        o = Op(queue, fn, is_dma=True)
        o.sem = sem
        self.dma_cnt[sem] = self.dma_cnt.get(sem, 0) + 16
        o.semval = self.dma_cnt[sem]
        self._deps(o, reads, writes, 'dma:' + sem)
        self.ops[queue].append(o)
        self.dma_last[sem] = o
        if is_output:
            self.out_last[sem] = o
        return o

    def barrier(self):
        last = []
        for e in ENGS:
            for o in reversed(self.ops[e]):
                if o.fn is not None and not o.is_dma:
                    last.append(o)
                    break
        last += list(self.dma_last.values())
        for e in ENGS:
            b = Op(e, None)
            b.deps = list(last)
            self.ops[e].append(b)
        self.W = {}
        self.R = {}

    def emit(self):
        nc = self.nc
        fin = Op('sp', None)
        fin.deps = list(self.out_last.values())
        self.ops['sp'].append(fin)
        for e in ENGS:
            for o in self.ops[e]:
                for d in o.deps:
                    if not d.is_dma:
                        d.signal = True
        for e in ENGS:
            c = 0
            for o in self.ops[e]:
                if not o.is_dma and o.fn is not None and o.signal:
                    c += 1
                    o.semval = c
                    o.sem = 'eng:' + e
        stats = {}
        with contextlib.ExitStack() as es:
            sems = {}
            for e in ENGS:
                sems['eng:' + e] = es.enter_context(nc.semaphore('s_' + e))
            for s in self.dma_cnt:
                sems[s] = es.enter_context(nc.semaphore('d_' + s))
            block = es.enter_context(nc.Block())

            def run(e, engobj):
                waited = {}
                nw = 0
                for o in self.ops[e]:
                    for d in o.deps:
                        if waited.get(d.sem, 0) >= d.semval:
                            continue
                        engobj.wait_ge(sems[d.sem], d.semval)
                        waited[d.sem] = d.semval
                        nw += 1
                    if o.fn is None:
                        continue
                    ins = o.fn(engobj)
                    if o.is_dma:
                        ins.then_inc(sems[o.sem], 16)
                    elif o.signal:
                        ins.then_inc(sems[o.sem], 1)
                stats[e] = (len(self.ops[e]), nw)

            @block.tensor
            def _(eng):
                run('pe', eng)

            @block.scalar
            def _(eng):
                run('act', eng)

            @block.vector
            def _(eng):
                run('dve', eng)

            @block.gpsimd
            def _(eng):
                run('pool', eng)

            @block.sync
            def _(eng):
                run('sp', eng)
        return stats


class Arena:
    def __init__(self, tensor, nbytes):
        self.t = tensor
        self.nbytes = nbytes
        self.off = 0

    def alloc(self, shape, dt):
        nfree = int(np.prod(shape[1:]))
        sz = 4 if dt == F32 else 2
        nb = (nfree * sz + 63) // 64 * 64
        a = self.off
        self.off += nb
        assert self.off <= self.nbytes, f"arena overflow {self.off} > {self.nbytes}"
        v = self.t[:, a // 4:(a + nb) // 4]
        if dt != F32:
            v = v.bitcast(dt)
        v = v[:, 0:nfree]
        if len(shape) > 2:
            names = " ".join(f"d{i}" for i in range(1, len(shape)))
            kw = {f"d{i}": int(shape[i]) for i in range(1, len(shape))}
            v = v.rearrange(f"p ({names}) -> p {names}", **kw)
        return v[0:shape[0]]


C_TRI, C_ONES, C_TRIBLK, C_M0, C_M4, C_MBD01, C_M0N, C_M4N, C_MBDN = range(9)
NCONST_SB = 9 * 128 + 4
OFF_IDENT = 9 * 128 + 4
NCONST = 12 * 128 + 4


def make_consts():
    k = np.arange(128)[:, None]
    q = np.arange(128)[None, :]
    tri = (k <= q).astype(np.float32)
    ones = np.ones((128, 128), np.float32)
    triblk = ((k // 32 == q // 32) & (k <= q)).astype(np.float32)
    m0 = np.ones((128, 128), np.float32)
    m0[0:64, 64:128] = 0.0
    m4 = np.ones((128, 128), np.float32)
    m4[64:128, 0:64] = 0.0
    mbd01 = (k // 32 == q // 32).astype(np.float32)
    ident = np.eye(128, dtype=np.float32)
    maskc = np.where(k > q, NEG, 0.0).astype(np.float32)
    maskbd = np.where((k // 32 != q // 32) | (k > q), NEG, 0.0).astype(np.float32)
    ind4 = (np.arange(128)[:, None] // 32 == np.arange(4)[None, :]).astype(np.float32)
    return np.ascontiguousarray(
        np.concatenate([tri, ones, triblk, m0, m4, mbd01, (m0 - 1) * (-NEG) , (m4 - 1) * (-NEG), (mbd01 - 1) * (-NEG),
                        ind4, ident, maskc, maskbd], axis=1).astype(np.float32))


def build_program(n_prompt_seq=SEQ_PER_CORE, do_sample=True, do_phase2=True):
    nc = bass.Bass("TRN2", target_bir_lowering=False)

    def din(name, shape):
        return nc.dram_tensor(name, list(shape), F32, kind="ExternalInput").ap()

    def dout(name, shape):
        return nc.dram_tensor(name, list(shape), F32, kind="ExternalOutput").ap()

    xp = din("xp", [SEQ_PER_CORE, T, D])
    pp = din("pp", [SEQ_PER_CORE, T, PLE])
    xs = din("xs", [128, D])
    psm = din("psm", [128, PLE])
    cka = din("cka", [SEQ_PER_CORE, WA, 512])
    cva = din("cva", [SEQ_PER_CORE, WA, 512])
    ckb = din("ckb", [SEQ_PER_CORE, PAST, 512])
    cvb = din("cvb", [SEQ_PER_CORE, PAST, 512])
    clf = din("clf", [SEQ_PER_CORE, PAST, 8])
    w_in = din("w_in", [D, DIN])
    w_out = din("w_out", [D, D])
    w_gate = din("w_gate", [D, DFF])
    w_up = din("w_up", [D, DFF])
    w_down = din("w_down", [DFF, D])
    w_pg = din("w_pg", [D, D])
    w_pp = din("w_pp", [PLE, D])
    gmix_d = din("gmix", [128, 8])
    gffn_d = din("gffn", [128, 8])
    gple_d = din("gple", [128, 8])
    bf_d = din("bf", [1, 8])
    qna_d = din("qna", [1, 64])
    kna_d = din("kna", [1, 64])
    qnb_d = din("qnb", [1, 64])
    knb_d = din("knb", [1, 64])
    biasT_d = din("biasT", [128, 8 * 4 * 128])
    biasN_d = din("biasN", [128, 8 * 128])
    cst_d = din("cst", [128, NCONST])

    yp = dout("yp", [SEQ_PER_CORE, T, D])
    ys = dout("ys", [128, D])
    kap = dout("kap", [SEQ_PER_CORE, WA, 512])
    vap = dout("vap", [SEQ_PER_CORE, WA, 512])
    kbp = dout("kbp", [SEQ_PER_CORE, T, 512])
    vbp = dout("vbp", [SEQ_PER_CORE, T, 512])
    lfp = dout("lfp", [SEQ_PER_CORE, T, 8])
    kas = dout("kas", [128, 512])
    vas = dout("vas", [128, 512])
    kbs = dout("kbs", [128, 512])
    vbs = dout("vbs", [128, 512])
    lfs = dout("lfs", [128, 8])

    NTILES = SEQ_PER_CORE * NT + 1
    def dscr(name, shape):
        return nc.dram_tensor(name, list(shape), BF16, kind="Internal").ap()
    wg_s = dscr("wg_s", [D, DFF])
    wu_s = dscr("wu_s", [D, DFF])
    wd_s = dscr("wd_s", [DFF, D])
    wpg_s = dscr("wpg_s", [D, D])
    wpp_s = dscr("wpp_s", [PLE, D])
    x1s = nc.dram_tensor("x1s", [NTILES * 128, D], F32, kind="Internal").ap()

    ARENA_BYTES = 212800
    with contextlib.ExitStack() as es:
        arena_t = es.enter_context(nc.sbuf_tensor("arena", [128, ARENA_BYTES // 4], F32))
        AR = Arena(arena_t, ARENA_BYTES)
        pbank = [es.enter_context(nc.psum_tensor(f"pb{i}", [128, 512], F32)) for i in range(8)]
        pbank_bf = [pb[:, :].bitcast(BF16) for pb in pbank]
        S = Sched(nc)

        bank_rot = {'mm': [0, 1, 2, 3], 'st': [4, 5, 6], 'oa': [7]}
        bank_ctr = {'mm': 0, 'st': 0, 'oa': 0}

        held = set()
        skipped = [0]

        def nextbank(cls, hold=False):
            lst = bank_rot[cls]
            for _ in range(len(lst)):
                b = lst[bank_ctr[cls] % len(lst)]
                bank_ctr[cls] += 1
                if b not in held:
                    if hold:
                        held.add(b)
                    return b
                skipped[0] += 1
            raise RuntimeError("all PSUM banks of class %s are held" % cls)

        def release(b):
            held.discard(b)

        def PK(b):
            return ('ps', b)

        def MM(out, lhsT, rhs, start, stop, reads, writes):
            S.op('pe', lambda e: e.matmul(out, lhsT=lhsT, rhs=rhs, start=start, stop=stop),
                 reads=reads, writes=writes)

        def TR(out, in_, reads, writes):
            S.op('pe', lambda e: e.transpose(out=out, in_=in_, identity=ident_bf[:]),
                 reads=list(reads) + ['cbf'], writes=writes)

        def ACT(out, in_, func, reads, writes, **kw):
            S.op('act', lambda e: e.activation(out=out, in_=in_, func=func, **kw),
                 reads=reads, writes=writes)

        def TT(eng, out, in0, in1, op, reads, writes):
            S.op(eng, lambda e: e.tensor_tensor(out=out, in0=in0, in1=in1, op=op),
                 reads=reads, writes=writes)

        def TS(eng, out, in0, s1, s2, op0, op1, reads, writes):
            if s2 is None:
                S.op(eng, lambda e: e.tensor_scalar(out=out, in0=in0, scalar1=s1, scalar2=None, op0=op0),
                     reads=reads, writes=writes)
            else:
                S.op(eng, lambda e: e.tensor_scalar(out=out, in0=in0, scalar1=s1, scalar2=s2, op0=op0, op1=op1),
                     reads=reads, writes=writes)

        def STT(out, in0, scalar, in1, op0, op1, reads, writes):
            S.op('dve', lambda e: e.scalar_tensor_tensor(out=out, in0=in0, scalar=scalar, in1=in1, op0=op0, op1=op1),
                 reads=reads, writes=writes)

        def CP(eng, out, in_, reads, writes):
            S.op(eng, lambda e: e.tensor_copy(out=out, in_=in_), reads=reads, writes=writes)

        def MEMSET(eng, ap, val, writes):
            S.op(eng, lambda e: e.memset(ap, val), writes=writes)

        def DMA(queue, out, in_, reads, writes, sem, is_output=False):
            S.dma(queue, lambda e: e.dma_start(out=out, in_=in_), reads=reads, writes=writes, sem=sem,
                  is_output=is_output)

        cst = AR.alloc([128, NCONST_SB], F32)
        tri_f = cst[:, C_TRI * 128:(C_TRI + 1) * 128]
        ones_f = cst[:, C_ONES * 128:(C_ONES + 1) * 128]
        triblk_f = cst[:, C_TRIBLK * 128:(C_TRIBLK + 1) * 128]
        m0_f = cst[:, C_M0 * 128:(C_M0 + 1) * 128]
        m4_f = cst[:, C_M4 * 128:(C_M4 + 1) * 128]
        mbd01_f = cst[:, C_MBD01 * 128:(C_MBD01 + 1) * 128]
        m0n_f = cst[:, C_M0N * 128:(C_M0N + 1) * 128]
        m4n_f = cst[:, C_M4N * 128:(C_M4N + 1) * 128]
        mbdn_f = cst[:, C_MBDN * 128:(C_MBDN + 1) * 128]
        ind4_f = cst[:, 9 * 128:9 * 128 + 4]
        cbf = AR.alloc([128, 3 * 128], BF16)
        ident_bf = cbf[:, 0:128]
        maskc_bf = cbf[:, 128:256]
        maskbd_bf = cbf[:, 256:384]
        gmix = AR.alloc([128, 8], F32)
        gffn = AR.alloc([128, 8], F32)
        gple = AR.alloc([128, 8], F32)
        bft = AR.alloc([128, 8], F32)
        qna = AR.alloc([128, 64], F32)
        kna = AR.alloc([128, 64], F32)
        qnb = AR.alloc([128, 64], F32)
        knb = AR.alloc([128, 64], F32)
        caug = AR.alloc([128, SEQ_PER_CORE, NCT * 8 * 3], BF16)
        tot8 = AR.alloc([128, SEQ_PER_CORE, 8], F32)
        ss = AR.alloc([128, 1], F32)
        rs = AR.alloc([128, 1], F32)
        cH = AR.alloc([128, 8], F32)
        PERSIST_END = AR.off

        DMA('sp', cst[:], cst_d[:, 0:NCONST_SB], [], ['cst'], 'c0_1')
        DMA('pool', cbf[:], cst_d[:, OFF_IDENT:OFF_IDENT + 3 * 128], [], ['cbf'], 'c1')
        DMA('sp', gmix[:], gmix_d[:, :], [], ['gmix'], 'c0_2')
        DMA('sp', gffn[:], gffn_d[:, :], [], ['gffn'], 'c0_3')
        DMA('sp', gple[:], gple_d[:, :], [], ['gple'], 'c0_4')
        DMA('sp', bft[:], bf_d.partition_broadcast(128), [], ['bft'], 'c0_5')
        DMA('sp', qna[:], qna_d.partition_broadcast(128), [], ['qna'], 'c0_6')
        DMA('sp', kna[:], kna_d.partition_broadcast(128), [], ['kna'], 'c0_7')
        DMA('sp', qnb[:], qnb_d.partition_broadcast(128), [], ['qnb'], 'c0_8')
        DMA('sp', knb[:], knb_d.partition_broadcast(128), [], ['knb'], 'c0_9')

        win = AR.alloc([128, 8, DIN], BF16)
        wout = AR.alloc([128, 8, D], BF16)
        Bhi = AR.alloc([128, 8, 4, 128], BF16)
        Blo = AR.alloc([128, 8, 4, 128], BF16)
        Xb = [AR.alloc([128, D], F32) for _ in range(3)]
        xn = AR.alloc([128, D], BF16)
        hTb = [AR.alloc([128, 8, 128], BF16) for _ in range(2)]
        kvo = [AR.alloc([128, 512], F32) for _ in range(4)]
        sqb = [AR.alloc([128, 512], F32)]
        ssqb = [AR.alloc([128, 8], F32) for _ in range(2)]
        qa_bf = AR.alloc([128, 8, 64], BF16)
        ka_bf = AR.alloc([128, 8, 64], BF16)
        qb_aug = AR.alloc([128, 8, 70], BF16)
        kb_aug = AR.alloc([128, 8, 70], BF16)
        qaT = [AR.alloc([64, 8, 128], BF16) for _ in range(2)]
        qbT = [AR.alloc([70, 8, 128], BF16) for _ in range(2)]
        KaT = AR.alloc([64, RA, 8, 128], BF16)
        Va = AR.alloc([128, RA, 8, 65], BF16)
        KbT = AR.alloc([70, NT, 8, 128], BF16)
        KBT_TILE_BYTES = 8 * 128 * 2
        kbt_base = AR.off - NT * KBT_TILE_BYTES
        Vb = AR.alloc([128, NT, 8, 65], BF16)
        lf_all = AR.alloc([128, NT, 8], F32)
        xf = AR.alloc([128, 8], F32)
        ef = AR.alloc([128, 8], F32)
        carry8 = AR.alloc([128, 8], F32)
        C8 = AR.alloc([128, 8], F32)
        r1 = AR.alloc([128, 8], F32)
        r2 = AR.alloc([128, 8], F32)
        sel8 = AR.alloc([128, 8], F32)
        PT = [AR.alloc([128, 512], BF16) for _ in range(3)]
        rinv = AR.alloc([128, 8], F32)
        O_bf = AR.alloc([128, D], BF16)
        OT = AR.alloc([128, 8, 128], BF16)
        P1_END = AR.off
        print("phase1 arena bytes", P1_END)

        SA = Arena(arena_t, kbt_base + NT * KBT_TILE_BYTES)
        SA.off = kbt_base + KBT_TILE_BYTES
        stgK = [SA.alloc([128, 512], F32) for _ in range(2)]
        stgV = [SA.alloc([128, 512], F32) for _ in range(2)]
        kc_aug = [SA.alloc([128, 8, 70], BF16) for _ in range(2)]
        vc_aug = [SA.alloc([128, 8, 65], BF16) for _ in range(2)]
        kcT = [SA.alloc([70, 8, 128], BF16) for _ in range(2)]
        PTz = [SA.alloc([128, 8, 128], BF16) for _ in range(2)]
        accA = SA.alloc([128, 8, 65], F32)
        accB = SA.alloc([128, 8, 65], F32)
        BNhi = SA.alloc([128, 8, 128], BF16)
        BNlo = SA.alloc([128, 8, 128], BF16)
        vb_base = kbt_base + NT * KBT_TILE_BYTES
        SB2 = Arena(arena_t, vb_base + NT * 8 * 65 * 2)
        SB2.off = vb_base + 1088
        BNf = SB2.alloc([128, 8, 128], F32)
        stgK3 = SB2.alloc([128, 512], F32)
        stgV3 = SB2.alloc([128, 512], F32)
        PA = Arena(arena_t, vb_base + 16 * 1024)
        PA.off = vb_base
        clf_sb = PA.alloc([128, NCT, 8], F32)
        Tsb = PA.alloc([128, NCT, 8], F32)
        Pincl = PA.alloc([128, NCT, 8], F32)
        ex8 = PA.alloc([128, NCT, 8], F32)
        Cc8 = PA.alloc([128, NCT, 8], F32)
        cr1 = PA.alloc([128, NCT, 8], F32)
        cr2 = PA.alloc([128, NCT, 8], F32)
        cbf1 = PA.alloc([128, NCT, 8], BF16)
        cbf2 = PA.alloc([128, NCT, 8], BF16)
        cbf3 = PA.alloc([128, NCT, 8], BF16)

        w_in_v = w_in.rearrange("(c p) n -> p c n", p=128)
        WIN_TILES = [(3072, DIN), (0, 512), (512, 1024), (1024, 1536), (1536, 2048), (2048, 2560), (2560, 3072)]
        for (c0, c1) in WIN_TILES:
            DMA('pool', win[:, :, c0:c1], w_in_v[:, :, c0:c1], [], [('win', c0)], f'w1_{c0}')
        DMA('pool', wout[:], w_out.rearrange("(c p) n -> p c n", p=128), [], ['wout'], 'w1b')

        PB = Arena(arena_t, kbt_base + NT * KBT_TILE_BYTES)
        PB.off = kbt_base
        Bf = PB.alloc([128, 8, 4, 128], F32)
        Bf2 = PB.alloc([128, 8 * 4 * 128], F32)
        Bff = Bf.rearrange("p h k q -> p (h k q)")
        DMA('sp', Bff, biasT_d[:, :], [], ['Bf'], 'c2')
        TS('dve', Bff, Bff, 8.0, None, ALU.mult, None, ['Bf'], ['Bf'])
        CP('dve', cH[:], Bf[:, :, 1, 0], ['Bf'], ['cH'])
        for idx in (0, 2, 3):
            TT('dve', Bf[:, :, idx, :], Bf[:, :, idx, :], cH.unsqueeze(2).to_broadcast([128, 8, 128]), ALU.subtract,
               ['Bf', 'cH'], ['Bf'])
        for (idx, mf, mn) in ((0, m0_f, m0n_f), (3, m4_f, m4n_f)):
            TT('dve', Bf[:, :, idx, :], Bf[:, :, idx, :], mf.unsqueeze(1).to_broadcast([128, 8, 128]), ALU.mult,
               ['Bf', 'cst'], ['Bf'])
            TT('dve', Bf[:, :, idx, :], Bf[:, :, idx, :], mn.unsqueeze(1).to_broadcast([128, 8, 128]), ALU.add,
               ['Bf', 'cst'], ['Bf'])
        Bhf = Bhi.rearrange("p h k q -> p (h k q)")
        Blf = Blo.rearrange("p h k q -> p (h k q)")
        CP('dve', Bhf, Bff, ['Bf'], ['Bhi'])
        TT('dve', Bf2[:], Bff, Bhf, ALU.subtract, ['Bf', 'Bhi'], ['Bf2'])
        CP('dve', Blf, Bf2[:], ['Bf2'], ['Blo'])

        if do_sample:
            for s in range(SEQ_PER_CORE):
                clf_v = clf[s].rearrange("(t p) h -> p t h", p=128)
                for t0 in range(0, NCT, 4):
                    DMA('sp', clf_sb[:, t0:t0 + 4, :], clf_v[:, t0:t0 + 4, :], [], ['clf_sb'], 'c3')
                clf2 = clf_sb.rearrange("p t h -> p (t h)")
                bT = nextbank('mm')
                bR = nextbank('mm')
                MM(pbank[bT][:, 0:256], ones_f, clf2, True, True, ['cst', 'clf_sb'], [PK(bT)])
                MM(pbank[bR][:, 0:256], tri_f, clf2, True, True, ['cst', 'clf_sb'], [PK(bR)])
                Ts2 = Tsb.rearrange("p t h -> p (t h)")
                ACT(Ts2, pbank[bT][:, 0:256], AF.Copy, [PK(bT)], ['Tsb'])
                for h in range(8):
                    S.op('dve', lambda e, h=h: e.tensor_tensor_scan(
                        out=Pincl[:, :, h], data0=ones_f[:, 0:NCT], data1=Tsb[:, :, h], initial=0.0,
                        op0=ALU.mult, op1=ALU.add), reads=['Tsb', 'cst'], writes=['Pincl'])
                P2 = Pincl.rearrange("p t h -> p (t h)")
                e2 = ex8.rearrange("p t h -> p (t h)")
                c2 = Cc8.rearrange("p t h -> p (t h)")
                STT(e2, Ts2, -1.0, P2, ALU.mult, ALU.add, ['Tsb', 'Pincl'], ['ex8'])
                TS('dve', e2, e2, 8.0, None, ALU.mult, None, ['ex8'], ['ex8'])
                STT(c2, pbank[bR][:, 0:256], 8.0, e2, ALU.mult, ALU.add, [PK(bR), 'ex8'], ['Cc8'])
                TS('dve', tot8[:, s, :], Pincl[:, NCT - 1, :], 8.0, None, ALU.mult, None, ['Pincl'], ['tot8'])
                b1 = cbf1.rearrange("p t h -> p (t h)")
                b2 = cbf2.rearrange("p t h -> p (t h)")
                b3 = cbf3.rearrange("p t h -> p (t h)")
                q1 = cr1.rearrange("p t h -> p (t h)")
                q2 = cr2.rearrange("p t h -> p (t h)")
                CP('dve', b1, c2, ['Cc8'], ['cbf1'])
                TT('dve', q1, c2, b1, ALU.subtract, ['Cc8', 'cbf1'], ['cr1'])
                CP('dve', b2, q1, ['cr1'], ['cbf2'])
                TT('dve', q2, q1, b2, ALU.subtract, ['cr1', 'cbf2'], ['cr2'])
                CP('dve', b3, q2, ['cr2'], ['cbf3'])
                cav = caug[:, s, :].rearrange("p (t h c) -> p t h c", t=NCT, h=8)
                for ci, bsrc in enumerate((cbf1, cbf2, cbf3)):
                    TS('dve', cav[:, :, :, ci], bsrc[:], -1.0, None, ALU.mult, None,
                       [f'cbf{ci + 1}'], ['caug'])
        S.barrier()

        MEMSET('pool', qb_aug[:, :, 67:70], 1.0, ['qb_aug'])
        MEMSET('pool', kb_aug[:, :, 64:67], 1.0, ['kb_aug'])
        MEMSET('pool', Va[:, :, :, 64:65], 1.0, ['Va_ones'])
        MEMSET('pool', Vb[:, :, :, 64:65], 1.0, ['Vb_ones'])

        xcnt = [0]
        tile_ctr = [0]
        conv_jobs = []
        for kc in range(8):
            for (c0, c1) in ((0, 1408), (1408, DFF)):
                conv_jobs.append((wg_s[kc * 128:(kc + 1) * 128, c0:c1], w_gate[kc * 128:(kc + 1) * 128, c0:c1]))
        for kc in range(8):
            for (c0, c1) in ((0, 1408), (1408, DFF)):
                conv_jobs.append((wu_s[kc * 128:(kc + 1) * 128, c0:c1], w_up[kc * 128:(kc + 1) * 128, c0:c1]))
        for kc in range(NFF):
            conv_jobs.append((wd_s[kc * 128:(kc + 1) * 128, :], w_down[kc * 128:(kc + 1) * 128, :]))
        for kc in range(8):
            conv_jobs.append((wpg_s[kc * 128:(kc + 1) * 128, :], w_pg[kc * 128:(kc + 1) * 128, :]))
        for kc in range(2):
            conv_jobs.append((wpp_s[kc * 128:(kc + 1) * 128, :], w_pp[kc * 128:(kc + 1) * 128, :]))
        conv_pos = [0]

        def emit_conv(n=1):
            for _ in range(n):
                if conv_pos[0] < len(conv_jobs):
                    o, i = conv_jobs[conv_pos[0]]
                    conv_pos[0] += 1
                    DMA('pool', o, i, [], ['wconv'], 'cvw')

        def rmsnorm_to_T(X, kx, gain, gkey, dstT, dkey='hT'):
            ACT(xn[:], X[:], AF.Square, [kx], ['xn', 'ss'], accum_out=ss[:])
            ACT(rs[:], ss[:], AF.Ln, ['ss'], ['rs'], scale=1.0 / D, bias=EPS)
            ACT(rs[:], rs[:], AF.Exp, ['rs'], ['rs'], scale=-0.5)
            TS('dve', xn[:], X[:], rs[:], None, ALU.mult, None, [kx, 'rs'], ['xn'])
            b = nextbank('mm')
            for c in range(8):
                TR(pbank_bf[b][:, c * 128:(c + 1) * 128], xn[:, c * 128:(c + 1) * 128], ['xn'], [PK(b)])
            TT('dve', dstT[:], pbank_bf[b][:, :].rearrange("p (c t) -> p c t", c=8),
               gain.unsqueeze(2).to_broadcast([128, 8, 128]), ALU.mult, [PK(b), gkey], [dkey])

        def head_norm_a(b, sq_buf, sqk, ssq_buf, ssk):
            ACT(sq_buf[:], pbank[b][:, :], AF.Square, [PK(b)], [sqk])
            S.op('dve', lambda e: e.tensor_reduce(out=ssq_buf[:], in_=sq_buf.rearrange("p (h d) -> p h d", h=8),
                                                  axis=AX.X, op=ALU.add), reads=[sqk], writes=[ssk])

        def head_norm_b(b, sq_buf, sqk, ssq_buf, ssk, gain, gkey, out_ap, out_key):
            pv = pbank[b][:, :].rearrange("p (h d) -> p h d", h=8)
            ACT(ssq_buf[:], ssq_buf[:], AF.Ln, [ssk], [ssk], scale=1.0 / 64, bias=EPS)
            ACT(ssq_buf[:], ssq_buf[:], AF.Exp, [ssk], [ssk], scale=-0.5)
            tv = sq_buf.rearrange("p (h d) -> p h d", h=8)
            TT('dve', tv, pv, ssq_buf.unsqueeze(2).to_broadcast([128, 8, 64]), ALU.mult, [PK(b), ssk], [sqk])
            TT('dve', out_ap, tv, gain.unsqueeze(1).to_broadcast([128, 8, 64]), ALU.mult, [sqk, gkey], [out_key])

        def stageA1(seq, j, sample=False, defer_pe=False):
            slot = xcnt[0] % 3
            xcnt[0] += 1
            par = tile_ctr[0] % 2
            tile_ctr[0] += 1
            X = Xb[slot]
            kx = ('X', slot)
            src = xs[:, :] if sample else xp[seq, j * 128:(j + 1) * 128, :]
            DMA('sp', X[:], src, [], [kx], f'x{slot}')
            ACT(xn[:], X[:], AF.Square, [kx], ['xn', 'ss'], accum_out=ss[:])
            ACT(rs[:], ss[:], AF.Ln, ['ss'], ['rs'], scale=1.0 / D, bias=EPS)
            ACT(rs[:], rs[:], AF.Exp, ['rs'], ['rs'], scale=-0.5)
            TS('dve', xn[:], X[:], rs[:], None, ALU.mult, None, [kx, 'rs'], ['xn'])

            def pe_part():
                b = nextbank('mm')
                for c in range(8):
                    TR(pbank_bf[b][:, c * 128:(c + 1) * 128], xn[:, c * 128:(c + 1) * 128], ['xn'], [PK(b)])
                TT('dve', hTb[par][:], pbank_bf[b][:, :].rearrange("p (c t) -> p c t", c=8),
                   gmix.unsqueeze(2).to_broadcast([128, 8, 128]), ALU.mult, [PK(b), 'gmix'], [('hT', par)])
            ctx = dict(slot=slot, par=par, j=j, seq=seq, sample=sample)
            if defer_pe:
                ctx['pe_part'] = pe_part
            else:
                pe_part()
            return ctx

        def stageA2(ctx):
            seq = ctx['seq']; j = ctx['j']; sample = ctx['sample']; par = ctx['par']
            hT = hTb[par]
            hk = ('hT', par)
            ra = j % RA
            chunks = []

            def inproj(c0, n, hold=False):
                b = nextbank('mm', hold)
                for kc in range(8):
                    MM(pbank[b][:, 0:n], hT[:, kc, :], win[:, kc, c0:c0 + n], kc == 0, kc == 7,
                       [hk, ('win', c0)], [PK(b)])
                return b

            st_ = {}

            def qa1():
                st_['qa'] = inproj(0, 512, True)
                head_norm_a(st_['qa'], sqb[0], ('sq', 0), ssqb[0], ('ssq', 0))

            def qa2():
                head_norm_b(st_['qa'], sqb[0], ('sq', 0), ssqb[0], ('ssq', 0), qna, 'qna', qa_bf[:], 'qa_bf')
                release(st_['qa'])

            def qa3():
                b = nextbank('mm')
                for h in range(8):
                    TR(pbank_bf[b][0:64, h * 128:(h + 1) * 128], qa_bf[:, h, :], ['qa_bf'], [PK(b)])
                ACT(qaT[par][:], pbank_bf[b][0:64, :].rearrange("p (h t) -> p h t", h=8), AF.Copy, [PK(b)], [('qaT', par)])

            def ka1():
                st_['ka'] = inproj(512, 512, True)
                head_norm_a(st_['ka'], kvo[0], ('kvo', 0), ssqb[1], ('ssq', 1))

            def ka2():
                head_norm_b(st_['ka'], kvo[0], ('kvo', 0), ssqb[1], ('ssq', 1), kna, 'kna',
                            kvo[0].rearrange("p (h d) -> p h d", h=8), ('kvo', 0))
                release(st_['ka'])
                CP('pool', ka_bf[:], kvo[0].rearrange("p (h d) -> p h d", h=8), [('kvo', 0)], ['ka_bf'])
                if sample:
                    DMA('sp', kas[:, :], kvo[0][:], [('kvo', 0)], [], 'o0', True)
                elif j >= NT - 4:
                    r0 = (j - (NT - 4)) * 128
                    DMA('sp', kap[seq, r0:r0 + 128, :], kvo[0][:], [('kvo', 0)], [], 'o0', True)

            def ka3():
                b = nextbank('mm')
                for h in range(8):
                    TR(pbank_bf[b][0:64, h * 128:(h + 1) * 128], ka_bf[:, h, :], ['ka_bf'], [PK(b)])
                ACT(KaT[:, ra, :, :], pbank_bf[b][0:64, :].rearrange("p (h t) -> p h t", h=8), AF.Copy, [PK(b)], [('KaT', ra)])

            def va1():
                b = inproj(1024, 512)
                ACT(kvo[1][:], pbank[b][:, :], AF.Copy, [PK(b)], [('kvo', 1)])
                CP('pool', Va[:, ra, :, 0:64], kvo[1].rearrange("p (h d) -> p h d", h=8), [('kvo', 1)], [('Va', ra)])
                if sample:
                    DMA('sp', vas[:, :], kvo[1][:], [('kvo', 1)], [], 'o1', True)
                elif j >= NT - 4:
                    r0 = (j - (NT - 4)) * 128
                    DMA('sp', vap[seq, r0:r0 + 128, :], kvo[1][:], [('kvo', 1)], [], 'o1', True)

            def gf1():
                b = inproj(3072, 8)
                TT('dve', xf[:], pbank[b][:, 0:8], bft[:], ALU.add, [PK(b), 'bft'], ['xf'])

            def gf1b():
                ACT(ef[:], xf[:], AF.Exp, ['xf'], ['ef'], scale=-1.0)
                ACT(ef[:], ef[:], AF.Ln, ['ef'], ['ef'], bias=1.0)
                lfj = lf_all[:, j, :]
                TS('dve', lfj, ef[:], -1.0, None, ALU.mult, None, ['ef'], [('lf', j)])
                if sample:
                    DMA('sp', lfs[:, :], lf_all[:, 0, :], [('lf', 0)], [], 'o4', True)
                elif j == NT - 1:
                    lfp_v = lfp[seq].rearrange("(t p) h -> p t h", p=128)
                    for t0 in range(0, NT, 4):
                        DMA('sp', lfp_v[:, t0:t0 + 4, :], lf_all[:, t0:t0 + 4, :],
                            [('lf', t) for t in range(t0, t0 + 4)], [], 'o4', True)

            def gf2():
                lfj = lf_all[:, j, :]
                b = nextbank('mm', True)
                if sample:
                    MM(pbank[b][:, 0:8], triblk_f, lfj, True, True, ['cst', ('lf', j)], [PK(b)])
                    for s_ in range(SEQ_PER_CORE):
                        if s_ == 0:
                            TS('dve', sel8[:], tot8[:, 0, :], ind4_f[:, 0:1], None, ALU.mult, None,
                               ['tot8', 'cst'], ['sel8'])
                        else:
                            STT(sel8[:], tot8[:, s_, :], ind4_f[:, s_:s_ + 1], sel8[:], ALU.mult, ALU.add,
                                ['tot8', 'cst', 'sel8'], ['sel8'])
                    st_['cb'] = b
                else:
                    if j == 0:
                        MEMSET('dve', carry8[:], 0.0, ['carry8'])
                    MM(pbank[b][:, 0:8], tri_f, lfj, True, True, ['cst', ('lf', j)], [PK(b)])
                    MM(pbank[b][:, 8:16], ones_f, lfj, True, True, ['cst', ('lf', j)], [PK(b)])
                    st_['cb'] = b

            def gf3():
                b = st_['cb']
                release(b)
                if sample:
                    STT(C8[:], pbank[b][:, 0:8], 8.0, sel8[:], ALU.mult, ALU.add, [PK(b), 'sel8'], ['C8'])
                else:
                    STT(C8[:], pbank[b][:, 0:8], 8.0, carry8[:], ALU.mult, ALU.add, [PK(b), 'carry8'], ['C8'])
                    STT(carry8[:], pbank[b][:, 8:16], 8.0, carry8[:], ALU.mult, ALU.add, [PK(b), 'carry8'], ['carry8'])

            def qb1():
                st_['qb'] = inproj(1536, 512, True)
                head_norm_a(st_['qb'], sqb[0], ('sq', 0), ssqb[0], ('ssq', 0))

            def qb2():
                head_norm_b(st_['qb'], sqb[0], ('sq', 0), ssqb[0], ('ssq', 0), qnb, 'qnb', qb_aug[:, :, 0:64], 'qb_aug')
                release(st_['qb'])
                CP('dve', qb_aug[:, :, 64], C8[:], ['C8'], ['qb_aug'])
                TT('dve', r1[:], C8[:], qb_aug[:, :, 64], ALU.subtract, ['C8', 'qb_aug'], ['r1'])
                CP('dve', qb_aug[:, :, 65], r1[:], ['r1'], ['qb_aug'])
                TT('dve', r2[:], r1[:], qb_aug[:, :, 65], ALU.subtract, ['r1', 'qb_aug'], ['r2'])
                CP('dve', qb_aug[:, :, 66], r2[:], ['r2'], ['qb_aug'])

            def qb3():
                b = nextbank('mm')
                for h in range(8):
                    TR(pbank_bf[b][0:70, h * 128:(h + 1) * 128], qb_aug[:, h, :], ['qb_aug'], [PK(b)])
                CP('dve', qbT[par][:], pbank_bf[b][0:70, :].rearrange("p (h t) -> p h t", h=8), [PK(b)], [('qbT', par)])

            def kb1():
                st_['kb'] = inproj(2048, 512, True)
                head_norm_a(st_['kb'], kvo[2], ('kvo', 2), ssqb[1], ('ssq', 1))

            def kb2():
                head_norm_b(st_['kb'], kvo[2], ('kvo', 2), ssqb[1], ('ssq', 1), knb, 'knb',
                            kvo[2].rearrange("p (h d) -> p h d", h=8), ('kvo', 2))
                release(st_['kb'])
                CP('pool', kb_aug[:, :, 0:64], kvo[2].rearrange("p (h d) -> p h d", h=8), [('kvo', 2)], ['kb_aug'])
                TS('dve', kb_aug[:, :, 67:70], qb_aug[:, :, 64:67], -1.0, None, ALU.mult, None, ['qb_aug'], ['kb_aug'])
                if sample:
                    DMA('sp', kbs[:, :], kvo[2][:], [('kvo', 2)], [], 'o2', True)
                else:
                    DMA('sp', kbp[seq, j * 128:(j + 1) * 128, :], kvo[2][:], [('kvo', 2)], [], 'o2', True)

            def kb3():
                b = nextbank('mm')
                for h in range(8):
                    TR(pbank_bf[b][0:70, h * 128:(h + 1) * 128], kb_aug[:, h, :], ['kb_aug'], [PK(b)])
                CP('dve', KbT[:, j, :, :], pbank_bf[b][0:70, :].rearrange("p (h t) -> p h t", h=8), [PK(b)], [('KbT', j)])

            def vb1():
                b = inproj(2560, 512)
                ACT(kvo[3][:], pbank[b][:, :], AF.Copy, [PK(b)], [('kvo', 3)])
                CP('pool', Vb[:, j, :, 0:64], kvo[3].rearrange("p (h d) -> p h d", h=8), [('kvo', 3)], [('Vb', j)])
                if sample:
                    DMA('sp', vbs[:, :], kvo[3][:], [('kvo', 3)], [], 'o3', True)
                else:
                    DMA('sp', vbp[seq, j * 128:(j + 1) * 128, :], kvo[3][:], [('kvo', 3)], [], 'o3', True)

            def seqc(*fs):
                def f():
                    for g in fs:
                        g()
                return f
            return [seqc(gf1, qa1), seqc(gf1b, qa2, ka1), seqc(gf2, ka2, va1, qa3), seqc(gf3, qb1, ka3),
                    seqc(qb2, kb1), seqc(kb2, vb1, qb3), seqc(kb3, emit_conv)]

        pt_ctr = [0]
        pe_ctr = [0]

        def run_units(units, inject=None):
            def emit_st(u):
                b = nextbank('st')
                for i, (lhsT, rhs, extra, rd) in enumerate(u['st']):
                    o = pbank[b][:, i * 128:(i + 1) * 128]
                    MM(o, lhsT, rhs, True, len(extra) == 0, rd, [PK(b)])
                    for ei, (xr, xrd) in enumerate(extra):
                        MM(o, ident_bf[:], xr, False, ei == len(extra) - 1, ['cbf'] + xrd, [PK(b)])
                u['b_st'] = b

            def emit_rest(u):
                b = u['b_st']
                n = len(u['st'])
                ps = pt_ctr[0] % 3
                pt_ctr[0] += 1
                ACT(PT[ps][:, 0:n * 128], pbank[b][:, 0:n * 128], AF.Exp, [PK(b)], [('PT', ps)], scale=0.125)
                ob = u['ob']
                hh = u['hh']
                for i, (rhs, rd) in enumerate(u['pv']):
                    MM(pbank[ob][:, hh * 65:(hh + 1) * 65], PT[ps][:, i * 128:(i + 1) * 128], rhs,
                       u['first'] and i == 0, u['last'] and i == n - 1, [('PT', ps)] + rd, [PK(ob)])
                if u.get('after') is not None:
                    u['after']()

            LOOK = 2
            n_u = len(units)
            for ui in range(n_u + LOOK):
                if ui < n_u:
                    emit_st(units[ui])
                if ui - LOOK >= 0:
                    emit_rest(units[ui - LOOK])
                    if inject is not None:
                        inject(ui - LOOK)

        def finish_heads(ob, col0, nheads=4):
            ov = pbank[ob][:, 0:nheads * 65].rearrange("p (h c) -> p h c", h=nheads)
            S.op('dve', lambda e: e.reciprocal(out=rinv[:, 0:nheads], in_=ov[:, :, 64]), reads=[PK(ob)], writes=['rinv'])
            TT('dve', O_bf[:, col0:col0 + 64 * nheads].rearrange("p (h d) -> p h d", h=nheads), ov[:, :, 0:64],
               rinv[:, 0:nheads].unsqueeze(2).to_broadcast([128, nheads, 64]), ALU.mult, [PK(ob), 'rinv'], [('O_bf', col0 // 256)])

        def o_transpose(half):
            b = nextbank('mm')
            rd = [('O_bf', half * 2), ('O_bf', half * 2 + 1)]
            for c in range(4):
                cc = half * 4 + c
                TR(pbank_bf[b][:, c * 128:(c + 1) * 128], O_bf[:, cc * 128:(cc + 1) * 128], rd, [PK(b)])
            eng = 'dve'
            if eng == 'act':
                ACT(OT[:, half * 4:(half + 1) * 4, :], pbank_bf[b][:, 0:512].rearrange("p (c t) -> p c t", c=4), AF.Copy,
                    [PK(b)], [('OT', half)])
            else:
                CP('dve', OT[:, half * 4:(half + 1) * 4, :], pbank_bf[b][:, 0:512].rearrange("p (c t) -> p c t", c=4),
                   [PK(b)], [('OT', half)])

        def out_proj_and_store(ctx, gtile):
            slot = ctx['slot']
            X = Xb[slot]
            kx = ('X', slot)
            for n in range(2):
                b = nextbank('mm')
                for kc in range(8):
                    MM(pbank[b][:, :], OT[:, kc, :], wout[:, kc, n * 512:(n + 1) * 512], kc == 0, kc == 7,
                       [('OT', kc // 4), 'wout'], [PK(b)])
                TT('dve', X[:, n * 512:(n + 1) * 512], X[:, n * 512:(n + 1) * 512], pbank[b][:, :], ALU.add,
                   [kx, PK(b)], [kx])
            DMA('sp', x1s[gtile * 128:(gtile + 1) * 128, :], X[:], [kx], [('x1s', gtile)], f'xs{slot}')

        def assign_oacc(units, after_fn):
            cur = None
            for u in units:
                if cur is None:
                    cur = nextbank('oa')
                u['ob'] = cur
                if 'hg_end' in u:
                    kind, hg = u['hg_end']
                    u['after'] = after_fn(cur, kind, hg)
                    cur = None

        def band_extra(h, o):
            if o == 0:
                return [(Bhi[:, h, 0, :], ['Bhi'])]
            if o in (1, 2):
                return []
            return [(Bhi[:, h, EIDX[o], :], ['Bhi']), (Blo[:, h, EIDX[o], :], ['Blo'])]

        def build_units(j, par):
            units = []
            tiles = list(range(max(0, j - 4), j + 1))
            for hg in range(2):
                for hh in range(4):
                    h = hg * 4 + hh
                    groups = [[t for t in tiles if t - (j - 4) <= 1], [t for t in tiles if t - (j - 4) >= 2]]
                    groups = [g for g in groups if g]
                    for gi, g in enumerate(groups):
                        u = dict(kind='A', h=h, hh=hh, first=(gi == 0), last=(gi == len(groups) - 1),
                                 st=[(KaT[:, t % RA, h, :], qaT[par][:, h, :], band_extra(h, t - (j - 4)),
                                      [('KaT', t % RA), ('qaT', par)]) for t in g],
                                 pv=[(Va[:, t % RA, h, :], [('Va', t % RA), 'Va_ones']) for t in g])
                        units.append(u)
                units[-1]['hg_end'] = ('A', hg)
            tilesb = list(range(0, j + 1))
            for hg in range(2):
                for hh in range(4):
                    h = hg * 4 + hh
                    groups = [tilesb[i:i + 4] for i in range(0, len(tilesb), 4)]
                    for gi, g in enumerate(groups):
                        u = dict(kind='B', h=h, hh=hh, first=(gi == 0), last=(gi == len(groups) - 1),
                                 st=[(KbT[:, t, h, :], qbT[par][:, h, :], ([(maskc_bf[:], [])] if t == j else []),
                                      [('KbT', t), ('qbT', par)]) for t in g],
                                 pv=[(Vb[:, t, h, :], [('Vb', t), 'Vb_ones']) for t in g])
                        units.append(u)
                units[-1]['hg_end'] = ('B', hg)
            return units

        def stageB(ctx, chunks=(), mid=None):
            j = ctx['j']
            par = ctx['par']
            units = build_units(j, par)

            pending = []
            cur_unit = [0]

            def after_fn(ob, kind, hg):
                col0 = (0 if kind == 'A' else 512) + hg * 256

                def f():
                    finish_heads(ob, col0)
                    if hg == 1:
                        pending.append((cur_unit[0], lambda: o_transpose(0 if kind == 'A' else 1)))
                return f
            assign_oacc(units, after_fn)
            chunks = list(chunks)
            nu = len(units)
            nch = len(chunks)
            sched_at = {}
            for k in range(nch):
                sched_at.setdefault(max(0, (k + 1) * nu // (nch + 1) - 1), []).append(chunks[k])
            mid_at = max(0, nu // 3 - 1)
            mid2_at = max(mid_at + 1, (2 * nu) // 3 - 1)
            state = {'mid': mid, 'mid2': None}

            def inject(i):
                while pending and pending[0][0] < i:
                    pending.pop(0)[1]()
                for c in sched_at.get(i, []):
                    c()
                if i == mid_at and state['mid'] is not None:
                    state['mid2'] = state['mid']()
                    state['mid'] = None
                if i >= mid2_at and state['mid2'] is not None:
                    state['mid2']()
                    state['mid2'] = None
                cur_unit[0] = i + 1
            run_units(units, inject)
            if state['mid'] is not None:
                state['mid2'] = state['mid']()
            if state['mid2'] is not None:
                state['mid2']()
            while pending:
                pending.pop(0)[1]()
            out_proj_and_store(ctx, ctx['gtile'])

        a1ctx = {}

        def do_A1(seq, j, defer_pe=False):
            c = stageA1(seq, j, defer_pe=defer_pe)
            c['gtile'] = seq * NT + j
            a1ctx[(seq, j)] = c
            return c.get('pe_part')

        for seq in range(n_prompt_seq):
            if (seq, 0) not in a1ctx:
                do_A1(seq, 0)
            for c in stageA2(a1ctx[(seq, 0)]):
                c()
            do_A1(seq, 1)
            for j in range(NT):
                chunks = stageA2(a1ctx[(seq, j + 1)]) if j + 1 < NT else []
                if j + 2 < NT:
                    mid = (lambda seq=seq, j=j: do_A1(seq, j + 2, True))
                elif j + 2 == NT + 1 and seq + 1 < n_prompt_seq:
                    mid = (lambda seq=seq: do_A1(seq + 1, 0, True))
                else:
                    mid = None
                stageB(a1ctx[(seq, j)], chunks, mid)

        if do_sample:
            S.barrier()
            MEMSET('pool', qb_aug[:, :, 67:70], 1.0, ['qb_aug'])
            MEMSET('pool', kb_aug[:, :, 64:67], 1.0, ['kb_aug'])
            for i in range(2):
                MEMSET('pool', kc_aug[i][:, :, 64:67], 1.0, [('kc_ones', i)])
                MEMSET('pool', vc_aug[i][:, :, 64:65], 1.0, [('vc_ones', i)])
                MEMSET('pool', PTz[i][:], 0.0, [('PTz', i)])
            MEMSET('dve', accA[:], 0.0, ['accA'])
            MEMSET('dve', accB[:], 0.0, ['accB'])
            BN2 = BNf.rearrange("p h q -> p (h q)")
            DMA('sp', BN2, biasN_d[:, :], [], ['BNf'], 'c2b')
            TS('dve', BN2, BN2, 8.0, None, ALU.mult, None, ['BNf'], ['BNf'])
            TT('dve', BNf[:], BNf[:], cH.unsqueeze(2).to_broadcast([128, 8, 128]), ALU.subtract, ['BNf', 'cH'], ['BNf'])
            TT('dve', BNf[:], BNf[:], mbd01_f.unsqueeze(1).to_broadcast([128, 8, 128]), ALU.mult, ['BNf', 'cst'], ['BNf'])
            TT('dve', BNf[:], BNf[:], mbdn_f.unsqueeze(1).to_broadcast([128, 8, 128]), ALU.add, ['BNf', 'cst'], ['BNf'])
            CP('dve', BNhi[:], BNf[:], ['BNf'], ['BNhi'])
            TT('dve', BNf[:], BNf[:], BNhi[:], ALU.subtract, ['BNf', 'BNhi'], ['BNf'])
            CP('dve', BNlo[:], BNf[:], ['BNf'], ['BNlo'])

            ctx = stageA1(0, 0, sample=True)
            ctx['gtile'] = SEQ_PER_CORE * NT
            for c in stageA2(ctx):
                c()
            par = ctx['par']
            stgK.append(stgK3)
            stgV.append(stgV3)
            ctiles_ = []
            for s_ in range(SEQ_PER_CORE):
                for i_ in range(WA // 128):
                    ctiles_.append((s_, i_, True))
                for i_ in range(NCT):
                    ctiles_.append((s_, i_, False))
            NCTL = len(ctiles_)
            cst8 = {}

            def c_load(t):
                s_, i_, band = ctiles_[t]
                k = t % 3
                ksrc = (cka if band else ckb)[s_, i_ * 128:(i_ + 1) * 128, :]
                vsrc = (cva if band else cvb)[s_, i_ * 128:(i_ + 1) * 128, :]
                DMA('sp', stgK[k][:], ksrc, [], [('stgK', k)], f'ck{k}')
                DMA('sp', stgV[k][:], vsrc, [], [('stgV', k)], f'cv{k}')

            def c_cast_k(t):
                s_, i_, band = ctiles_[t]
                k = t % 2
                CP('dve', kc_aug[k][:, :, 0:64], stgK[t % 3].rearrange("p (h d) -> p h d", h=8),
                   [('stgK', t % 3)], [('kc', k)])
                if not band:
                    cav = caug[:, s_, :].rearrange("p (t h c) -> p t h c", t=NCT, h=8)
                    CP('pool', kc_aug[k][:, :, 67:70], cav[:, i_, :, :], ['caug'], [('kc', k)])

            def c_tr_k(t):
                s_, i_, band = ctiles_[t]
                k = t % 2
                nr = 64 if band else 70
                rdk = [('kc', k)]
                if not band:
                    rdk.append(('kc_ones', k))
                b = nextbank('mm')
                for h in range(8):
                    TR(pbank_bf[b][0:nr, h * 128:(h + 1) * 128], kc_aug[k][:, h, 0:nr], rdk, [PK(b)])
                ACT(kcT[k][0:nr], pbank_bf[b][0:nr, :].rearrange("p (h t) -> p h t", h=8), AF.Copy,
                    [PK(b)], [('kcT', k)])

            def c_prep_v(t):
                k = t % 2
                sv_ = stgV[t % 3].rearrange("p (h d) -> p h d", h=8)
                CP('pool', vc_aug[k][:, 0:4, 0:64], sv_[:, 0:4, :], [('stgV', t % 3)], [('vcA', k)])
                CP('dve', vc_aug[k][:, 4:8, 0:64], sv_[:, 4:8, :], [('stgV', t % 3)], [('vcB', k)])

            def c_score(t):
                s_, i_, band = ctiles_[t]
                k = t % 2
                nr = 64 if band else 70
                bs = nextbank('st')
                qT = qaT[par] if band else qbT[par]
                qk = ('qaT', par) if band else ('qbT', par)
                kt = 2
                wb = band and i_ == 3
                for h in range(8):
                    o = pbank[bs][:, h * 32:(h + 1) * 32]
                    MM(o, kcT[k][0:nr, h, :], qT[0:nr, h, s_ * 32:(s_ + 1) * 32], True, not wb,
                       [('kcT', k), qk], [PK(bs)])
                    if wb:
                        MM(o, ident_bf[:], Bhi[:, h, kt, 0:32], False, False, ['cbf', 'Bhi'], [PK(bs)])
                        MM(o, ident_bf[:], Blo[:, h, kt, 0:32], False, True, ['cbf', 'Blo'], [PK(bs)])
                pz = PTz[k][:, :, s_ * 32:(s_ + 1) * 32]
                sv = pbank[bs][:, 0:256].rearrange("p (h q) -> p h q", h=8)
                ACT(pz, sv, AF.Exp, [PK(bs)], [('PTz', k)], scale=0.125)

            def c_pv(t):
                s_, i_, band = ctiles_[t]
                k = t % 2
                acc = accA if band else accB
                ak = 'accA' if band else 'accB'
                for half in range(2):
                    ob = nextbank('mm')
                    for hh in range(4):
                        h = half * 4 + hh
                        MM(pbank[ob][:, hh * 65:(hh + 1) * 65], PTz[k][:, h, :], vc_aug[k][:, h, :], True, True,
                           [('PTz', k), (('vcA', k) if half == 0 else ('vcB', k)), ('vc_ones', k)], [PK(ob)])
                    TT('dve', acc[:, half * 4:(half + 1) * 4, :], acc[:, half * 4:(half + 1) * 4, :],
                       pbank[ob][:, 0:260].rearrange("p (h c) -> p h c", h=4), ALU.add, [ak, PK(ob)], [ak])
                if t + 1 == NCTL or ctiles_[t + 1][0] != s_:
                    for i2 in range(2):
                        MEMSET('pool', PTz[i2][:, :, s_ * 32:(s_ + 1) * 32], 0.0, [('PTz', i2)])

            c_load(0)
            c_load(1)
            c_cast_k(0)
            for t in range(NCTL + 2):
                if t < NCTL:
                    c_tr_k(t)
                if t + 1 < NCTL:
                    c_cast_k(t + 1)
                if 0 <= t - 2 < NCTL:
                    c_pv(t - 2)
                if 0 <= t - 1 < NCTL:
                    c_score(t - 1)
                if t + 2 < NCTL:
                    c_load(t + 2)
                if t < NCTL:
                    c_prep_v(t)

            units = []
            for hg in range(2):
                for hh in range(4):
                    h = hg * 4 + hh
                    units.append(dict(kind='A', h=h, hh=hh, first=True, last=True,
                                      st=[(KaT[:, 0, h, :], qaT[par][:, h, :],
                                           [(BNhi[:, h, :], ['BNhi']), (BNlo[:, h, :], ['BNlo'])],
                                           [('KaT', 0), ('qaT', par)])],
                                      pv=[(Va[:, 0, h, :], [('Va', 0), 'Va_ones'])]))
                units[-1]['hg_end'] = ('A', hg)
            for hg in range(2):
                for hh in range(4):
                    h = hg * 4 + hh
                    units.append(dict(kind='B', h=h, hh=hh, first=True, last=True,
                                      st=[(KbT[:, 0, h, :], qbT[par][:, h, :], [(maskbd_bf[:], [])],
                                           [('KbT', 0), ('qbT', par)])],
                                      pv=[(Vb[:, 0, h, :], [('Vb', 0), 'Vb_ones'])]))
                units[-1]['hg_end'] = ('B', hg)

            def after_fn(ob, kind, hg):
                acc = accA if kind == 'A' else accB
                ak = 'accA' if kind == 'A' else 'accB'

                def after():
                    TT('dve', acc[:, hg * 4:(hg + 1) * 4, :], acc[:, hg * 4:(hg + 1) * 4, :],
                       pbank[ob][:, 0:260].rearrange("p (h c) -> p h c", h=4), ALU.add, [ak, PK(ob)], [ak])
                return after
            assign_oacc(units, after_fn)
            S.op('pool', lambda e: e.memset(Va[:, 0, :, 64:65], 1.0), writes=['Va_ones'])
            S.op('pool', lambda e: e.memset(Vb[:, 0, :, 64:65], 1.0), writes=['Vb_ones'])
            run_units(units)
            for (acc, ak, col0) in ((accA, 'accA', 0), (accB, 'accB', 512)):
                S.op('dve', lambda e, acc=acc: e.reciprocal(out=rinv[:, 0:8], in_=acc[:, :, 64]), reads=[ak], writes=['rinv'])
                TT('dve', O_bf[:, col0:col0 + 512].rearrange("p (h d) -> p h d", h=8), acc[:, :, 0:64],
                   rinv[:, 0:8].unsqueeze(2).to_broadcast([128, 8, 64]), ALU.mult, [ak, 'rinv'],
                   [('O_bf', col0 // 256), ('O_bf', col0 // 256 + 1)])
                o_transpose(col0 // 512)
            out_proj_and_store(ctx, ctx['gtile'])

        if do_phase2:
            S.barrier()
            AR.off = PERSIST_END
            wg = AR.alloc([128, 8, DFF], BF16)
            wu = AR.alloc([128, 8, DFF], BF16)
            wd = AR.alloc([128, NFF, D], BF16)
            wpg = AR.alloc([128, 8, D], BF16)
            wpp = AR.alloc([128, 2, D], BF16)
            X2 = [AR.alloc([128, D], F32) for _ in range(3)]
            xnf = AR.alloc([128, D], BF16)
            xnp = AR.alloc([128, D], BF16)
            hTf = [AR.alloc([128, 8, 128], BF16) for _ in range(2)]
            hTp = AR.alloc([128, 8, 128], BF16)
            a_bf = AR.alloc([128, DFF], BF16)
            aT = AR.alloc([128, NFF, 128], BF16)
            p_sb = [AR.alloc([128, PLE], F32) for _ in range(2)]
            p_bf = AR.alloc([128, PLE], BF16)
            pT = [AR.alloc([128, 2, 128], BF16) for _ in range(3)]
            sig = [AR.alloc([128, 512], F32) for _ in range(2)]
            ssp = AR.alloc([128, 1], F32)
            rsp = AR.alloc([128, 1], F32)
            print("phase2 arena bytes", AR.off)
            emit_conv(len(conv_jobs))
            def wload(dst, src, nk, key, sem, step=4):
                v = src.rearrange("(c p) n -> p c n", p=128)
                for k0 in range(0, nk, step):
                    k1 = min(nk, k0 + step)
                    DMA('sp', dst[:, k0:k1, :], v[:, k0:k1, :], ['wconv'], [key], sem)
            wload(wg, wg_s, 8, 'wg', 'w2')
            wload(wu, wu_s, 8, 'wu', 'w2b')
            wload(wd, wd_s, NFF, 'wd', 'w3')
            wload(wpg, wpg_s, 8, 'wpg', 'w3b')
            wload(wpp, wpp_s, 2, 'wpp', 'w3c')

            tiles2 = []
            for seq in range(n_prompt_seq):
                for j in range(NT):
                    tiles2.append((seq * NT + j, pp[seq, j * 128:(j + 1) * 128, :], yp[seq, j * 128:(j + 1) * 128, :]))
            if do_sample:
                tiles2.append((SEQ_PER_CORE * NT, psm[:, :], ys[:, :]))
            NT2 = len(tiles2)
            bank_rot['mm'] = [0, 1, 2, 3, 4, 5, 6, 7]
            ctiles = [(c0, min(512, DFF - c0)) for c0 in range(0, DFF, 512)]

            def XK(t):
                return X2[t % 3], ('X2', t % 3)

            def rms_nonpe(X, kx, xnb, xk, ssb, sk, rsb, rk):
                ACT(xnb[:], X[:], AF.Square, [kx], [xk, sk], accum_out=ssb[:])
                ACT(rsb[:], ssb[:], AF.Ln, [sk], [rk], scale=1.0 / D, bias=EPS)
                ACT(rsb[:], rsb[:], AF.Exp, [rk], [rk], scale=-0.5)
                TS('dve', xnb[:], X[:], rsb[:], None, ALU.mult, None, [kx, rk], [xk])

            def rms_pe(xnb, xk, gain, gkey, dst, dkey):
                b = nextbank('mm')
                for c in range(8):
                    TR(pbank_bf[b][:, c * 128:(c + 1) * 128], xnb[:, c * 128:(c + 1) * 128], [xk], [PK(b)])
                TT('dve', dst[:], pbank_bf[b][:, :].rearrange("p (c t) -> p c t", c=8),
                   gain.unsqueeze(2).to_broadcast([128, 8, 128]), ALU.mult, [PK(b), gkey], [dkey])

            def P_nonpe(t):
                g, psrc, ydst = tiles2[t]
                X, kx = XK(t)
                pk = t % 2
                DMA('sp', X[:], x1s[g * 128:(g + 1) * 128, :], [('x1s', g)], [kx], f'y{t % 3}')
                DMA('sp', p_sb[pk][:], psrc, [], [('p', pk)], f'p{pk}')
                rms_nonpe(X, kx, xnf, 'xnf', ss, 'ss', rs, 'rs')
                CP('pool', p_bf[:], p_sb[pk][:], [('p', pk)], ['p_bf'])

            def P_pe(t):
                pk = t % 2
                rms_pe(xnf, 'xnf', gffn, 'gffn', hTf[pk], ('hTf', pk))
                b = nextbank('mm')
                for c in range(2):
                    TR(pbank_bf[b][:, c * 128:(c + 1) * 128], p_bf[:, c * 128:(c + 1) * 128], ['p_bf'], [PK(b)])
                ACT(pT[t % 3][:], pbank_bf[b][:, 0:256].rearrange("p (c t) -> p c t", c=2), AF.Copy, [PK(b)], [('pT', t % 3)])

            def GU_stage(t, cis):
                pk = t % 2
                hT2 = hTf[pk]
                hk = ('hTf', pk)
                for ci in cis:
                    c0, n = ctiles[ci]
                    bg = nextbank('mm')
                    for kc in range(8):
                        MM(pbank[bg][:, 0:n], hT2[:, kc, :], wg[:, kc, c0:c0 + n], kc == 0, kc == 7, [hk, 'wg'], [PK(bg)])
                    bu = nextbank('mm')
                    for kc in range(8):
                        MM(pbank[bu][:, 0:n], hT2[:, kc, :], wu[:, kc, c0:c0 + n], kc == 0, kc == 7, [hk, 'wu'], [PK(bu)])
                    k2 = ci % 2
                    ACT(sig[k2][:, 0:n], pbank[bg][:, 0:n], AF.Sigmoid, [PK(bg)], [('sig', k2)])
                    TT('dve', sig[k2][:, 0:n], pbank[bg][:, 0:n], sig[k2][:, 0:n], ALU.mult, [PK(bg), ('sig', k2)], [('sig', k2)])
                    TT('dve', a_bf[:, c0:c0 + n], sig[k2][:, 0:n], pbank[bu][:, 0:n], ALU.mult, [('sig', k2), PK(bu)], [('a', ci)])

            def AT_stage(t):
                for g0 in range(0, NFF, 8):
                    ng = min(8, NFF - g0)
                    b = nextbank('mm')
                    rd = sorted(set(('a', (c * 128) // 512) for c in range(g0, g0 + ng)))
                    for c in range(ng):
                        TR(pbank_bf[b][:, c * 128:(c + 1) * 128], a_bf[:, (g0 + c) * 128:(g0 + c + 1) * 128], rd, [PK(b)])
                    ACT(aT[:, g0:g0 + ng, :], pbank_bf[b][:, 0:ng * 128].rearrange("p (c t) -> p c t", c=ng), AF.Copy,
                        [PK(b)], [('aT', g0)])

            def DN_stage(t):
                X, kx = XK(t)
                for n in range(2):
                    b = nextbank('mm')
                    for kc in range(NFF):
                        MM(pbank[b][:, :], aT[:, kc, :], wd[:, kc, n * 512:(n + 1) * 512], kc == 0, kc == NFF - 1,
                           [('aT', (kc // 8) * 8), 'wd'], [PK(b)])
                    TT('dve', X[:, n * 512:(n + 1) * 512], X[:, n * 512:(n + 1) * 512], pbank[b][:, :], ALU.add,
                       [kx, PK(b)], [kx])

            def R2_nonpe(t):
                X, kx = XK(t)
                rms_nonpe(X, kx, xnp, 'xnp', ssp, 'ssp', rsp, 'rsp')

            def R2_pe(t):
                rms_pe(xnp, 'xnp', gple, 'gple', hTp, 'hTp')

            def PL_stage(t):
                g, psrc, ydst = tiles2[t]
                X, kx = XK(t)
                pk = t % 2
                for n in range(2):
                    bg = nextbank('mm')
                    for kc in range(8):
                        MM(pbank[bg][:, :], hTp[:, kc, :], wpg[:, kc, n * 512:(n + 1) * 512], kc == 0, kc == 7,
                           ['hTp', 'wpg'], [PK(bg)])
                    bp = nextbank('mm')
                    for kc in range(2):
                        MM(pbank[bp][:, :], pT[t % 3][:, kc, :], wpp[:, kc, n * 512:(n + 1) * 512], kc == 0, kc == 1,
                           [('pT', t % 3), 'wpp'], [PK(bp)])
                    k2 = n % 2
                    ACT(sig[k2][:], pbank[bg][:, :], AF.Sigmoid, [PK(bg)], [('sig', k2)])
                    TT('dve', sig[k2][:], pbank[bp][:, :], sig[k2][:], ALU.mult, [PK(bp), ('sig', k2)], [('sig', k2)])
                    TT('dve', X[:, n * 512:(n + 1) * 512], X[:, n * 512:(n + 1) * 512], sig[k2][:], ALU.add,
                       [kx, ('sig', k2)], [kx])
                DMA('sp', ydst, X[:], [kx], [], f'yo{t % 3}', True)

            P_nonpe(0)
            P_pe(0)
            if NT2 > 1:
                P_nonpe(1)
                P_pe(1)
            for k in range(NT2 + 1):
                if k >= 1:
                    R2_nonpe(k - 1)
                if k < NT2:
                    GU_stage(k, [0, 1, 2])
                if k >= 1:
                    R2_pe(k - 1)
                if k < NT2:
                    GU_stage(k, [3, 4, 5])
                if k >= 1:
                    PL_stage(k - 1)
                if k < NT2:
                    AT_stage(k)
                if k + 2 < NT2:
                    P_nonpe(k + 2)
                if k < NT2:
                    DN_stage(k)
                if k + 2 < NT2:
                    P_pe(k + 2)

        stats = S.emit()
        print("sched stats (ops, waits):", stats, "held-bank skips:", skipped[0], "still held:", sorted(held))
    return nc


_PROGRAM = {}


def _rel_bias_layout(rel):
    k = np.arange(128)[:, None]
    q = np.arange(128)[None, :]
    out = np.empty((128, 8, 4, 128), np.float32)
    for si, kt in enumerate((0, 1, 3, 4)):
        r = (4 - kt) * 128 + q - k
        idx = np.clip(r, -128, 128) + 128
        out[:, :, si, :] = rel[:, idx].transpose(1, 0, 2)
    idxn = np.clip((q % 32) - (k % 32), -128, 128) + 128
    outn = np.ascontiguousarray(rel[:, idxn].transpose(1, 0, 2))
    return np.ascontiguousarray(out.reshape(128, -1)), np.ascontiguousarray(outn.reshape(128, -1))


def kernel(x_prompt, x_sample, cache_k_a, cache_v_a, cache_k_b, cache_v_b, cache_logf_b,
           p_prompt, p_sample, norm_mix, w_in, b_f, q_norm_a, k_norm_a, q_norm_b, k_norm_b,
           rel_bias_a, w_out, norm_ffn, w_gate, w_up, w_down, norm_ple, w_ple_gate, w_ple_proj):
    f = lambda a: np.ascontiguousarray(np.asarray(a, dtype=np.float32))
    x_prompt = f(x_prompt); x_sample = f(x_sample)
    cache_k_a = f(cache_k_a); cache_v_a = f(cache_v_a); cache_k_b = f(cache_k_b); cache_v_b = f(cache_v_b)
    cache_logf_b = f(cache_logf_b); p_prompt = f(p_prompt); p_sample = f(p_sample)
    if 'nc' not in _PROGRAM:
        _PROGRAM['nc'] = build_program()
    nc = _PROGRAM['nc']
    biasT, biasN = _rel_bias_layout(f(rel_bias_a)[0])
    cst = make_consts()
    g2 = lambda g: np.ascontiguousarray(f(g)[0].reshape(8, 128).T)
    shared = dict(
        w_in=f(w_in)[0], w_out=f(w_out)[0], w_gate=f(w_gate)[0], w_up=f(w_up)[0], w_down=f(w_down)[0],
        w_pg=f(w_ple_gate)[0], w_pp=f(w_ple_proj)[0],
        gmix=g2(norm_mix), gffn=g2(norm_ffn), gple=g2(norm_ple),
        bf=f(b_f), qna=f(q_norm_a), kna=f(k_norm_a), qnb=f(q_norm_b), knb=f(k_norm_b),
        biasT=biasT, biasN=biasN, cst=cst)
    in_maps = []
    for c in range(NCORES):
        sl = slice(c * SEQ_PER_CORE, (c + 1) * SEQ_PER_CORE)
        m = dict(shared)
        m.update(
            xp=x_prompt[sl], pp=p_prompt[0, sl],
            xs=x_sample[sl].reshape(128, D), psm=p_sample[0, sl].reshape(128, PLE),
            cka=cache_k_a[0, sl].reshape(SEQ_PER_CORE, WA, 512), cva=cache_v_a[0, sl].reshape(SEQ_PER_CORE, WA, 512),
            ckb=cache_k_b[0, sl].reshape(SEQ_PER_CORE, PAST, 512), cvb=cache_v_b[0, sl].reshape(SEQ_PER_CORE, PAST, 512),
            clf=cache_logf_b[0, sl])
        in_maps.append(m)
    res = run_bass_kernel_spmd(nc, in_maps, core_ids=list(range(NCORES)))
    R = res.results
    cat = lambda k: np.concatenate([r[k] for r in R], axis=0)
    B = NCORES * SEQ_PER_CORE
    y_prompt = cat("yp")
    y_sample = cat("ys").reshape(B, 32, D)
    kap = cat("kap").reshape(1, B, WA, 8, 64)
    vap = cat("vap").reshape(1, B, WA, 8, 64)
    kbp = cat("kbp").reshape(1, B, T, 8, 64)
    vbp = cat("vbp").reshape(1, B, T, 8, 64)
    lfp = cat("lfp").reshape(1, B, T, 8)
    kas = cat("kas").reshape(1, B, 32, 8, 64)
    vas = cat("vas").reshape(1, B, 32, 8, 64)
    kbs = cat("kbs").reshape(1, B, 32, 8, 64)
    vbs = cat("vbs").reshape(1, B, 32, 8, 64)
    lfs = cat("lfs").reshape(1, B, 32, 8)
    return (y_prompt, y_sample, kap, vap, kbp, vbp, lfp, kas, vas, kbs, vbs, lfs)
```
